# Optimizing a Trainium2 kernel written in Bass

```python
import math
import jax, jax.numpy as jnp
from jax import lax
import numpy as np

D_MODEL = 1024
BATCH = 1
SEQ = 16384
DEPTH = 2
DEC_BATCH = 4
DEC_SEQ = 8192
PAST_LEN = 128

BRANCH_W = 512
N_BRANCH = 4
POOL_WINDOWS = (2, 4, 8, 16)
POOL_GROUPS = 4
POOL_GW = BRANCH_W // POOL_GROUPS
DA_HEADS = 4
DA_HEAD_DIM = 64
DA_V_DIM = 2 * DA_HEAD_DIM
Q_BLOCK = 128
REL_BUCKETS = 32
REL_MAX_DIST = 128
HY_SHORT = 3
HY_EMB = 33
HY_BANDS = (HY_EMB - 1) // 2
HY_FILTER_ORDER = 64
HY_DECAY_TARGET = 1e-2
HY_FAST = 0.3
HY_SLOW = 1.5
CF_WIDTH = 31
EPS = 1e-6

OFF_POOL = 0
OFF_Q = OFF_POOL + BRANCH_W
OFF_K = OFF_Q + DA_HEADS * 2 * DA_HEAD_DIM
OFF_V = OFF_K + DA_HEADS * 2 * DA_HEAD_DIM
OFF_HY = OFF_V + DA_HEADS * DA_V_DIM
OFF_CF = OFF_HY + 3 * BRANCH_W
OFF_SILU = OFF_CF + 2 * BRANCH_W
OFF_MERGE = OFF_SILU + N_BRANCH * BRANCH_W
N_IN = OFF_MERGE + N_BRANCH * D_MODEL

kernel_name = 'hybrid_pool_diffattn_hyena_conformer_encoder'

F32 = jnp.float32


def rmsnorm(x, g):
    xf = x.astype(F32)
    y = xf * lax.rsqrt(jnp.mean(xf * xf, axis=-1, keepdims=True) + EPS)
    return (y * g.astype(F32)).astype(x.dtype)


def layernorm(x, g, b):
    xf = x.astype(F32)
    mu = jnp.mean(xf, axis=-1, keepdims=True)
    var = jnp.mean(jnp.square(xf - mu), axis=-1, keepdims=True)
    y = (xf - mu) * lax.rsqrt(var + EPS)
    return (y * g.astype(F32) + b.astype(F32)).astype(x.dtype)


def depthwise_conv_centred(u, w, b):
    width = w.shape[0]
    out = lax.conv_general_dilated(u, w[:, None, :].astype(u.dtype), window_strides=(1,),
                                   padding=[(width // 2, width // 2)],
                                   dimension_numbers=('NWC', 'WIO', 'NWC'),
                                   feature_group_count=u.shape[-1])
    return out + b.astype(u.dtype)


def pool_mixer(u, w_mix, scale):
    b, L, c = u.shape
    uf = u.astype(F32)
    cs = jnp.concatenate([jnp.zeros((b, 1, c), F32), jnp.cumsum(uf, axis=1)], axis=1)
    pos = jnp.arange(L)
    outs = []
    for g, w in enumerate(POOL_WINDOWS):
        sl = slice(g * POOL_GW, (g + 1) * POOL_GW)
        lo = jnp.clip(pos - w // 2, 0, L - 1)
        hi = jnp.clip(pos + (w - 1 - w // 2), 0, L - 1)
        csg = cs[:, :, sl]
        s = jnp.take(csg, hi + 1, axis=1) - jnp.take(csg, lo, axis=1)
        cnt = (hi - lo + 1).astype(F32)[None, :, None]
        outs.append(s / cnt - uf[:, :, sl])
    pooled = jnp.stack(outs, axis=2)
    mixed = jnp.einsum('blgc,gcd->blgd', pooled, w_mix.astype(F32)).reshape(b, L, c)
    return (mixed * scale.astype(F32)).astype(u.dtype)


def rel_bucket(rel):
    nb = REL_BUCKETS // 2
    max_exact = nb // 2
    ret = jnp.where(rel > 0, nb, 0)
    n = jnp.abs(rel)
    large = max_exact + (jnp.log(jnp.maximum(n, 1).astype(F32) / max_exact)
                         / math.log(REL_MAX_DIST / max_exact) * (nb - max_exact)).astype(jnp.int32)
    large = jnp.minimum(large, nb - 1)
    return ret + jnp.where(n < max_exact, n, large)


def diff_attention(q, k, v, lam, rel_bias):
    b, L = q.shape[0], q.shape[1]
    nblk = L // Q_BLOCK
    scale = DA_HEAD_DIM ** -0.5
    kpos = jnp.arange(L)
    table = rel_bias.astype(F32)
    qb = q.reshape(b, nblk, Q_BLOCK, DA_HEADS, 2, DA_HEAD_DIM).transpose(1, 0, 2, 3, 4, 5)

    def block(args):
        q_blk, start = args
        s = jnp.einsum('bqhmd,bkhmd->bhmqk', q_blk, k, preferred_element_type=F32) * scale
        qpos = start + jnp.arange(Q_BLOCK)
        bias = table[rel_bucket(kpos[None, :] - qpos[:, None])]
        s = s + bias.transpose(2, 0, 1)[None, :, None]
        p = jax.nn.softmax(s, axis=-1)
        a = p[:, :, 0] - lam * p[:, :, 1]
        return jnp.einsum('bhqk,bkhe->bqhe', a.astype(v.dtype), v)

    starts = jnp.arange(nblk, dtype=jnp.int32) * Q_BLOCK
    out = lax.map(block, (qb, starts))
    return out.transpose(1, 0, 2, 3, 4).reshape(b, L, DA_HEADS, DA_V_DIM)


def hyena_filter_freq(L, w1, b1, freq, w2, b2, w3):
    t = jnp.linspace(0.0, 1.0, L, dtype=F32)[:, None]
    w = 2.0 * math.pi * jnp.arange(L, dtype=F32)[:, None] / L
    bands = jnp.linspace(1e-4, HY_BANDS - 1, HY_BANDS, dtype=F32)[None, :]
    emb = jnp.concatenate([t, jnp.cos(bands * w), -jnp.sin(bands * w)], axis=-1)
    freq = freq.astype(F32)
    h = jnp.sin(freq[0] * (emb @ w1.astype(F32) + b1.astype(F32)))
    h = jnp.sin(freq[1] * (h @ w2.astype(F32) + b2.astype(F32)))
    h = (h @ w3.astype(F32)).reshape(L, 2, BRANCH_W)
    max_decay = math.log(HY_DECAY_TARGET) / HY_FAST
    min_decay = math.log(HY_DECAY_TARGET) / HY_SLOW
    deltas = jnp.linspace(min_decay, max_decay, BRANCH_W, dtype=F32)
    decay = jnp.exp(-t * jnp.abs(deltas)[None, :])
    h = h * decay[:, None, :]
    h = h / (jnp.sum(jnp.abs(h), axis=(0, 1), keepdims=True) + EPS)
    filt = jnp.concatenate([h[:, 0], jnp.zeros((1, BRANCH_W), F32), h[:0:-1, 1]], axis=0)
    return jnp.fft.rfft(filt, axis=0)


def hyena_mixer(u, short_w, short_b, filt_f, d_bias):
    L = u.shape[1]
    uc = depthwise_conv_centred(u, short_w, short_b)
    x0, x1, v = jnp.split(uc, 3, axis=-1)
    z = (x1 * v).astype(F32)
    zf = jnp.fft.rfft(z, n=2 * L, axis=1)
    y = jnp.fft.irfft(zf * filt_f[None], n=2 * L, axis=1)[:, :L] + z * d_bias.astype(F32)
    return (x0.astype(F32) * y).astype(u.dtype)


def conformer_conv(u, dw_w, dw_b, ln_g, ln_b):
    a, g = jnp.split(u, 2, axis=-1)
    h = a * jax.nn.sigmoid(g)
    h = depthwise_conv_centred(h, dw_w, dw_b)
    h = layernorm(h, ln_g, ln_b)
    return jax.nn.silu(h)


def hybrid_layer(x, layer_idx, filt_f, norm_g, w_in, pool_w, pool_scale, lq1, lk1, lq2, lk2, subln_g,
                 rel_bias, short_w, short_b, hy_d, dw_w, dw_b, ln_g, ln_b, w_branch, w_out):
    b, L, _ = x.shape
    h = rmsnorm(x, norm_g)
    p = h @ w_in
    ya = pool_mixer(p[..., OFF_POOL:OFF_Q], pool_w, pool_scale)
    q = p[..., OFF_Q:OFF_K].reshape(b, L, DA_HEADS, 2, DA_HEAD_DIM)
    k = p[..., OFF_K:OFF_V].reshape(b, L, DA_HEADS, 2, DA_HEAD_DIM)
    v = p[..., OFF_V:OFF_HY].reshape(b, L, DA_HEADS, DA_V_DIM)
    lam_init = 0.8 - 0.6 * math.exp(-0.3 * layer_idx)
    lam = (jnp.exp(jnp.sum(lq1.astype(F32) * lk1.astype(F32)))
           - jnp.exp(jnp.sum(lq2.astype(F32) * lk2.astype(F32))) + lam_init)
    o = diff_attention(q, k, v, lam, rel_bias)
    yb = (rmsnorm(o, subln_g) * (1.0 - lam_init)).reshape(b, L, BRANCH_W)
    yc = hyena_mixer(p[..., OFF_HY:OFF_CF], short_w, short_b, filt_f, hy_d)
    yd = conformer_conv(p[..., OFF_CF:OFF_SILU], dw_w, dw_b, ln_g, ln_b)
    branches = jnp.stack([ya, yb, yc, yd], axis=2)
    silu_gate = jax.nn.silu(p[..., OFF_SILU:OFF_MERGE].reshape(b, L, N_BRANCH, BRANCH_W))
    proj = jnp.einsum('blnw,nwd->blnd', branches * silu_gate, w_branch)
    merge = jax.nn.sigmoid(p[..., OFF_MERGE:].reshape(b, L, N_BRANCH, D_MODEL))
    mixed = jnp.sum(merge * proj, axis=2)
    return x + mixed @ w_out


def setup_inputs(seed: int = 0) -> dict:
    key = jax.random.key(seed)
    ks = jax.random.split(key, 28)
    W = BRANCH_W

    def nrm(k, shape, s):
        return jax.random.normal(k, shape, F32) * s

    return {
        'x_prompt': nrm(ks[0], (BATCH, SEQ, D_MODEL), 1.0),
        'x_sample': nrm(ks[1], (DEC_BATCH, DEC_SEQ, D_MODEL), 1.0),
        'norm_g': 1.0 + nrm(ks[2], (DEPTH, D_MODEL), 0.02),
        'w_in': nrm(ks[3], (DEPTH, D_MODEL, N_IN), D_MODEL ** -0.5),
        'pool_w': nrm(ks[4], (DEPTH, POOL_GROUPS, POOL_GW, POOL_GW), POOL_GW ** -0.5),
        'pool_scale': 1.0 + nrm(ks[5], (DEPTH, W), 0.1),
        'lambda_q1': nrm(ks[6], (DEPTH, DA_HEAD_DIM), 0.1),
        'lambda_k1': nrm(ks[7], (DEPTH, DA_HEAD_DIM), 0.1),
        'lambda_q2': nrm(ks[8], (DEPTH, DA_HEAD_DIM), 0.1),
        'lambda_k2': nrm(ks[9], (DEPTH, DA_HEAD_DIM), 0.1),
        'subln_g': 1.0 + nrm(ks[10], (DEPTH, DA_V_DIM), 0.02),
        'rel_bias': nrm(ks[11], (REL_BUCKETS, DA_HEADS), 0.5),
        'hy_short_w': nrm(ks[12], (DEPTH, HY_SHORT, 3 * W), HY_SHORT ** -0.5),
        'hy_short_b': nrm(ks[13], (DEPTH, 3 * W), 0.02),
        'hy_w1': nrm(ks[14], (DEPTH, HY_EMB, HY_FILTER_ORDER), HY_EMB ** -0.5),
        'hy_b1': nrm(ks[15], (DEPTH, HY_FILTER_ORDER), 0.02),
        'hy_freq': 1.0 + nrm(ks[16], (DEPTH, 2, HY_FILTER_ORDER), 0.02),
        'hy_w2': nrm(ks[17], (DEPTH, HY_FILTER_ORDER, HY_FILTER_ORDER), HY_FILTER_ORDER ** -0.5),
        'hy_b2': nrm(ks[18], (DEPTH, HY_FILTER_ORDER), 0.02),
        'hy_w3': nrm(ks[19], (DEPTH, HY_FILTER_ORDER, 2 * W), HY_FILTER_ORDER ** -0.5),
        'hy_d': nrm(ks[20], (DEPTH, W), 0.5),
        'cf_dw_w': nrm(ks[21], (DEPTH, CF_WIDTH, W), CF_WIDTH ** -0.5),
        'cf_dw_b': nrm(ks[22], (DEPTH, W), 0.02),
        'cf_ln_g': 1.0 + nrm(ks[23], (DEPTH, W), 0.02),
        'cf_ln_b': nrm(ks[24], (DEPTH, W), 0.02),
        'w_branch': nrm(ks[25], (DEPTH, N_BRANCH, W, D_MODEL), W ** -0.5),
        'w_out': nrm(ks[26], (DEPTH, D_MODEL, D_MODEL), D_MODEL ** -0.5),
        'final_g': 1.0 + nrm(ks[27], (D_MODEL,), 0.02),
    }


def reference(x_prompt, x_sample, norm_g, w_in, pool_w, pool_scale, lambda_q1, lambda_k1, lambda_q2,
              lambda_k2, subln_g, rel_bias, hy_short_w, hy_short_b, hy_w1, hy_b1, hy_freq, hy_w2, hy_b2,
              hy_w3, hy_d, cf_dw_w, cf_dw_b, cf_ln_g, cf_ln_b, w_branch, w_out, final_g):
    hp = x_prompt
    hs = x_sample
    for l in range(DEPTH):
        filt_p = hyena_filter_freq(hp.shape[1], hy_w1[l], hy_b1[l], hy_freq[l], hy_w2[l], hy_b2[l], hy_w3[l])
        filt_s = hyena_filter_freq(hs.shape[1], hy_w1[l], hy_b1[l], hy_freq[l], hy_w2[l], hy_b2[l], hy_w3[l])
        hp = hybrid_layer(hp, l, filt_p, norm_g[l], w_in[l], pool_w[l], pool_scale[l], lambda_q1[l],
                          lambda_k1[l], lambda_q2[l], lambda_k2[l], subln_g[l], rel_bias, hy_short_w[l],
                          hy_short_b[l], hy_d[l], cf_dw_w[l], cf_dw_b[l], cf_ln_g[l], cf_ln_b[l],
                          w_branch[l], w_out[l])
        hs = hybrid_layer(hs, l, filt_s, norm_g[l], w_in[l], pool_w[l], pool_scale[l], lambda_q1[l],
                          lambda_k1[l], lambda_q2[l], lambda_k2[l], subln_g[l], rel_bias, hy_short_w[l],
                          hy_short_b[l], hy_d[l], cf_dw_w[l], cf_dw_b[l], cf_ln_g[l], cf_ln_b[l],
                          w_branch[l], w_out[l])
    y_prompt = rmsnorm(hp, final_g)
    y_sample = rmsnorm(hs, final_g)
    return (y_prompt, y_sample)
```

```python
import math
from contextlib import ExitStack
import numpy as np
import concourse.bass as bass
import concourse.mybir as mybir
from concourse.bass_utils import run_bass_kernel_spmd

F32 = mybir.dt.float32
BF16 = mybir.dt.bfloat16
I32 = mybir.dt.int32
AF = mybir.ActivationFunctionType
ALU = mybir.AluOpType

D = 1024
W = 512
N_IN = 10752
NG4 = 21
EPS = 1e-6
G_POOL, G_Q, G_K, G_V, G_X0, G_X1, G_VH, G_CFA, G_CFG, G_SILU, G_MERGE = 0, 1, 2, 3, 4, 5, 6, 7, 8, 9, 13
CF_WIDTH = 31
HY_FAST, HY_SLOW, HY_DECAY_TARGET = 0.3, 1.5, 1e-2


class Sched:
    def __init__(self, nc, n_dma_sems=8, needed=None):
        self.nc = nc
        self.dry = needed is None
        self.needed = set() if needed is None else needed
        self.engs = {'pe': nc.tensor, 'act': nc.scalar, 'dve': nc.vector, 'pool': nc.gpsimd, 'sp': nc.sync}
        self.sems, self.vals, self.real, self.vmap = {}, {}, {}, {}
        for k in ('pe', 'act', 'dve', 'pool'):
            self.sems['c_' + k] = nc.alloc_semaphore('c_' + k)
            self.vals['c_' + k] = 0
            self.real['c_' + k] = 0
            self.vmap['c_' + k] = {}
        self.dq = {}
        for q in ('sp', 'pool', 'act'):
            self.dq[q] = []
            for i in range(n_dma_sems):
                sk = 'd_%s%d' % (q, i)
                self.sems[sk] = nc.alloc_semaphore(sk)
                self.vals[sk] = 0
                self.dq[q].append(sk)
        self.dqi = {q: 0 for q in self.dq}
        self.seen = {e: {} for e in self.engs}
        self.res = {}
        self.n_ins = 0

    def _deps(self, reads, writes):
        deps = {}

        def add(sk, v):
            if v > deps.get(sk, 0):
                deps[sk] = v
        for r in reads:
            st = self.res.get(r)
            if st and st['w']:
                add(*st['w'])
        for w in writes:
            st = self.res.get(w)
            if st:
                if st['w']:
                    add(*st['w'])
                for sk, v in st['r'].items():
                    add(sk, v)
        return deps

    def _wait(self, ek, deps):
        eng = self.engs[ek]
        seen = self.seen[ek]
        for sk, v in deps.items():
            if ek == 'pe' and sk == 'c_pe':
                continue
            if seen.get(sk, 0) < v:
                seen[sk] = v
                self.n_ins += 1
                if sk.startswith('c_'):
                    if self.dry:
                        self.needed.add((sk, v))
                    else:
                        eng.wait_ge(self.sems[sk], self.vmap[sk][v])
                elif not self.dry:
                    eng.wait_ge(self.sems[sk], v)

    def _update(self, sk, v, reads, writes):
        for r in reads:
            st = self.res.setdefault(r, {'w': None, 'r': {}})
            st['r'][sk] = v
        for w in writes:
            self.res[w] = {'w': (sk, v), 'r': {}}

    def op(self, ek, fn, reads=(), writes=()):
        self._wait(ek, self._deps(reads, writes))
        sk = 'c_' + ek
        self.vals[sk] += 1
        v = self.vals[sk]
        if not self.dry:
            ins = fn(self.engs[ek])
            if (sk, v) in self.needed:
                self.real[sk] += 1
                self.vmap[sk][v] = self.real[sk]
                ins.then_inc(self.sems[sk], 1)
        self.n_ins += 1
        self._update(sk, v, reads, writes)

    def dma(self, q, out, in_, reads=(), writes=(), fn=None):
        self._wait(q, self._deps(reads, writes))
        sk = self.dq[q][self.dqi[q] % len(self.dq[q])]
        self.dqi[q] += 1
        self.vals[sk] += 16
        if not self.dry:
            eng = self.engs[q]
            if fn is not None:
                ins = fn(eng)
            else:
                ins = eng.dma_start(out=out, in_=in_)
            ins.then_inc(self.sems[sk], 16)
        self.n_ins += 1
        self._update(sk, self.vals[sk], reads, writes)

    def barrier(self):
        for ek in self.engs:
            self._wait(ek, {sk: v for sk, v in self.vals.items() if v > 0})
        self.res = {}

    def finish(self):
        self._wait('sp', {sk: v for sk, v in self.vals.items() if v > 0})


def _bucket_table(lo, hi):
    import jax
    import jax.numpy as jnp
    with jax.default_device(jax.devices('cpu')[0]):
        rel = jnp.arange(lo, hi + 1)
        nb, max_exact = 16, 8
        ret = jnp.where(rel > 0, nb, 0)
        n = jnp.abs(rel)
        large = max_exact + (jnp.log(jnp.maximum(n, 1).astype(jnp.float32) / max_exact)
                             / math.log(128 / max_exact) * (nb - max_exact)).astype(jnp.int32)
        large = jnp.minimum(large, nb - 1)
        return np.asarray(ret + jnp.where(n < max_exact, n, large)).astype(np.int64)


class Builder:
    def __init__(self, seqs, n_layers=2, taps=(), stop_after=None, needed=None):
        self.seqs = seqs
        self.n_layers = n_layers
        self.taps = set(taps)
        self.stop_after = stop_after
        self.nc = bass.Bass("TRN2", target_bir_lowering=False)
        self.s = Sched(self.nc, needed=needed)
        self.inputs = {}
        self.consts = {}
        self.core_consts = {}

    def inp(self, name, shape, dtype=F32):
        t = self.nc.dram_tensor(name, list(shape), dtype, kind="ExternalInput")
        self.inputs[name] = t
        return t

    def scratch(self, name, shape, dtype):
        kind = "ExternalOutput" if name in self.taps else "Internal"
        return self.nc.dram_tensor(name, list(shape), dtype, kind=kind)

    def core_const_inp(self, name, arrs):
        arrs = [np.ascontiguousarray(a) for a in arrs]
        dt = {np.dtype('float32'): F32, np.dtype('int32'): I32}[arrs[0].dtype]
        t = self.nc.dram_tensor(name, list(arrs[0].shape), dt, kind="ExternalInput")
        self.core_consts[name] = arrs
        self.consts[name] = arrs[0]
        self.inputs[name] = t
        return t

    def const_inp(self, name, arr):
        arr = np.ascontiguousarray(arr)
        dt = {np.dtype('float32'): F32, np.dtype('int32'): I32}[arr.dtype]
        t = self.nc.dram_tensor(name, list(arr.shape), dt, kind="ExternalInput")
        self.consts[name] = arr
        self.inputs[name] = t
        return t


def build(seqs, n_layers=2, taps=(), stop_after=None, needed=None, own=None):
    B = Builder(seqs, n_layers, taps, stop_after, needed)
    B.own = own
    nc, s = B.nc, B.s
    DEPTH = 2
    uid = [0]

    def SBT(name, shape, dt):
        uid[0] += 1
        return nc.sbuf_tensor('%s_%d' % (name, uid[0]), shape, dt)

    def PST(name, shape, dt):
        uid[0] += 1
        return nc.psum_tensor('%s_%d' % (name, uid[0]), shape, dt)
    x_in = {nm: B.inp('x_' + nm, [L, D]) for nm, L in seqs}
    norm_g = B.inp('norm_g', [DEPTH, D])
    w_in = B.inp('w_in', [DEPTH, D, N_IN])
    pool_w = B.inp('pool_w', [DEPTH, 4, 128, 128])
    pool_scale = B.inp('pool_scale', [DEPTH, W])
    lam_in = {k: B.inp(k, [DEPTH, 64]) for k in ('lambda_q1', 'lambda_k1', 'lambda_q2', 'lambda_k2')}
    subln_g = B.inp('subln_g', [DEPTH, 128])
    rel_bias = B.inp('rel_bias', [32, 4])
    hy_short_w = B.inp('hy_short_w', [DEPTH, 3, 3 * W])
    hy_short_b = B.inp('hy_short_b', [DEPTH, 3 * W])
    hy_w1 = B.inp('hy_w1', [DEPTH, 33, 64])
    hy_b1 = B.inp('hy_b1', [DEPTH, 64])
    hy_freq = B.inp('hy_freq', [DEPTH, 2, 64])
    hy_w2 = B.inp('hy_w2', [DEPTH, 64, 64])
    hy_b2 = B.inp('hy_b2', [DEPTH, 64])
    hy_w3 = B.inp('hy_w3', [DEPTH, 64, 2 * W])
    hy_d = B.inp('hy_d', [DEPTH, W])
    cf_dw_w = B.inp('cf_dw_w', [DEPTH, CF_WIDTH, W])
    cf_dw_b = B.inp('cf_dw_b', [DEPTH, W])
    cf_ln_g = B.inp('cf_ln_g', [DEPTH, W])
    cf_ln_b = B.inp('cf_ln_b', [DEPTH, W])
    w_branch = B.inp('w_branch', [DEPTH, 4, W, D])
    w_out = B.inp('w_out', [DEPTH, D, D])
    final_g = B.inp('final_g', [D])
    y_out = {nm: nc.dram_tensor('y_' + nm, [L, D], F32, kind="ExternalOutput") for nm, L in seqs}

    wbf = B.scratch('wbf', [NG4, 128, 8, 512], BF16)
    wbr_bf = B.scratch('wbr_bf', [128, 16, D], BF16)
    wout_bf = B.scratch('wout_bf', [128, 8, D], BF16)
    SC = {}
    for nm, L in seqs:
        SC[nm] = dict(
            xres=B.scratch('xres_' + nm, [L, D], F32),
            uP=B.scratch('uP_' + nm, [W, L], BF16),
            Qt=B.scratch('Qt_' + nm, [W, L], BF16),
            Kt=B.scratch('Kt_' + nm, [W, L], BF16),
            V=B.scratch('V_' + nm, [L, W], BF16),
            uH=B.scratch('uH_' + nm, [3 * W, L], BF16),
            hcf=B.scratch('hcf_' + nm, [W, L], BF16),
            sgT=B.scratch('sgT_' + nm, [4 * W, L], BF16),
            mgT=B.scratch('mgT_' + nm, [4 * D, L], BF16),
            zT=B.scratch('zT_' + nm, [W, L], BF16),
            x0sT=B.scratch('x0sT_' + nm, [W, L], BF16),
            bsT=B.scratch('bsT_' + nm, [4 * W, L], BF16),
            QtB=B.scratch('QtB_' + nm, [4 * (L // 128), 128, 128], BF16),
            KtB=B.scratch('KtB_' + nm, [4 * (L // 128 + 1), 128, 128], BF16),
            sgB=B.scratch('sgB_' + nm, [4 * (L // 128), 128, 128], BF16),
            bsB=B.scratch('bsB_' + nm, [4 * (L // 128), 128, 128], BF16),
            V2=B.scratch('V2_' + nm, [4, L + 128, 128], BF16),
        )

    ident_np = np.eye(128, dtype=np.float32)
    ident_d = B.const_inp('c_ident', ident_np)

    rr = [0]

    def cast_eng():
        rr[0] += 1
        return ('dve', 'act', 'pool')[rr[0] % 3]

    def copy_on(ek, out, in_):
        if ek == 'act':
            return lambda e: e.copy(out=out, in_=in_)
        return lambda e: e.tensor_copy(out=out, in_=in_)

    def pass_W(l):
        with ExitStack() as es:
            wf = es.enter_context(SBT('wW_f', [128, 2, 8, 512], F32))
            wb = es.enter_context(SBT('wW_b', [128, 2, 8, 512], BF16))
            jobs = []
            for g in range(NG4):
                src = w_in.ap()[l, :, g * 512:(g + 1) * 512].rearrange('(dc p) c -> p dc c', p=128)
                jobs.append((src, wbf.ap()[g], 8, 512))
            for n in range(4):
                for hf in range(2):
                    src = w_branch.ap()[l, n, :, hf * 512:(hf + 1) * 512].rearrange('(wc p) c -> p wc c', p=128)
                    jobs.append((src, wbr_bf.ap()[:, n * 4:(n + 1) * 4, hf * 512:(hf + 1) * 512], 4, 512))
            for hf in range(2):
                src = w_out.ap()[l, :, hf * 512:(hf + 1) * 512].rearrange('(dc p) c -> p dc c', p=128)
                jobs.append((src, wout_bf.ap()[:, :, hf * 512:(hf + 1) * 512], 8, 512))
            for i, (src, dst, a, c) in enumerate(jobs):
                sl = i % 2
                s.dma('sp', wf[:, sl, 0:a, 0:c], src, reads=[], writes=['wWf%d' % sl])
                ek = cast_eng()
                s.op(ek, copy_on(ek, wb[:, sl, 0:a, 0:c], wf[:, sl, 0:a, 0:c]),
                     reads=['wWf%d' % sl], writes=['wWb%d' % sl])
                s.dma('pool', dst, wb[:, sl, 0:a, 0:c], reads=['wWb%d' % sl], writes=['wbf'])
        s.barrier()

    def pass_1(l, nm, L, blk_out=False):
        sc = SC[nm]
        NBLK = L // 128
        SGT = min(2048, L)
        NSG, TPS, GPS = L // SGT, SGT // 128, SGT // 512
        x_src = x_in[nm] if l == 0 else sc['xres']
        with ExitStack() as es:
            xt = es.enter_context(SBT('p1_xt', [128, 2, D], F32))
            junk = es.enter_context(SBT('p1_junk', [128, D], BF16))
            hn = es.enter_context(SBT('p1_hn', [128, 2, D], BF16))
            gt = es.enter_context(SBT('p1_gt', [128, D], F32))
            ss = es.enter_context(SBT('p1_ss', [128, 4], F32))
            idf = es.enter_context(SBT('p1_idf', [128, 128], F32))
            idb = es.enter_context(SBT('p1_idb', [128, 128], BF16))
            hnT = es.enter_context(SBT('p1_hnT', [128, 8, SGT], BF16))
            wt = es.enter_context(SBT('p1_wt', [128, 2, 8, 512], BF16))
            wa = es.enter_context(SBT('p1_wa', [128, 8, 512], BF16))
            stage = es.enter_context(SBT('p1_stage', [128, 2, SGT], BF16))
            vst = es.enter_context(SBT('p1_vst', [128, 2, 512], BF16))
            sig = es.enter_context(SBT('p1_sig', [128, 2, 512], F32))
            pT0 = es.enter_context(PST('p1_pT0', [128, D], BF16))
            pT1 = es.enter_context(PST('p1_pT1', [128, D], BF16))
            pm0 = es.enter_context(PST('p1_pm0', [128, 512], F32))
            pm1 = es.enter_context(PST('p1_pm1', [128, 512], F32))
            pm2 = es.enter_context(PST('p1_pm2', [128, 512], F32))
            pm3 = es.enter_context(PST('p1_pm3', [128, 512], F32))
            pT = [pT0, pT1]
            pm = [pm0, pm1, pm2, pm3]
            s.dma('sp', gt[:], norm_g.ap()[l:l + 1, :].partition_broadcast(128).rearrange('p a d -> p (a d)'), writes=['gt'])
            s.dma('sp', idf[:], ident_d.ap(), writes=['idf'])
            s.op('dve', lambda e: e.tensor_copy(out=idb[:], in_=idf[:]), reads=['idf'], writes=['idb'])
            kk = [0]
            wl = [0]
            if blk_out:
                s.op('pool', lambda e: e.memset(vst[:], 0.0), writes=['vst0', 'vst1'])
                for h_ in range(4):
                    s.dma('pool', sc['KtB'].ap()[h_ * (NBLK + 1) + NBLK], vst[:, 0, 0:128], reads=['vst0'], writes=['p1out'])
                s.dma('pool', sc['V2'].ap()[:, L:L + 128, :].rearrange('h p e -> p h e'),
                      vst[:, 1].rearrange('p (h e) -> p h e', h=4), reads=['vst1'], writes=['p1out'])
            for sg in range(NSG):
                t0 = sg * SGT
                for t in range(TPS):
                    sl = t % 2
                    r0 = t0 + t * 128
                    s.dma('sp', xt[:, sl], x_src.ap()[r0:r0 + 128, :], writes=['xt%d' % sl])
                    s.op('act', lambda e: e.activation(out=junk[:], in_=xt[:, sl], func=AF.Square,
                                                       accum_out=ss[:, sl:sl + 1]),
                         reads=['xt%d' % sl], writes=['junk', 'ss%d' % sl])
                    s.op('dve', lambda e: e.tensor_scalar(out=ss[:, 2 + sl:3 + sl], in0=ss[:, sl:sl + 1], scalar1=1.0 / D,
                                                          scalar2=EPS, op0=ALU.mult, op1=ALU.add),
                         reads=['ss%d' % sl], writes=['rs%d' % sl])
                    s.op('act', lambda e: e.activation(out=ss[:, 2 + sl:3 + sl], in_=ss[:, 2 + sl:3 + sl], func=AF.Sqrt),
                         reads=['rs%d' % sl], writes=['rs%d' % sl])
                    s.op('dve', lambda e: e.reciprocal(out=ss[:, 2 + sl:3 + sl], in_=ss[:, 2 + sl:3 + sl]),
                         reads=['rs%d' % sl], writes=['rs%d' % sl])
                    s.op('dve', lambda e: e.scalar_tensor_tensor(out=hn[:, sl], in0=xt[:, sl], scalar=ss[:, 2 + sl:3 + sl],
                                                                 in1=gt[:], op0=ALU.mult, op1=ALU.mult),
                         reads=['xt%d' % sl, 'rs%d' % sl, 'gt'], writes=['hn%d' % sl])
                    for dc in range(8):
                        s.op('pe', lambda e: e.transpose(out=pT[sl][:, dc * 128:(dc + 1) * 128],
                                                         in_=hn[:, sl, dc * 128:(dc + 1) * 128], identity=idb[:]),
                             reads=['hn%d' % sl, 'idb'], writes=['pT%d' % sl])
                    ek = 'act' if t % 2 else 'dve'
                    s.op(ek, copy_on(ek, hnT[:, :, t * 128:(t + 1) * 128], pT[sl][:].rearrange('p (dc t) -> p dc t', dc=8)),
                         reads=['pT%d' % sl], writes=['hnT_%d' % t])
                for g4 in range(NG4):
                    if g4 == G_CFA:
                        s.dma('sp', wa[:], wbf.ap()[g4], writes=['wa'])
                        continue
                    wsl = wl[0] % 2
                    wl[0] += 1
                    s.dma('sp', wt[:, wsl], wbf.ap()[g4], writes=['wt%d' % wsl])
                    if g4 == G_V:
                        for t in range(TPS):
                            k = kk[0] % 4
                            kk[0] += 1
                            for dc in range(8):
                                s.op('pe', lambda e: e.matmul(pm[k][:], lhsT=hnT[:, dc, t * 128:(t + 1) * 128],
                                                              rhs=wt[:, wsl, dc, :], start=(dc == 0), stop=(dc == 7)),
                                     reads=['hnT_%d' % t, 'wt%d' % wsl], writes=['pm%d' % k])
                            vs = t % 2
                            ek = 'act' if t % 2 else 'dve'
                            s.op(ek, copy_on(ek, vst[:, vs], pm[k][:]), reads=['pm%d' % k], writes=['vst%d' % vs])
                            r0 = t0 + t * 128
                            if blk_out:
                                s.dma('pool', sc['V2'].ap()[:, r0:r0 + 128, :].rearrange('h p e -> p h e'),
                                      vst[:, vs].rearrange('p (h e) -> p h e', h=4), reads=['vst%d' % vs], writes=['V'])
                            else:
                                s.dma('pool', sc['V'].ap()[r0:r0 + 128, :], vst[:, vs], reads=['vst%d' % vs], writes=['V'])
                        continue
                    for j in range(4):
                        stsl = (g4 * 4 + j) % 2
                        for grp in range(GPS):
                            rd = ['hnT_%d' % (grp * 4 + i) for i in range(4)]
                            k = kk[0] % 4
                            kk[0] += 1
                            for dc in range(8):
                                s.op('pe', lambda e: e.matmul(pm[k][:], lhsT=wt[:, wsl, dc, j * 128:(j + 1) * 128],
                                                              rhs=hnT[:, dc, grp * 512:(grp + 1) * 512],
                                                              start=(dc == 0), stop=(dc == 7)),
                                     reads=rd + ['wt%d' % wsl], writes=['pm%d' % k])
                            dst = stage[:, stsl, grp * 512:(grp + 1) * 512]
                            if g4 == G_CFG:
                                k2 = kk[0] % 4
                                kk[0] += 1
                                for dc in range(8):
                                    s.op('pe', lambda e: e.matmul(pm[k2][:], lhsT=wa[:, dc, j * 128:(j + 1) * 128],
                                                                  rhs=hnT[:, dc, grp * 512:(grp + 1) * 512],
                                                                  start=(dc == 0), stop=(dc == 7)),
                                         reads=rd + ['wa'], writes=['pm%d' % k2])
                                sgs = grp % 2
                                s.op('act', lambda e: e.activation(out=sig[:, sgs], in_=pm[k][:], func=AF.Sigmoid),
                                     reads=['pm%d' % k], writes=['sig%d' % sgs])
                                s.op('dve', lambda e: e.tensor_tensor(out=dst, in0=pm[k2][:], in1=sig[:, sgs], op=ALU.mult),
                                     reads=['pm%d' % k2, 'sig%d' % sgs], writes=['stage%d' % stsl])
                            elif g4 >= G_MERGE:
                                s.op('act', lambda e: e.activation(out=dst, in_=pm[k][:], func=AF.Sigmoid),
                                     reads=['pm%d' % k], writes=['stage%d' % stsl])
                            elif g4 >= G_SILU:
                                s.op('act', lambda e: e.activation(out=dst, in_=pm[k][:], func=AF.Silu),
                                     reads=['pm%d' % k], writes=['stage%d' % stsl])
                            else:
                                ek = 'act' if grp % 2 else 'dve'
                                s.op(ek, copy_on(ek, dst, pm[k][:]), reads=['pm%d' % k], writes=['stage%d' % stsl])
                        if g4 == G_POOL:
                            d_ap = sc['uP'].ap()[j * 128:(j + 1) * 128, t0:t0 + SGT]
                        elif g4 == G_Q:
                            d_ap = sc['Qt'].ap()[j * 128:(j + 1) * 128, t0:t0 + SGT]
                        elif g4 == G_K:
                            d_ap = sc['Kt'].ap()[j * 128:(j + 1) * 128, t0:t0 + SGT]
                        elif g4 in (G_X0, G_X1, G_VH):
                            r = (g4 - G_X0) * 512 + j * 128
                            d_ap = sc['uH'].ap()[r:r + 128, t0:t0 + SGT]
                        elif g4 == G_CFG:
                            d_ap = sc['hcf'].ap()[j * 128:(j + 1) * 128, t0:t0 + SGT]
                        elif g4 >= G_MERGE:
                            r = (g4 - G_MERGE) * 512 + j * 128
                            d_ap = sc['mgT'].ap()[r:r + 128, t0:t0 + SGT]
                        else:
                            r = (g4 - G_SILU) * 512 + j * 128
                            d_ap = sc['sgT'].ap()[r:r + 128, t0:t0 + SGT]
                        s.dma('pool', d_ap, stage[:, stsl], reads=['stage%d' % stsl], writes=['p1out'])
                        if blk_out and g4 in (G_Q, G_K, G_SILU + 1):
                            nb_ = SGT // 128
                            b0 = t0 // 128
                            if g4 == G_Q:
                                bt, base = sc['QtB'], j * NBLK
                            elif g4 == G_K:
                                bt, base = sc['KtB'], j * (NBLK + 1)
                            else:
                                bt, base = sc['sgB'], j * NBLK
                            s.dma('pool', bt.ap()[base + b0:base + b0 + nb_].rearrange('b p k -> p b k'),
                                  stage[:, stsl].rearrange('p (b k) -> p b k', k=128), reads=['stage%d' % stsl], writes=['p1out'])
        s.barrier()

    B.pass_1 = pass_1
    ES = ExitStack()
    B.ES = ES
    idf_g = ES.enter_context(SBT('g_idf', [128, 128], F32))
    idb_g = ES.enter_context(SBT('g_idb', [128, 128], BF16))
    ones_g = ES.enter_context(SBT('g_ones', [128, 128], F32))
    PRM = {}
    for nm_, nb_, r_ in (('hsw', 12, 3), ('hsb', 12, 1), ('cdw', 4, 31), ('cdb', 4, 1), ('clg', 4, 1), ('clb', 4, 1),
                         ('psc', 4, 1), ('hyd', 4, 1)):
        PRM[nm_] = ES.enter_context(SBT('prm_' + nm_, [128, nb_, r_], F32))

    def pass_init():
        s.dma('sp', idf_g[:], ident_d.ap(), writes=['idf_g'])
        s.op('dve', lambda e: e.tensor_copy(out=idb_g[:], in_=idf_g[:]), reads=['idf_g'], writes=['idb_g'])
        s.op('dve', lambda e: e.memset(ones_g[:], 1.0), writes=['ones_g'])
        s.barrier()

    def pass_params(l):
        srcs = dict(hsw=hy_short_w.ap()[l], hsb=hy_short_b.ap()[l:l + 1, :], cdw=cf_dw_w.ap()[l],
                    cdb=cf_dw_b.ap()[l:l + 1, :], clg=cf_ln_g.ap()[l:l + 1, :], clb=cf_ln_b.ap()[l:l + 1, :],
                    psc=pool_scale.ap()[l:l + 1, :], hyd=hy_d.ap()[l:l + 1, :])
        with ExitStack() as es:
            stg = es.enter_context(SBT('pp_stg', [32, 1536], F32))
            pp = es.enter_context(PST('pp_ps', [128, 512], F32))
            for nm_, src in srcs.items():
                R, C = src.shape
                s.dma('sp', stg[0:R, 0:C], src, writes=['pp_stg'])
                for b in range(C // 128):
                    s.op('pe', lambda e: e.transpose(out=pp[:, 0:R], in_=stg[0:R, b * 128:(b + 1) * 128],
                                                     identity=idf_g[0:R, 0:R]),
                         reads=['pp_stg'], writes=['pp_ps'])
                    s.op('dve', lambda e: e.tensor_copy(out=PRM[nm_][:, b, :], in_=pp[:, 0:R]),
                         reads=['pp_ps'], writes=['prm'])
        s.barrier()

    B.pass_init = pass_init
    B.pass_params = pass_params

    invcnt_d = {}
    for nm, L in seqs:
        pos = np.arange(L)
        tab = np.zeros((4, L), np.float32)
        for g, w in enumerate((2, 4, 8, 16)):
            lo = np.clip(pos - w // 2, 0, L - 1)
            hi = np.clip(pos + (w - 1 - w // 2), 0, L - 1)
            tab[g] = 1.0 / (hi - lo + 1)
        invcnt_d[nm] = B.const_inp('c_invcnt_' + nm, tab)

    def pass_2a(l, nm, L):
        sc = SC[nm]
        TC = min(2048, L)
        n = TC + 16
        with ExitStack() as es:
            uex = es.enter_context(SBT('a_uex', [128, 2, n], BF16))
            Ab = es.enter_context(SBT('a_A', [128, 2, n], F32))
            invc = es.enter_context(SBT('a_invc', [128, 2, TC], F32))
            tmp = es.enter_context(SBT('a_tmp', [128, TC], F32))
            pooled = es.enter_context(SBT('a_pooled', [128, 2, TC], BF16))
            sgt = es.enter_context(SBT('a_sgt', [128, 2, TC], BF16))
            ost = es.enter_context(SBT('a_ost', [128, 2, TC], BF16))
            wmf = es.enter_context(SBT('a_wmf', [128, 4, 128], F32))
            wmb = es.enter_context(SBT('a_wmb', [128, 4, 128], BF16))
            ps = [es.enter_context(PST('a_ps%d' % i, [128, 512], F32)) for i in range(2)]
            s.dma('sp', wmf[:], pool_w.ap()[l].rearrange('g c d -> c g d'), writes=['wmf'])
            s.op('dve', lambda e: e.tensor_copy(out=wmb[:], in_=wmf[:]), reads=['wmf'], writes=['wmb'])
            it = 0
            for g in range(4):
                for c in range(L // TC):
                    sl = it % 2
                    it += 1
                    t0 = c * TC
                    lo, hi = max(0, t0 - 8), min(L, t0 + TC + 8)
                    e0 = lo - (t0 - 8)
                    if e0 > 0:
                        s.op('pool', lambda e: e.memset(uex[:, sl, 0:e0], 0.0), writes=['uex%d' % sl])
                    if hi - (t0 - 8) < n:
                        s.op('pool', lambda e: e.memset(uex[:, sl, hi - (t0 - 8):n], 0.0), writes=['uex%d' % sl])
                    s.dma('sp', uex[:, sl, e0:e0 + hi - lo], sc['uP'].ap()[g * 128:(g + 1) * 128, lo:hi],
                          writes=['uex%d' % sl])
                    s.dma('sp', invc[:, sl], invcnt_d[nm].ap()[g:g + 1, t0:t0 + TC].partition_broadcast(128)
                          .rearrange('p a t -> p (a t)'), writes=['invc%d' % sl])
                    s.dma('sp', sgt[:, sl], sc['sgT'].ap()[g * 128:(g + 1) * 128, t0:t0 + TC], writes=['sgt%d' % sl])
                    u = uex[:, sl]
                    s.op('dve', lambda e: e.tensor_tensor(out=Ab[:, 0, 1:n], in0=u[:, 1:n], in1=u[:, 0:n - 1], op=ALU.add),
                         reads=['uex%d' % sl], writes=['A0'])
                    cur = 0
                    for st in range(1, g + 1):
                        sh = 1 << st
                        a0 = 2 * sh - 1
                        nxt = 1 - cur
                        s.op('dve', lambda e: e.tensor_tensor(out=Ab[:, nxt, a0:n], in0=Ab[:, cur, a0:n],
                                                              in1=Ab[:, cur, a0 - sh:n - sh], op=ALU.add),
                             reads=['A%d' % cur], writes=['A%d' % nxt])
                        cur = nxt
                    off = 8 + (1 << g) - 1
                    s.op('dve', lambda e: e.tensor_tensor(out=tmp[:], in0=Ab[:, cur, off:off + TC], in1=invc[:, sl], op=ALU.mult),
                         reads=['A%d' % cur, 'invc%d' % sl], writes=['a_tmp'])
                    s.op('dve', lambda e: e.tensor_tensor(out=pooled[:, sl], in0=tmp[:], in1=u[:, 8:8 + TC], op=ALU.subtract),
                         reads=['a_tmp', 'uex%d' % sl], writes=['pooled%d' % sl])
                    for q in range(TC // 512):
                        k = q % 2
                        s.op('pe', lambda e: e.matmul(ps[k][:], lhsT=wmb[:, g, :], rhs=pooled[:, sl, q * 512:(q + 1) * 512],
                                                      start=True, stop=True),
                             reads=['wmb', 'pooled%d' % sl], writes=['a_ps%d' % k])
                        s.op('dve', lambda e: e.scalar_tensor_tensor(out=ost[:, sl, q * 512:(q + 1) * 512], in0=ps[k][:],
                                                                     scalar=PRM['psc'][:, g, 0:1],
                                                                     in1=sgt[:, sl, q * 512:(q + 1) * 512],
                                                                     op0=ALU.mult, op1=ALU.mult),
                             reads=['a_ps%d' % k, 'sgt%d' % sl], writes=['ost%d' % sl])
                    s.dma('pool', sc['bsT'].ap()[g * 128:(g + 1) * 128, t0:t0 + TC], ost[:, sl],
                          reads=['ost%d' % sl], writes=['bsT'])
        s.barrier()

    B.pass_2a = pass_2a

    def pass_2c(l, nm, L):
        sc = SC[nm]
        TC = min(2048, L)
        n = TC + 2
        with ExitStack() as es:
            uex = es.enter_context(SBT('c_uex', [128, 2, 3, n], BF16))
            Dh = es.enter_context(SBT('c_Dh', [128, 9, 128], BF16))
            xc = es.enter_context(SBT('c_xc', [128, 3, 512], F32))
            sgt = es.enter_context(SBT('c_sgt', [128, 2, TC], BF16))
            zst = es.enter_context(SBT('c_zst', [128, 2, TC], BF16))
            xst = es.enter_context(SBT('c_xst', [128, 2, TC], BF16))
            ps = [es.enter_context(PST('c_ps%d' % i, [128, 512], F32)) for i in range(3)]
            it = 0
            for cb in range(4):
                for sj in range(9):
                    s_, j = sj // 3, sj % 3
                    s.op('dve', lambda e: e.tensor_scalar(out=Dh[:, sj, :], in0=idf_g[:], scalar1=PRM['hsw'][:, s_ * 4 + cb, j:j + 1],
                                                          scalar2=None, op0=ALU.mult),
                         reads=['idf_g'], writes=['Dh'])
                for c in range(L // TC):
                    sl = it % 2
                    it += 1
                    t0 = c * TC
                    lo, hi = max(0, t0 - 1), min(L, t0 + TC + 1)
                    e0 = lo - (t0 - 1)
                    if e0 > 0:
                        s.op('pool', lambda e: e.memset(uex[:, sl, :, 0:e0], 0.0), writes=['cuex%d' % sl])
                    if hi - (t0 - 1) < n:
                        s.op('pool', lambda e: e.memset(uex[:, sl, :, hi - (t0 - 1):n], 0.0), writes=['cuex%d' % sl])
                    s.dma('sp', uex[:, sl, :, e0:e0 + hi - lo],
                          sc['uH'].ap().rearrange('(s q) t -> q s t', s=3)[cb * 128:(cb + 1) * 128, :, lo:hi],
                          writes=['cuex%d' % sl])
                    s.dma('sp', sgt[:, sl], sc['sgT'].ap()[2 * W + cb * 128:2 * W + (cb + 1) * 128, t0:t0 + TC],
                          writes=['csgt%d' % sl])
                    for q in range(TC // 512):
                        for s_ in range(3):
                            for j in range(3):
                                s.op('pe', lambda e: e.matmul(ps[s_][:], lhsT=Dh[:, s_ * 3 + j, :],
                                                              rhs=uex[:, sl, s_, q * 512 + j:q * 512 + j + 512],
                                                              start=(j == 0), stop=(j == 2)),
                                     reads=['Dh', 'cuex%d' % sl], writes=['c_ps%d' % s_])
                            s.op('act', lambda e: e.activation(out=xc[:, s_], in_=ps[s_][:], func=AF.Identity,
                                                               bias=PRM['hsb'][:, s_ * 4 + cb, 0:1]),
                                 reads=['c_ps%d' % s_], writes=['xc%d' % s_])
                        s.op('dve', lambda e: e.tensor_tensor(out=zst[:, sl, q * 512:(q + 1) * 512], in0=xc[:, 1], in1=xc[:, 2],
                                                              op=ALU.mult),
                             reads=['xc1', 'xc2'], writes=['zst%d' % sl])
                        s.op('dve', lambda e: e.tensor_tensor(out=xst[:, sl, q * 512:(q + 1) * 512], in0=xc[:, 0],
                                                              in1=sgt[:, sl, q * 512:(q + 1) * 512], op=ALU.mult),
                             reads=['xc0', 'csgt%d' % sl], writes=['xst%d' % sl])
                    s.dma('pool', sc['zT'].ap()[cb * 128:(cb + 1) * 128, t0:t0 + TC], zst[:, sl],
                          reads=['zst%d' % sl], writes=['zT'])
                    s.dma('pool', sc['x0sT'].ap()[cb * 128:(cb + 1) * 128, t0:t0 + TC], xst[:, sl],
                          reads=['xst%d' % sl], writes=['x0sT'])
        s.barrier()

    B.pass_2c = pass_2c

    def pass_2b(l, nm, L):
        sc = SC[nm]
        n = 512 + 30
        with ExitStack() as es:
            hex_ = es.enter_context(SBT('b_hex', [128, 2, 4, n], BF16))
            Dc = es.enter_context(SBT('b_Dc', [128, 4 * CF_WIDTH, 128], BF16))
            onesN = es.enter_context(SBT('b_onesN', [128, 128], F32))
            hc = es.enter_context(SBT('b_hc', [128, 4, 512], F32))
            sq = es.enter_context(SBT('b_sq', [128, 4, 512], F32))
            mean = es.enter_context(SBT('b_mean', [128, 512], F32))
            rstd = es.enter_context(SBT('b_rstd', [128, 512], F32))
            xn = es.enter_context(SBT('b_xn', [128, 2, 512], F32))
            sgt = es.enter_context(SBT('b_sgt', [128, 2, 4, 512], BF16))
            ost = es.enter_context(SBT('b_ost', [128, 2, 4, 512], BF16))
            ps = [es.enter_context(PST('b_ps%d' % i, [128, 512], F32)) for i in range(4)]
            psm = es.enter_context(PST('b_psm', [128, 512], F32))
            psq = es.enter_context(PST('b_psq', [128, 512], F32))
            s.op('dve', lambda e: e.memset(onesN[:], 1.0 / W), writes=['onesN'])
            for blk in range(4):
                for j in range(CF_WIDTH):
                    ek = 'dve' if j % 2 else 'pool'
                    s.op(ek, lambda e: e.tensor_scalar(out=Dc[:, blk * CF_WIDTH + j, :], in0=idf_g[:],
                                                       scalar1=PRM['cdw'][:, blk, j:j + 1], scalar2=None, op0=ALU.mult),
                         reads=['idf_g'], writes=['Dc'])
            for c in range(L // 512):
                sl = c % 2
                t0 = c * 512
                lo, hi = max(0, t0 - 15), min(L, t0 + 512 + 15)
                e0 = lo - (t0 - 15)
                if e0 > 0:
                    s.op('pool', lambda e: e.memset(hex_[:, sl, :, 0:e0], 0.0), writes=['hex%d' % sl])
                if hi - (t0 - 15) < n:
                    s.op('pool', lambda e: e.memset(hex_[:, sl, :, hi - (t0 - 15):n], 0.0), writes=['hex%d' % sl])
                s.dma('sp', hex_[:, sl, :, e0:e0 + hi - lo],
                      sc['hcf'].ap().rearrange('(b q) t -> q b t', b=4)[:, :, lo:hi], writes=['hex%d' % sl])
                s.dma('sp', sgt[:, sl], sc['sgT'].ap()[3 * W:4 * W, t0:t0 + 512].rearrange('(b q) t -> q b t', b=4),
                      writes=['bsgt%d' % sl])
                for blk in range(4):
                    for j in range(CF_WIDTH):
                        s.op('pe', lambda e: e.matmul(ps[blk][:], lhsT=Dc[:, blk * CF_WIDTH + j, :],
                                                      rhs=hex_[:, sl, blk, j:j + 512], start=(j == 0), stop=(j == CF_WIDTH - 1)),
                             reads=['Dc', 'hex%d' % sl], writes=['b_ps%d' % blk])
                    s.op('act', lambda e: e.activation(out=hc[:, blk], in_=ps[blk][:], func=AF.Identity,
                                                       bias=PRM['cdb'][:, blk, 0:1]),
                         reads=['b_ps%d' % blk], writes=['hc%d' % blk])
                    s.op('act', lambda e: e.activation(out=sq[:, blk], in_=ps[blk][:], func=AF.Square,
                                                       bias=PRM['cdb'][:, blk, 0:1]),
                         reads=['b_ps%d' % blk], writes=['sq%d' % blk])
                for blk in range(4):
                    s.op('pe', lambda e: e.matmul(psm[:], lhsT=onesN[:], rhs=hc[:, blk], start=(blk == 0), stop=(blk == 3)),
                         reads=['onesN', 'hc%d' % blk], writes=['psm'])
                for blk in range(4):
                    s.op('pe', lambda e: e.matmul(psq[:], lhsT=onesN[:], rhs=sq[:, blk], start=(blk == 0), stop=(blk == 3)),
                         reads=['onesN', 'sq%d' % blk], writes=['psq'])
                s.op('act', lambda e: e.copy(out=mean[:], in_=psm[:]), reads=['psm'], writes=['mean'])
                s.op('dve', lambda e: e.tensor_tensor(out=rstd[:], in0=mean[:], in1=mean[:], op=ALU.mult),
                     reads=['mean'], writes=['rstd'])
                s.op('dve', lambda e: e.tensor_tensor(out=rstd[:], in0=psq[:], in1=rstd[:], op=ALU.subtract),
                     reads=['psq', 'rstd'], writes=['rstd'])
                s.op('dve', lambda e: e.tensor_scalar(out=rstd[:], in0=rstd[:], scalar1=EPS, scalar2=None, op0=ALU.add),
                     reads=['rstd'], writes=['rstd'])
                s.op('act', lambda e: e.activation(out=rstd[:], in_=rstd[:], func=AF.Sqrt), reads=['rstd'], writes=['rstd'])
                s.op('dve', lambda e: e.reciprocal(out=rstd[:], in_=rstd[:]), reads=['rstd'], writes=['rstd'])
                for blk in range(4):
                    xs = blk % 2
                    s.op('dve', lambda e: e.tensor_tensor(out=xn[:, xs], in0=hc[:, blk], in1=mean[:], op=ALU.subtract),
                         reads=['hc%d' % blk, 'mean'], writes=['xn%d' % xs])
                    s.op('dve', lambda e: e.tensor_tensor(out=xn[:, xs], in0=xn[:, xs], in1=rstd[:], op=ALU.mult),
                         reads=['xn%d' % xs, 'rstd'], writes=['xn%d' % xs])
                    s.op('act', lambda e: e.activation(out=xn[:, xs], in_=xn[:, xs], func=AF.Silu,
                                                       scale=PRM['clg'][:, blk, 0:1], bias=PRM['clb'][:, blk, 0:1]),
                         reads=['xn%d' % xs], writes=['xn%d' % xs])
                    s.op('dve', lambda e: e.tensor_tensor(out=ost[:, sl, blk], in0=xn[:, xs], in1=sgt[:, sl, blk], op=ALU.mult),
                         reads=['xn%d' % xs, 'bsgt%d' % sl], writes=['bost%d' % sl])
                s.dma('pool', sc['bsT'].ap()[3 * W:4 * W, t0:t0 + 512].rearrange('(b q) t -> q b t', b=4), ost[:, sl],
                      reads=['bost%d' % sl], writes=['bsT'])
        s.barrier()

    B.pass_2b = pass_2b

    NA = 1280
    bk = _bucket_table(-640, 639)
    oh = np.zeros((32, NA), np.float32)
    for n_ in range(NA - 1):
        oh[bk[(639 - n_) + 640], n_] = 1.0
    onehot_d = B.const_inp('c_onehot', oh)
    jmat_d = B.const_inp('c_jmat', np.ascontiguousarray(np.eye(128, dtype=np.float32)[::-1]))
    Gd = B.scratch('Gd', [4, NA], F32)
    Bt = B.scratch('Bt', [4, 6, 128, 512], F32)
    cbcol = ES.enter_context(SBT('g_cbcol', [128, 2, 4], F32))
    neglam = ES.enter_context(SBT('g_neglam', [128, 1], F32))
    gsub = ES.enter_context(SBT('g_gsub', [128, 128], F32))

    def pass_att_setup():
        with ExitStack() as es:
            ohs = es.enter_context(SBT('as_oh', [32, NA], F32))
            rb = es.enter_context(SBT('as_rb', [32, 4], F32))
            gs = es.enter_context(SBT('as_gs', [4, NA], F32))
            jm = es.enter_context(SBT('as_jm', [128, 128], F32))
            hk = es.enter_context(SBT('as_hk', [128, 2, 512], F32))
            tt = es.enter_context(SBT('as_tt', [128, 2, 512], F32))
            ps = [es.enter_context(PST('as_ps%d' % i, [128, 512], F32)) for i in range(2)]
            s.dma('sp', ohs[:], onehot_d.ap(), writes=['ohs'])
            s.dma('sp', rb[:], rel_bias.ap(), writes=['rb'])
            s.dma('sp', jm[:], jmat_d.ap(), writes=['jm'])
            s.dma('sp', cbcol[:, 0, :], rel_bias.ap()[15:16, :].partition_broadcast(128).rearrange('p a h -> p (a h)'), writes=['cbcol'])
            s.dma('sp', cbcol[:, 1, :], rel_bias.ap()[31:32, :].partition_broadcast(128).rearrange('p a h -> p (a h)'), writes=['cbcol'])
            for i, (c0, c1) in enumerate(((0, 512), (512, 1024), (1024, NA))):
                k = i % 2
                s.op('pe', lambda e: e.matmul(ps[k][0:4, 0:c1 - c0], lhsT=rb[:], rhs=ohs[:, c0:c1], start=True, stop=True),
                     reads=['rb', 'ohs'], writes=['as_ps%d' % k])
                s.op('dve', lambda e: e.tensor_copy(out=gs[:, c0:c1], in_=ps[k][0:4, 0:c1 - c0]),
                     reads=['as_ps%d' % k], writes=['gs'])
            s.dma('sp', Gd.ap(), gs[:], reads=['gs'], writes=['Gd'])
            it = 0
            for h in range(4):
                for di in range(6):
                    dl = di - 1
                    off = 512 - 128 * dl
                    sl = it % 2
                    it += 1
                    src = bass.AP(tensor=Gd.ap().tensor, offset=h * NA + off, ap=[[1, 128], [1, 512]])
                    s.dma('sp', hk[:, sl], src, reads=['Gd'], writes=['hk%d' % sl])
                    s.op('pe', lambda e: e.matmul(ps[sl][:], lhsT=jm[:], rhs=hk[:, sl], start=True, stop=True),
                         reads=['jm', 'hk%d' % sl], writes=['as_ps%d' % sl])
                    s.op('act', lambda e: e.mul(out=tt[:, sl], in_=ps[sl][:], mul=8.0),
                         reads=['as_ps%d' % sl], writes=['tt%d' % sl])
                    s.dma('pool', Bt.ap()[h, di], tt[:, sl], reads=['tt%d' % sl], writes=['Bt'])
        s.barrier()

    def pass_lambda(l):
        lam_init = 0.8 - 0.6 * math.exp(-0.3 * l)
        with ExitStack() as es:
            lt = es.enter_context(SBT('lm_lt', [128, 4, 64], F32))
            pr = es.enter_context(SBT('lm_pr', [128, 2, 64], F32))
            dd = es.enter_context(SBT('lm_dd', [128, 4], F32))
            for i, k in enumerate(('lambda_q1', 'lambda_k1', 'lambda_q2', 'lambda_k2')):
                s.dma('sp', lt[:, i], lam_in[k].ap()[l:l + 1, :].partition_broadcast(128).rearrange('p a d -> p (a d)'),
                      writes=['lt'])
            for i in range(2):
                s.op('dve', lambda e: e.tensor_tensor(out=pr[:, i], in0=lt[:, 2 * i], in1=lt[:, 2 * i + 1], op=ALU.mult),
                     reads=['lt'], writes=['pr'])
                s.op('act', lambda e: e.activation(out=pr[:, i], in_=pr[:, i], func=AF.Identity, accum_out=dd[:, i:i + 1]),
                     reads=['pr'], writes=['pr', 'dd'])
                s.op('act', lambda e: e.activation(out=dd[:, 2 + i:3 + i], in_=dd[:, i:i + 1], func=AF.Exp),
                     reads=['dd'], writes=['dd'])
            s.op('dve', lambda e: e.tensor_tensor(out=dd[:, 0:1], in0=dd[:, 3:4], in1=dd[:, 2:3], op=ALU.subtract),
                 reads=['dd'], writes=['dd'])
            s.op('dve', lambda e: e.tensor_scalar(out=neglam[:], in0=dd[:, 0:1], scalar1=-lam_init, scalar2=None, op0=ALU.add),
                 reads=['dd'], writes=['neglam'])
            s.dma('sp', gsub[:], subln_g.ap()[l:l + 1, :].partition_broadcast(128).rearrange('p a d -> p (a d)'), writes=['gsub'])
            s.op('act', lambda e: e.mul(out=gsub[:], in_=gsub[:], mul=1.0 - lam_init), reads=['gsub'], writes=['gsub'])
        s.barrier()

    B.pass_att_setup = pass_att_setup
    B.pass_lambda = pass_lambda

    def att_alloc(es, pfx):
        A = {}
        A['Qc'] = es.enter_context(SBT(pfx + 'Qc', [128, 2, 512], BF16))
        A['Pt'] = es.enter_context(SBT(pfx + 'Pt', [128, 2, 2, 512], BF16))
        A['tS'] = es.enter_context(SBT(pfx + 'tS', [128, 2, 2, 512], F32))
        A['osb'] = es.enter_context(SBT(pfx + 'osb', [128, 2, 4, 129], F32))
        A['om'] = es.enter_context(SBT(pfx + 'om', [128, 2, 4, 128], F32))
        A['rc'] = es.enter_context(SBT(pfx + 'rc', [128, 8], F32))
        A['junk'] = es.enter_context(SBT(pfx + 'junk', [128, 128], F32))
        A['yb'] = es.enter_context(SBT(pfx + 'yb', [128, 4, 128], F32))
        A['sgt'] = es.enter_context(SBT(pfx + 'sgt', [128, 2, 512], BF16))
        A['ost'] = es.enter_context(SBT(pfx + 'ost', [128, 2, 512], BF16))
        A['Bth'] = es.enter_context(SBT(pfx + 'Bth', [128, 6, 512], F32))
        A['psS'] = [es.enter_context(PST(pfx + 'psS%d' % i, [128, 2, 512], F32)) for i in range(2)]
        A['psO'] = [[es.enter_context(PST(pfx + 'psO%d%d' % (m, j), [128, 512], F32)) for j in range(2)] for m in range(2)]
        A['tsi'] = 0
        return A

    def att_chunk(A, qs, NTt, getK, getV, kind, kv_res):
        Qc, Pt, tS, osb, om, rc, junk, yb, sgt, ost, Bth, psS, psO = (A[k] for k in (
            'Qc', 'Pt', 'tS', 'osb', 'om', 'rc', 'junk', 'yb', 'sgt', 'ost', 'Bth', 'psS', 'psO'))
        scale = 64 ** -0.5

        def Oap(m, qb):
            return psO[m][0][:, qb * 129:(qb + 1) * 129] if qb < 3 else psO[m][1][:, 0:129]

        def QK(kt):
            k = kt % 2
            for m in range(2):
                s.op('pe', lambda e: e.matmul(psS[k][:, m, :], lhsT=getK(kt)[64 * m:64 * m + 64, :], rhs=Qc[64 * m:64 * m + 64, qs, :],
                                              start=True, stop=True, tile_position=(64 * m, 0)),
                     reads=kv_res + ['aQc%d' % qs], writes=['apsS%d' % k])

        def EXP(kt):
            k = kt % 2
            kd = kind(kt)
            if kd[0] == 'mixed':
                ts_ = A['tsi'] % 2
                A['tsi'] += 1
                for m in range(2):
                    s.op('dve', lambda e: e.tensor_tensor(out=tS[:, ts_, m], in0=psS[k][:, m, :], in1=Bth[:, kd[1], :], op=ALU.add),
                         reads=['apsS%d' % k, 'aBth'], writes=['atS%d' % ts_])
                src = tS[:, ts_].rearrange('p a b -> p (a b)')
                rd = ['atS%d' % ts_]
            else:
                src = psS[k][:].rearrange('p a b -> p (a b)')
                rd = ['apsS%d' % k]
            dst = Pt[:, k].rearrange('p a b -> p (a b)')
            if kd[-1] is None:
                s.op('act', lambda e: e.activation(out=dst, in_=src, func=AF.Exp, scale=scale), reads=rd, writes=['aPt%d' % k])
            else:
                s.op('act', lambda e: e.activation(out=dst, in_=src, func=AF.Exp, scale=scale, bias=kd[-1]),
                     reads=rd + ['cbcol', 'kbc', 'otab'], writes=['aPt%d' % k])

        def PV(kt):
            k = kt % 2
            for m in range(2):
                for qb in range(4):
                    s.op('pe', lambda e: e.matmul(Oap(m, qb), lhsT=Pt[:, k, m, qb * 128:(qb + 1) * 128], rhs=getV(kt),
                                                  start=(kt == 0 and qb in (0, 3)), stop=(kt == NTt - 1)),
                         reads=['aPt%d' % k] + kv_res, writes=['apsO'])
        QK(0)
        for kt in range(NTt):
            if kt + 1 < NTt:
                QK(kt + 1)
            EXP(kt)
            PV(kt)
        for m in range(2):
            ek = 'dve' if m == 0 else 'act'
            s.op(ek, copy_on(ek, osb[:, m, 0:3, :].rearrange('p a b -> p (a b)'), psO[m][0][:, 0:387]),
                 reads=['apsO'], writes=['aosb'])
            s.op('dve', lambda e: e.tensor_copy(out=osb[:, m, 3, :], in_=psO[m][1][:, 0:129]), reads=['apsO'], writes=['aosb'])
        s.op('dve', lambda e: e.reciprocal(out=rc[:, 0:8].rearrange('p (a b) -> p a b', a=2), in_=osb[:, :, :, 128]),
             reads=['aosb'], writes=['arc'])
        for m in range(2):
            for qb in range(4):
                s.op('dve', lambda e: e.tensor_scalar(out=om[:, m, qb], in0=osb[:, m, qb, 0:128],
                                                      scalar1=rc[:, m * 4 + qb:m * 4 + qb + 1], scalar2=None, op0=ALU.mult),
                     reads=['aosb', 'arc'], writes=['aom'])
        psT = psS[0][:, 0, :]
        omf = om[:, 0].rearrange('p a b -> p (a b)')
        s.op('dve', lambda e: e.scalar_tensor_tensor(out=omf, in0=om[:, 1].rearrange('p a b -> p (a b)'), scalar=neglam[:, 0:1],
                                                     in1=omf, op0=ALU.mult, op1=ALU.add),
             reads=['aom', 'neglam'], writes=['aom'])
        sqv = om[:, 1]
        s.op('dve', lambda e: e.tensor_tensor(out=sqv, in0=om[:, 0], in1=om[:, 0], op=ALU.mult), reads=['aom'], writes=['aom1'])
        s.op('dve', lambda e: e.reduce_sum(out=rc[:, 0:4], in_=sqv, axis=mybir.AxisListType.X), reads=['aom1', 'arc'], writes=['arc'])
        s.op('dve', lambda e: e.tensor_scalar(out=rc[:, 0:4], in0=rc[:, 0:4], scalar1=1.0 / 128, scalar2=EPS, op0=ALU.mult, op1=ALU.add),
             reads=['arc'], writes=['arc'])
        s.op('act', lambda e: e.activation(out=rc[:, 0:4], in_=rc[:, 0:4], func=AF.Sqrt), reads=['arc'], writes=['arc'])
        s.op('dve', lambda e: e.reciprocal(out=rc[:, 0:4], in_=rc[:, 0:4]), reads=['arc'], writes=['arc'])
        for qb in range(4):
            s.op('dve', lambda e: e.scalar_tensor_tensor(out=yb[:, qb], in0=om[:, 0, qb], scalar=rc[:, qb:qb + 1],
                                                         in1=gsub[:], op0=ALU.mult, op1=ALU.mult),
                 reads=['aom', 'arc', 'gsub'], writes=['ayb'])
            s.op('pe', lambda e: e.transpose(out=psT[:, qb * 128:(qb + 1) * 128], in_=yb[:, qb], identity=idf_g[:]),
                 reads=['ayb', 'idf_g'], writes=['apsS0'])
        s.op('dve', lambda e: e.tensor_tensor(out=ost[:, qs], in0=psT, in1=sgt[:, qs], op=ALU.mult),
             reads=['apsS0', 'asgt%d' % qs], writes=['aost%d' % qs])

    def pass_4(l, nm, L, heads=(0, 1, 2, 3)):
        sc = SC[nm]
        NT, NJ = L // 128, L // 512
        with ExitStack() as es:
            A = att_alloc(es, 't_')
            Kh2 = es.enter_context(SBT('t_Kh', [128, 2, L], BF16))
            Vh2 = es.enter_context(SBT('t_Vh', [128, 2, NT, 129], BF16))

            def load_kv(hi):
                h_, hb_ = heads[hi], hi % 2
                s.dma('sp', Kh2[:, hb_], sc['Kt'].ap()[h_ * 128:(h_ + 1) * 128, :], writes=['aKh%d' % hb_])
                s.op('pool', lambda e: e.memset(Vh2[:, hb_, :, 128:129], 1.0), writes=['aVh%d' % hb_])
                for v0 in range(0, NT, 32):
                    v1 = min(NT, v0 + 32)
                    s.dma('sp', Vh2[:, hb_, v0:v1, 0:128],
                          sc['V'].ap()[v0 * 128:v1 * 128, h_ * 128:(h_ + 1) * 128].rearrange('(kt p) e -> p kt e', p=128),
                          writes=['aVh%d' % hb_])
            load_kv(0)
            for hi, h in enumerate(heads):
                hb = hi % 2
                Kh, Vh = Kh2[:, hb], Vh2[:, hb]
                if hi + 1 < len(heads):
                    load_kv(hi + 1)
                s.dma('sp', A['Bth'][:], Bt.ap()[h].rearrange('d k q -> k d q'), writes=['aBth'])
                for J in range(NJ):
                    qs = J % 2
                    s.dma('sp', A['Qc'][:, qs], sc['Qt'].ap()[h * 128:(h + 1) * 128, J * 512:(J + 1) * 512], writes=['aQc%d' % qs])
                    s.dma('sp', A['sgt'][:, qs], sc['sgT'].ap()[W + h * 128:W + (h + 1) * 128, J * 512:(J + 1) * 512],
                          writes=['asgt%d' % qs])

                    def kind(kt, J=J, h=h):
                        dl = kt - 4 * J
                        if -1 <= dl <= 4:
                            return ('mixed', dl + 1, None)
                        return ('far', cbcol[:, (0 if dl < 0 else 1), h:h + 1])
                    att_chunk(A, qs, NT, lambda kt, Kh=Kh: Kh[:, kt * 128:(kt + 1) * 128], lambda kt, Vh=Vh: Vh[:, kt, :], kind, ['aKh%d' % hb, 'aVh%d' % hb])
                    s.dma('pool', sc['bsT'].ap()[W + h * 128:W + (h + 1) * 128, J * 512:(J + 1) * 512], A['ost'][:, qs],
                          reads=['aost%d' % qs], writes=['bsT'])
        s.barrier()

    B.pass_4 = pass_4

    OWN = {}
    for nm, L in seqs:
        if not own or nm not in own:
            continue
        NB, s0s = own[nm]
        NBLK = L // 128
        NV = NBLK + 2
        tabs = dict(idxK=[], idxV=[], idxQ=[], idxO=[], msk=[], dcol=[])
        for s0 in s0s:
            slots = [s0 - 1 if s0 > 0 else None] + list(range(s0, s0 + NB)) + [s0 + NB if s0 + NB < NBLK else None]
            slots += list(range(s0 + NB + 1, NBLK)) + list(range(0, max(0, s0 - 1)))
            slots += [None] * (NV - len(slots))
            assert len(slots) == NV and sorted(b for b in slots if b is not None) == list(range(NBLK))
            iK = np.zeros((128, 4, NV), np.int32)
            iV = np.zeros((128, 4, NV), np.int32)
            msk = np.zeros((128, 3, NV), np.float32)
            for v, b in enumerate(slots):
                bb = NBLK if b is None else b
                for h in range(4):
                    iK[:, h, v] = (h * (NBLK + 1) + bb) * 128 + np.arange(128)
                    iV[:, h, v] = h * (L + 128) + bb * 128 + np.arange(128)
                if v >= NB + 2:
                    if b is None:
                        msk[:, 2, v] = 1.0
                    elif b > s0:
                        msk[:, 1, v] = 1.0
                    else:
                        msk[:, 0, v] = 1.0
            iQ = np.zeros((64, 4, 2, NB), np.int32)
            iO = np.zeros((128, 4, NB), np.int32)
            for h in range(4):
                for b in range(NB):
                    for m in range(2):
                        iQ[:, h, m, b] = (h * NBLK + s0 + b) * 128 + m * 64 + np.arange(64)
                    iO[:, h, b] = (h * NBLK + s0 + b) * 128 + np.arange(128)
            dcol = np.zeros((128, 2), np.float32)
            dcol[:, 0] = -30000.0 if slots[0] is None else 0.0
            dcol[:, 1] = -30000.0 if slots[NB + 1] is None else 0.0
            tabs['idxK'].append(iK.reshape(128, -1)); tabs['idxV'].append(iV.reshape(128, -1))
            tabs['idxQ'].append(iQ.reshape(64, -1)); tabs['idxO'].append(iO.reshape(128, -1))
            tabs['msk'].append(msk.reshape(128, -1)); tabs['dcol'].append(dcol)
        OWN[nm] = dict(NB=NB, NV=NV, **{k: B.core_const_inp('o_%s_%s' % (k, nm), v) for k, v in tabs.items()})

    def pass_4o(l, nm, L, heads=(0, 1, 2, 3)):
        sc = SC[nm]
        ow = OWN[nm]
        NB, NV = ow['NB'], ow['NV']
        NJ = NB // 4
        KtBv = sc['KtB'].ap().rearrange('b p k -> (b p) k')
        QtBv = sc['QtB'].ap().rearrange('b p k -> (b p) k')
        sgBv = sc['sgB'].ap().rearrange('b p k -> (b p) k')
        bsBv = sc['bsB'].ap().rearrange('b p k -> (b p) k')
        V2v = sc['V2'].ap().rearrange('h t e -> (h t) e')
        with ExitStack() as es:
            A = att_alloc(es, 'o_')
            iV = es.enter_context(SBT('o_iV', [128, 4 * NV], I32))
            iKf = es.enter_context(SBT('o_iKf', [128, 4 * NV], I32))
            iO = es.enter_context(SBT('o_iO', [128, 4 * NB], I32))
            msk = es.enter_context(SBT('o_msk', [128, 3, NV], F32))
            dcol = es.enter_context(SBT('o_dcol', [128, 2], F32))
            kbc = es.enter_context(SBT('o_kbc', [128, 4, NV], F32))
            kbe = es.enter_context(SBT('o_kbe', [128, 4, 2], F32))
            Kh = es.enter_context(SBT('o_Kh', [128, NV * 128], BF16))
            Vh = es.enter_context(SBT('o_Vh', [128, NV, 129], BF16))

            def gather(out, src2d, idx_ap, reads, writes):
                s.dma('pool', None, None, reads=reads, writes=writes,
                      fn=lambda e: e.indirect_dma_start(out=out, out_offset=None, in_=src2d,
                                                        in_offset=bass.IndirectOffsetOnAxis(ap=idx_ap, axis=0)))
            for nm_, tl in (('idxV', iV), ('idxK', iKf), ('idxO', iO), ('dcol', dcol)):
                s.dma('sp', tl[:], ow[nm_].ap(), writes=['otab'])
            s.dma('sp', msk[:].rearrange('p a b -> p (a b)'), ow['msk'].ap(), writes=['otab'])
            for h in range(4):
                s.op('dve', lambda e: e.tensor_scalar(out=kbc[:, h, :], in0=msk[:, 0, :], scalar1=cbcol[:, 0, h:h + 1], scalar2=None, op0=ALU.mult),
                     reads=['otab', 'cbcol'], writes=['kbc'])
                s.op('dve', lambda e: e.scalar_tensor_tensor(out=kbc[:, h, :], in0=msk[:, 1, :], scalar=cbcol[:, 1, h:h + 1], in1=kbc[:, h, :],
                                                             op0=ALU.mult, op1=ALU.add), reads=['otab', 'cbcol', 'kbc'], writes=['kbc'])
                s.op('dve', lambda e: e.scalar_tensor_tensor(out=kbc[:, h, :], in0=msk[:, 2, :], scalar=-30000.0, in1=kbc[:, h, :],
                                                             op0=ALU.mult, op1=ALU.add), reads=['otab', 'kbc'], writes=['kbc'])
                s.op('dve', lambda e: e.tensor_tensor(out=kbe[:, h, 0:1], in0=cbcol[:, 0, h:h + 1], in1=dcol[:, 0:1], op=ALU.add),
                     reads=['otab', 'cbcol'], writes=['kbc'])
                s.op('dve', lambda e: e.tensor_tensor(out=kbe[:, h, 1:2], in0=cbcol[:, 1, h:h + 1], in1=dcol[:, 1:2], op=ALU.add),
                     reads=['otab', 'cbcol'], writes=['kbc'])
            for h in heads:
                s.op('pool', lambda e: e.memset(Vh[:, :, 128:129], 1.0), writes=['aVh'])
                for v in range(NV):
                    c_ = h * NV + v
                    gather(Kh[:, v * 128:(v + 1) * 128], KtBv, iKf[:, c_:c_ + 1], ['otab', 'p1out'], ['aKh'])
                    gather(Vh[:, v, 0:128], V2v, iV[:, c_:c_ + 1], ['otab', 'p1out'], ['aVh'])
                s.dma('sp', A['Bth'][:], Bt.ap()[h].rearrange('d k q -> k d q'), writes=['aBth'])
                for J in range(NJ):
                    qs = J % 2
                    for b_ in range(4):
                        c_ = h * NB + J * 4 + b_
                        gather(A['Qc'][:, qs, b_ * 128:(b_ + 1) * 128], QtBv, iO[:, c_:c_ + 1], ['otab', 'p1out'], ['aQc%d' % qs])
                        gather(A['sgt'][:, qs, b_ * 128:(b_ + 1) * 128], sgBv, iO[:, c_:c_ + 1], ['otab', 'p1out'], ['asgt%d' % qs])

                    def kind(v, J=J, h=h):
                        dl = v - 4 * J
                        if 0 <= dl <= 5:
                            if v == 0:
                                return ('mixed', dl, dcol[:, 0:1])
                            if v == NB + 1:
                                return ('mixed', dl, dcol[:, 1:2])
                            return ('mixed', dl, None)
                        if v == 0:
                            return ('far', kbe[:, h, 0:1])
                        if v == NB + 1:
                            return ('far', kbe[:, h, 1:2])
                        if v <= NB:
                            return ('far', cbcol[:, (0 if v < 4 * J else 1), h:h + 1])
                        return ('far', kbc[:, h, v:v + 1])
                    att_chunk(A, qs, NV, lambda v: Kh[:, v * 128:(v + 1) * 128], lambda v: Vh[:, v, :], kind, ['aKh', 'aVh'])
                    for b_ in range(4):
                        c_ = h * NB + J * 4 + b_
                        s.dma('pool', None, None, reads=['aost%d' % qs, 'otab'], writes=['bsB'],
                              fn=lambda e: e.indirect_dma_start(out=bsBv, out_offset=bass.IndirectOffsetOnAxis(ap=iO[:, c_:c_ + 1], axis=0),
                                                                in_=A['ost'][:, qs, b_ * 128:(b_ + 1) * 128], in_offset=None))
        s.barrier()

    B.pass_4o = pass_4o

    def pass_5(l, nm, L, last, b_blk=False):
        sc = SC[nm]
        x_src = x_in[nm] if l == 0 else sc['xres']
        with ExitStack() as es:
            wbr = es.enter_context(SBT('f_wbr', [128, 16, D], BF16))
            wo = es.enter_context(SBT('f_wo', [128, 8, D], BF16))
            bs = es.enter_context(SBT('f_bs', [128, 2, 16, 512], BF16))
            mg = es.enter_context(SBT('f_mg', [128, 2, 4, 512], BF16))
            acc = es.enter_context(SBT('f_acc', [128, 512], F32))
            tmp = es.enter_context(SBT('f_tmp', [128, 2, 512], F32))
            mx = es.enter_context(SBT('f_mx', [128, 8, 512], BF16))
            xt = es.enter_context(SBT('f_xt', [128, 2, D], F32))
            xo = es.enter_context(SBT('f_xo', [128, 2, D], F32))
            gt = es.enter_context(SBT('f_gt', [128, D], F32))
            junk = es.enter_context(SBT('f_junk', [128, D], BF16))
            ss = es.enter_context(SBT('f_ss', [128, 4], F32))
            psp = [es.enter_context(PST('f_psp%d' % i, [128, 512], F32)) for i in range(4)]
            pso = [es.enter_context(PST('f_pso%d' % i, [128, 512], F32)) for i in range(2)]
            s.dma('sp', wbr[:], wbr_bf.ap(), writes=['wbr'])
            s.dma('sp', wo[:], wout_bf.ap(), writes=['wo'])
            if last:
                s.dma('sp', gt[:], final_g.ap().rearrange('(a d) -> a d', a=1).partition_broadcast(128).rearrange('p a d -> p (a d)'),
                      writes=['fgt'])
            mgv = sc['mgT'].ap().rearrange('(n r) t -> r n t', n=4)
            mi = 0
            ti = 0
            for g in range(L // 512):
                bsl = g % 2
                t0 = g * 512
                if not b_blk:
                    s.dma('sp', bs[:, bsl], sc['bsT'].ap()[:, t0:t0 + 512].rearrange('(a p) t -> p a t', p=128), writes=['bs%d' % bsl])
                else:
                    s.dma('sp', bs[:, bsl, 0:4], sc['bsT'].ap()[0:W, t0:t0 + 512].rearrange('(a p) t -> p a t', p=128), writes=['bs%d' % bsl])
                    s.dma('sp', bs[:, bsl, 8:16], sc['bsT'].ap()[2 * W:4 * W, t0:t0 + 512].rearrange('(a p) t -> p a t', p=128), writes=['bs%d' % bsl])
                    NBK = L // 128
                    for h_ in range(4):
                        s.dma('sp', bs[:, bsl, 4 + h_, :].rearrange('p (b k) -> p b k', k=128),
                              sc['bsB'].ap()[h_ * NBK + g * 4:h_ * NBK + g * 4 + 4].rearrange('b p k -> p b k'), writes=['bs%d' % bsl])
                for dmb in range(8):
                    msl = mi % 2
                    mi += 1
                    s.dma('sp', mg[:, msl], mgv[dmb * 128:(dmb + 1) * 128, :, t0:t0 + 512], writes=['mg%d' % msl])
                    for n in range(4):
                        for wc in range(4):
                            s.op('pe', lambda e: e.matmul(psp[n][:], lhsT=wbr[:, n * 4 + wc, dmb * 128:(dmb + 1) * 128],
                                                          rhs=bs[:, bsl, n * 4 + wc, :], start=(wc == 0), stop=(wc == 3)),
                                 reads=['wbr', 'bs%d' % bsl], writes=['psp%d' % n])
                        if n == 0:
                            s.op('dve', lambda e: e.tensor_tensor(out=acc[:], in0=psp[n][:], in1=mg[:, msl, n], op=ALU.mult),
                                 reads=['psp%d' % n, 'mg%d' % msl], writes=['acc'])
                        else:
                            ts_ = n % 2
                            s.op('dve', lambda e: e.tensor_tensor(out=tmp[:, ts_], in0=psp[n][:], in1=mg[:, msl, n], op=ALU.mult),
                                 reads=['psp%d' % n, 'mg%d' % msl], writes=['ftmp%d' % ts_])
                            if n < 3:
                                s.op('pool', lambda e: e.tensor_tensor(out=acc[:], in0=acc[:], in1=tmp[:, ts_], op=ALU.add),
                                     reads=['acc', 'ftmp%d' % ts_], writes=['acc'])
                            else:
                                s.op('pool', lambda e: e.tensor_tensor(out=mx[:, dmb], in0=acc[:], in1=tmp[:, ts_], op=ALU.add),
                                     reads=['acc', 'ftmp%d' % ts_], writes=['mx%d' % dmb])
                for tt in range(4):
                    xs = ti % 2
                    ti += 1
                    r0 = t0 + tt * 128
                    s.dma('sp', xt[:, xs], x_src.ap()[r0:r0 + 128, :], writes=['fxt%d' % xs])
                    for hf in range(2):
                        for dc in range(8):
                            s.op('pe', lambda e: e.matmul(pso[hf][:], lhsT=mx[:, dc, tt * 128:(tt + 1) * 128],
                                                          rhs=wo[:, dc, hf * 512:(hf + 1) * 512], start=(dc == 0), stop=(dc == 7)),
                                 reads=['mx%d' % dc, 'wo'], writes=['pso%d' % hf])
                        s.op('dve', lambda e: e.tensor_tensor(out=xo[:, xs, hf * 512:(hf + 1) * 512], in0=pso[hf][:],
                                                              in1=xt[:, xs, hf * 512:(hf + 1) * 512], op=ALU.add),
                             reads=['pso%d' % hf, 'fxt%d' % xs], writes=['fxo%d' % xs])
                    if not last:
                        s.dma('pool', sc['xres'].ap()[r0:r0 + 128, :], xo[:, xs], reads=['fxo%d' % xs], writes=['xres'])
                    else:
                        s.op('act', lambda e: e.activation(out=junk[:], in_=xo[:, xs], func=AF.Square, accum_out=ss[:, xs:xs + 1]),
                             reads=['fxo%d' % xs], writes=['fjunk', 'fss%d' % xs])
                        s.op('dve', lambda e: e.tensor_scalar(out=ss[:, 2 + xs:3 + xs], in0=ss[:, xs:xs + 1], scalar1=1.0 / D,
                                                              scalar2=EPS, op0=ALU.mult, op1=ALU.add),
                             reads=['fss%d' % xs], writes=['frs%d' % xs])
                        s.op('act', lambda e: e.activation(out=ss[:, 2 + xs:3 + xs], in_=ss[:, 2 + xs:3 + xs], func=AF.Sqrt),
                             reads=['frs%d' % xs], writes=['frs%d' % xs])
                        s.op('dve', lambda e: e.reciprocal(out=ss[:, 2 + xs:3 + xs], in_=ss[:, 2 + xs:3 + xs]),
                             reads=['frs%d' % xs], writes=['frs%d' % xs])
                        s.op('dve', lambda e: e.scalar_tensor_tensor(out=xt[:, xs], in0=xo[:, xs], scalar=ss[:, 2 + xs:3 + xs],
                                                                     in1=gt[:], op0=ALU.mult, op1=ALU.mult),
                             reads=['fxo%d' % xs, 'frs%d' % xs, 'fgt'], writes=['fxt%d' % xs])
                        s.dma('pool', y_out[nm].ap()[r0:r0 + 128, :], xt[:, xs], reads=['fxt%d' % xs], writes=['yout'])
        s.barrier()

    B.pass_5 = pass_5

    HC = {}
    for nm, L in seqs:
        if L % 8192:
            continue
        N = 2 * L
        N2 = N // 128
        NC2 = N2 // 128
        if N2 in HC:
            continue
        a128 = np.arange(128)
        th = 2 * np.pi * np.outer(a128, a128) / 128.0
        FA = np.concatenate([np.cos(th), -np.sin(th)], 1)
        n2 = np.arange(N2)
        ph = 2 * np.pi * np.outer(n2, a128) / N
        Tr, Ti = np.cos(ph), -np.sin(ph)
        TT4 = np.stack([np.repeat(Tr[:, None, :], 4, 1), np.repeat(Ti[:, None, :], 4, 1)], 1)
        TT4 = TT4.reshape(NC2, 128, 2, 4, 128).transpose(1, 0, 2, 3, 4)
        TTt = np.stack([np.repeat(Tr.T[:, None, :], 2, 1), np.repeat(Ti.T[:, None, :], 2, 1)], 1)
        cph = 2 * np.pi * np.outer(n2, n2) / N2
        Cr, Ci = np.cos(cph), -np.sin(cph)
        CC = np.zeros((128, NC2, NC2, 3, 128))
        for j in range(NC2):
            for kc in range(NC2):
                blk = (slice(j * 128, (j + 1) * 128), slice(kc * 128, (kc + 1) * 128))
                CC[:, j, kc, 0], CC[:, j, kc, 1], CC[:, j, kc, 2] = Cr[blk], Ci[blk], -Ci[blk]
        CI1 = np.concatenate([Cr, -Ci], 1).reshape(NC2, 128, 2 * N2).transpose(1, 0, 2)
        CI2 = np.concatenate([Ci, Cr], 1).reshape(NC2, 128, 2 * N2).transpose(1, 0, 2)
        thi = 2 * np.pi * np.outer(a128, np.arange(64)) / 128.0
        FI = np.stack([np.cos(thi) / N, -np.sin(thi) / N], 1)
        f32 = lambda a: np.ascontiguousarray(a, dtype=np.float32)
        HC[N2] = dict(FA=B.const_inp('c_FA', f32(FA)) if 'FA' not in HC.get('_', {}) else HC['_']['FA'],
                      TT4=B.const_inp('c_TT4_%d' % N2, f32(TT4)), TTt=B.const_inp('c_TTt_%d' % N2, f32(TTt)),
                      CC=B.const_inp('c_CC_%d' % N2, f32(CC)), CI1=B.const_inp('c_CI1_%d' % N2, f32(CI1)),
                      CI2=B.const_inp('c_CI2_%d' % N2, f32(CI2)), FI=B.const_inp('c_FI_%d' % N2, f32(FI)))
        HC.setdefault('_', {})['FA'] = HC[N2]['FA']
    HS = {}
    for nm, L in seqs:
        if L % 8192:
            continue
        N = 2 * L
        import jax
        import jax.numpy as jnp
        with jax.default_device(jax.devices('cpu')[0]):
            tt_ = jnp.linspace(0.0, 1.0, L, dtype=jnp.float32)[:, None]
            w_ = 2.0 * math.pi * jnp.arange(L, dtype=jnp.float32)[:, None] / L
            bands = jnp.linspace(1e-4, 15, 16, dtype=jnp.float32)[None, :]
            emb = np.asarray(jnp.concatenate([tt_, jnp.cos(bands * w_), -jnp.sin(bands * w_)], axis=-1))
            max_decay = math.log(HY_DECAY_TARGET) / HY_FAST
            min_decay = math.log(HY_DECAY_TARGET) / HY_SLOW
            deltas = np.asarray(jnp.abs(jnp.linspace(min_decay, max_decay, W, dtype=jnp.float32)))
        posn = np.concatenate([np.arange(L), (2 * L - np.arange(L, 2 * L)) % L])
        HS[nm] = dict(
            embT=B.const_inp('c_embT_' + nm, np.ascontiguousarray(emb[posn].T)),
            negd=B.const_inp('c_negd', np.ascontiguousarray(-deltas.reshape(4, 128).T)) if 'negd' not in HS.get('_', {}) else HS['_']['negd'],
            filt=B.scratch('filt_' + nm, [W, N], F32),
            filtn=B.scratch('filtn_' + nm, [W, N], BF16),
            Hd=B.scratch('Hd_' + nm, [N // 128 // 128, 2, 128, W, 128], BF16),
        )
        HS.setdefault('_', {})['negd'] = HS[nm]['negd']

    def pass_F(l, nm, L):
        N = 2 * L
        hs = HS[nm]
        NCH = N // 2048
        with ExitStack() as es:
            w1 = es.enter_context(SBT('F_w1', [33, 64], F32))
            w2 = es.enter_context(SBT('F_w2', [64, 64], F32))
            w3 = es.enter_context(SBT('F_w3', [64, 2 * W], F32))
            cols = es.enter_context(SBT('F_cols', [64, 8], F32))
            negd = es.enter_context(SBT('F_negd', [128, 4], F32))
            emb = es.enter_context(SBT('F_emb', [33, 2, 2048], F32))
            trow = es.enter_context(SBT('F_trow', [128, 2, 2048], F32))
            arg = es.enter_context(SBT('F_arg', [64, 2048], F32))
            kf = es.enter_context(SBT('F_kf', [64, 2048], F32))
            ki = es.enter_context(SBT('F_ki', [64, 2048], I32))
            sn = es.enter_context(SBT('F_sn', [64, 2, 2048], F32))
            h1 = es.enter_context(SBT('F_h1', [64, 2048], F32))
            h2 = es.enter_context(SBT('F_h2', [64, 2048], F32))
            dec = es.enter_context(SBT('F_dec', [128, 2048], F32))
            fo = es.enter_context(SBT('F_fo', [128, 2, 2048], F32))
            junk = es.enter_context(SBT('F_junk', [128, 2048], F32))
            asum = es.enter_context(SBT('F_asum', [128, 4, NCH], F32))
            rn = es.enter_context(SBT('F_rn', [128, 8], F32))
            fl = es.enter_context(SBT('F_fl', [128, 2, 2048], F32))
            fb = es.enter_context(SBT('F_fb', [128, 2, 2048], BF16))
            psh = es.enter_context(PST('F_psh', [128, 4, 512], F32))
            psf = [es.enter_context(PST('F_psf%d' % i, [128, 512], F32)) for i in range(2)]
            s.dma('sp', w1[:], hy_w1.ap()[l], writes=['Fw'])
            s.dma('sp', w2[:], hy_w2.ap()[l], writes=['Fw'])
            s.dma('sp', w3[:], hy_w3.ap()[l], writes=['Fw'])
            s.dma('sp', cols[:, 0:1], hy_b1.ap()[l].rearrange('(p o) -> p o', o=1), writes=['Fcols'])
            s.dma('sp', cols[:, 1:2], hy_freq.ap()[l, 0].rearrange('(p o) -> p o', o=1), writes=['Fcols'])
            s.dma('sp', cols[:, 2:3], hy_b2.ap()[l].rearrange('(p o) -> p o', o=1), writes=['Fcols'])
            s.dma('sp', cols[:, 3:4], hy_freq.ap()[l, 1].rearrange('(p o) -> p o', o=1), writes=['Fcols'])
            s.dma('sp', negd[:], hs['negd'].ap(), writes=['Fnegd'])
            s.op('dve', lambda e: e.tensor_tensor(out=cols[:, 4:5], in0=cols[:, 0:1], in1=cols[:, 1:2], op=ALU.mult),
                 reads=['Fcols'], writes=['Fcols'])
            s.op('dve', lambda e: e.tensor_tensor(out=cols[:, 5:6], in0=cols[:, 2:3], in1=cols[:, 3:4], op=ALU.mult),
                 reads=['Fcols'], writes=['Fcols'])
            s.op('dve', lambda e: e.memset(cols[:, 6:7], math.pi / 2), reads=['Fcols'], writes=['Fcols'])
            s.op('dve', lambda e: e.memset(cols[:, 7:8], 0.0), reads=['Fcols'], writes=['Fcols'])

            def sin_layer(ps, fcol, fbcol, out):
                s.op('act', lambda e: e.activation(out=arg[:], in_=ps, func=AF.Identity, scale=cols[:, fcol:fcol + 1],
                                                   bias=cols[:, fbcol:fbcol + 1]), reads=['psh', 'Fcols'], writes=['arg'])
                s.op('dve', lambda e: e.tensor_scalar(out=kf[:], in0=arg[:], scalar1=1.0 / (2 * math.pi), scalar2=None, op0=ALU.mult),
                     reads=['arg'], writes=['kf'])
                s.op('dve', lambda e: e.tensor_copy(out=ki[:], in_=kf[:]), reads=['kf'], writes=['ki'])
                s.op('dve', lambda e: e.tensor_copy(out=kf[:], in_=ki[:]), reads=['ki'], writes=['kf'])
                s.op('dve', lambda e: e.scalar_tensor_tensor(out=arg[:], in0=kf[:], scalar=-2 * math.pi, in1=arg[:],
                                                             op0=ALU.mult, op1=ALU.add), reads=['kf', 'arg'], writes=['arg'])
                s.op('act', lambda e: e.activation(out=sn[:, 0], in_=arg[:], func=AF.Sin, scale=0.25, bias=cols[:, 7:8]),
                     reads=['arg', 'Fcols'], writes=['sn0'])
                s.op('act', lambda e: e.activation(out=sn[:, 1], in_=arg[:], func=AF.Sin, scale=0.25, bias=cols[:, 6:7]),
                     reads=['arg', 'Fcols'], writes=['sn1'])
                s.op('dve', lambda e: e.tensor_tensor(out=kf[:], in0=sn[:, 0], in1=sn[:, 0], op=ALU.mult), reads=['sn0'], writes=['kf'])
                s.op('dve', lambda e: e.tensor_scalar(out=kf[:], in0=kf[:], scalar1=-2.0, scalar2=1.0, op0=ALU.mult, op1=ALU.add),
                     reads=['kf'], writes=['kf'])
                s.op('dve', lambda e: e.tensor_tensor(out=sn[:, 0], in0=sn[:, 0], in1=sn[:, 1], op=ALU.mult),
                     reads=['sn0', 'sn1'], writes=['sn0'])
                s.op('dve', lambda e: e.scalar_tensor_tensor(out=out, in0=sn[:, 0], scalar=4.0, in1=kf[:], op0=ALU.mult, op1=ALU.mult),
                     reads=['sn0', 'kf'], writes=['Fh'])

            pshv = psh[0:64].rearrange('p a b -> p (a b)')
            for ch in range(NCH):
                sl = ch % 2
                c0 = ch * 2048
                d = 0 if c0 < L else 1
                s.dma('sp', emb[:, sl], hs['embT'].ap()[:, c0:c0 + 2048], writes=['emb%d' % sl])
                s.dma('sp', trow[:, sl], hs['embT'].ap()[0:1, c0:c0 + 2048].partition_broadcast(128).rearrange('p a t -> p (a t)'),
                      writes=['trow%d' % sl])
                for q in range(4):
                    s.op('pe', lambda e: e.matmul(psh[0:64, q, :], lhsT=w1[:], rhs=emb[:, sl, q * 512:(q + 1) * 512], start=True, stop=True),
                         reads=['Fw', 'emb%d' % sl], writes=['psh'])
                sin_layer(pshv, 1, 4, h1[:])
                for q in range(4):
                    s.op('pe', lambda e: e.matmul(psh[0:64, q, :], lhsT=w2[:], rhs=h1[:, q * 512:(q + 1) * 512], start=True, stop=True),
                         reads=['Fw', 'Fh'], writes=['psh'])
                sin_layer(pshv, 3, 5, h2[:])
                for cb in range(4):
                    fk = cb % 2
                    s.op('act', lambda e: e.activation(out=dec[:], in_=trow[:, sl], func=AF.Exp, scale=negd[:, cb:cb + 1]),
                         reads=['trow%d' % sl, 'Fnegd'], writes=['dec'])
                    for q in range(4):
                        k = q % 2
                        s.op('pe', lambda e: e.matmul(psf[k][:], lhsT=w3[:, d * W + cb * 128:d * W + (cb + 1) * 128], rhs=h2[:, q * 512:(q + 1) * 512],
                                                      start=True, stop=True), reads=['Fw', 'Fh'], writes=['psf%d' % k])
                        s.op('dve', lambda e: e.tensor_tensor(out=fo[:, fk, q * 512:(q + 1) * 512], in0=psf[k][:], in1=dec[:, q * 512:(q + 1) * 512],
                                                              op=ALU.mult), reads=['psf%d' % k, 'dec'], writes=['fo%d' % fk])
                    s.op('act', lambda e: e.activation(out=junk[:], in_=fo[:, fk], func=AF.Abs, accum_out=asum[:, cb, ch:ch + 1]),
                         reads=['fo%d' % fk], writes=['Fjunk', 'asum'])
                    s.dma('pool', hs['filt'].ap()[cb * 128:(cb + 1) * 128, c0:c0 + 2048], fo[:, fk], reads=['fo%d' % fk], writes=['filt'])
            for cb in range(4):
                s.op('act', lambda e: e.activation(out=junk[:, 0:NCH], in_=asum[:, cb, :], func=AF.Identity, accum_out=rn[:, cb:cb + 1]),
                     reads=['asum'], writes=['Fjunk', 'rn'])
            s.op('dve', lambda e: e.tensor_scalar(out=rn[:, 0:4], in0=rn[:, 0:4], scalar1=EPS, scalar2=None, op0=ALU.add),
                 reads=['rn'], writes=['rn'])
            s.op('dve', lambda e: e.reciprocal(out=rn[:, 4:8], in_=rn[:, 0:4]), reads=['rn'], writes=['rn'])
            it = 0
            for cb in range(4):
                for c0 in range(0, N, 2048):
                    sl = it % 2
                    it += 1
                    s.dma('sp', fl[:, sl], hs['filt'].ap()[cb * 128:(cb + 1) * 128, c0:c0 + 2048], reads=['filt'], writes=['fl%d' % sl])
                    s.op('dve', lambda e: e.tensor_scalar(out=fl[:, sl], in0=fl[:, sl], scalar1=rn[:, 4 + cb:5 + cb], scalar2=None,
                                                          op0=ALU.mult), reads=['fl%d' % sl, 'rn'], writes=['fl%d' % sl])
                    if c0 == 0:
                        s.op('dve', lambda e: e.tensor_tensor(out=fl[:, sl, 0:1], in0=fl[:, sl, 0:1], in1=PRM['hyd'][:, cb, 0:1], op=ALU.add),
                             reads=['fl%d' % sl], writes=['fl%d' % sl])
                    if c0 <= L < c0 + 2048:
                        s.op('dve', lambda e: e.memset(fl[:, sl, L - c0:L - c0 + 1], 0.0), reads=['fl%d' % sl], writes=['fl%d' % sl])
                    s.op('act', lambda e: e.copy(out=fb[:, sl], in_=fl[:, sl]), reads=['fl%d' % sl], writes=['fb%d' % sl])
                    s.dma('pool', hs['filtn'].ap()[cb * 128:(cb + 1) * 128, c0:c0 + 2048], fb[:, sl], reads=['fb%d' % sl], writes=['filtn'])
        s.barrier()

    B.pass_F = pass_F

    def pass_H(l, nm, L, mode, cgroups=None):
        sc = SC[nm]
        hs = HS[nm]
        N = 2 * L
        N2 = N // 128
        NC2 = N2 // 128
        hc = HC[N2]
        CG = 32
        KA = 128 if mode == 'filter' else 64
        src = hs['filtn'] if mode == 'filter' else sc['zT']
        with ExitStack() as es:
            cf32 = es.enter_context(SBT('H_cf32', [128, 3 * NC2 * NC2 * 128], F32))
            FAb = es.enter_context(SBT('H_FA', [128, 256], BF16))
            TT4 = es.enter_context(SBT('H_TT4', [128, NC2, 2, 4, 128], F32))
            CCb = es.enter_context(SBT('H_CC', [128, NC2, NC2, 3, 128], BF16))
            zA = es.enter_context(SBT('H_zA', [128, CG, N2], BF16))
            ApT = es.enter_context(SBT('H_ApT', [128, NC2, 2, CG, 128], BF16))
            tq = es.enter_context(SBT('H_tq', [128, 2, 4, 1024], F32))
            cq = es.enter_context(SBT('H_cq', [128, 2, 2, 1024], F32))
            tqi = [0]
            PB2 = [es.enter_context(PST('H_pb%d' % i, [128, 2, 512], F32)) for i in range(4)]
            pbi = [0, 0, 0, 0]
            if mode == 'data':
                TTt = es.enter_context(SBT('H_TTt', [128, 2, 2, N2], F32))
                CI1 = es.enter_context(SBT('H_CI1', [128, NC2, 2 * N2], BF16))
                CI2 = es.enter_context(SBT('H_CI2', [128, NC2, 2 * N2], BF16))
                FIb = es.enter_context(SBT('H_FI', [128, 2, 64], BF16))
                x0s = es.enter_context(SBT('H_x0s', [64, CG, N2], BF16))
                Yt = es.enter_context(SBT('H_Yt', [128, NC2, 2, CG, 128], BF16))
                Ht = es.enter_context(SBT('H_Ht', [128, 2, 2, 4, 128], BF16))
                Bp = es.enter_context(SBT('H_Bp', [128, 2, 2, 2, N2], BF16))
                ost = es.enter_context(SBT('H_ost', [64, CG, N2], BF16))
            else:
                Xo = es.enter_context(SBT('H_Xo', [128, 2, 2, 4, 128], BF16))

            def load_cast(dst, src_ap, shape_cols):
                v = cf32[:, 0:shape_cols]
                s.dma('sp', v, src_ap, writes=['cf32'])
                s.op('dve', lambda e: e.tensor_copy(out=dst, in_=v), reads=['cf32'], writes=['Hconst'])
            load_cast(FAb[:], hc['FA'].ap(), 256)
            load_cast(CCb[:].rearrange('p a b c d -> p (a b c d)'), hc['CC'].ap().rearrange('p a b c d -> p (a b c d)'), NC2 * NC2 * 3 * 128)
            s.dma('sp', TT4[:], hc['TT4'].ap(), writes=['Hconst'])
            if mode == 'data':
                load_cast(CI1[:].rearrange('p a b -> p (a b)'), hc['CI1'].ap().rearrange('p a b -> p (a b)'), NC2 * 2 * N2)
                load_cast(CI2[:].rearrange('p a b -> p (a b)'), hc['CI2'].ap().rearrange('p a b -> p (a b)'), NC2 * 2 * N2)
                load_cast(FIb[:].rearrange('p a b -> p (a b)'), hc['FI'].ap().rearrange('p a b -> p (a b)'), 128)
                s.dma('sp', TTt[:], hc['TTt'].ap(), writes=['Hconst'])

            def cmul(pr, pi, tr, ti, outr, outi, conj, n, psn='psC', outn='cm_out', tabn='Hconst'):
                tb = tqi[0] % 2
                tqi[0] += 1
                t1, t2, t3, t4 = (tq[:, tb, i, 0:n] for i in range(4))
                sh = list(pr.shape)

                def vw(a):
                    return a if len(sh) == 2 else a.rearrange('p (a b) -> p a b', a=sh[1])
                s.op('dve', lambda e: e.tensor_tensor(out=vw(t1), in0=pr, in1=tr, op=ALU.mult), reads=[psn, 'Hconst', tabn], writes=['tq1_%d' % tb])
                s.op('dve', lambda e: e.tensor_tensor(out=vw(t2), in0=pi, in1=ti, op=ALU.mult), reads=[psn, 'Hconst', tabn], writes=['tq2_%d' % tb])
                s.op('pool', lambda e: e.tensor_tensor(out=outr, in0=vw(t1), in1=vw(t2), op=(ALU.add if conj else ALU.subtract)),
                     reads=['tq1_%d' % tb, 'tq2_%d' % tb], writes=[outn])
                if conj:
                    s.op('dve', lambda e: e.tensor_tensor(out=vw(t3), in0=pi, in1=tr, op=ALU.mult), reads=[psn, 'Hconst', tabn], writes=['tq3_%d' % tb])
                    s.op('dve', lambda e: e.tensor_tensor(out=vw(t4), in0=pr, in1=ti, op=ALU.mult), reads=[psn, 'Hconst', tabn], writes=['tq4_%d' % tb])
                    s.op('pool', lambda e: e.tensor_tensor(out=outi, in0=vw(t3), in1=vw(t4), op=ALU.subtract),
                         reads=['tq3_%d' % tb, 'tq4_%d' % tb], writes=[outn])
                else:
                    s.op('dve', lambda e: e.tensor_tensor(out=vw(t3), in0=pr, in1=ti, op=ALU.mult), reads=[psn, 'Hconst', tabn], writes=['tq3_%d' % tb])
                    s.op('dve', lambda e: e.tensor_tensor(out=vw(t4), in0=pi, in1=tr, op=ALU.mult), reads=[psn, 'Hconst', tabn], writes=['tq4_%d' % tb])
                    s.op('pool', lambda e: e.tensor_tensor(out=outi, in0=vw(t3), in1=vw(t4), op=ALU.add),
                         reads=['tq3_%d' % tb, 'tq4_%d' % tb], writes=[outn])

            hi_ = 0
            for cg in (cgroups if cgroups is not None else range(W // CG)):
                c0 = cg * CG
                s.dma('sp', zA[0:KA], src.ap()[c0:c0 + CG, 0:KA * N2].rearrange('c (a b) -> a c b', b=N2),
                      reads=['zT'], writes=['zA'])
                if mode == 'data':
                    s.dma('sp', x0s[:], sc['x0sT'].ap()[c0:c0 + CG, :].rearrange('c (a b) -> a c b', b=N2), writes=['x0s'])
                for c4 in range(CG // 4):
                    for j in range(NC2):
                        ka = pbi[0] % 2
                        pbi[0] += 1
                        psA = [PB2[ka][:, 0, :], PB2[ka][:, 1, :]]
                        for ci in range(4):
                            c = c4 * 4 + ci
                            for ri in range(2):
                                s.op('pe', lambda e: e.matmul(psA[ri][:, ci * 128:(ci + 1) * 128], lhsT=zA[0:KA, c, j * 128:(j + 1) * 128],
                                                              rhs=FAb[0:KA, ri * 128:(ri + 1) * 128], start=(ci == 0), stop=True),
                                     reads=['zA', 'Hconst'], writes=['pb%d' % ka])
                        cmul(psA[0].rearrange('p (a b) -> p a b', a=4), psA[1].rearrange('p (a b) -> p a b', a=4),
                             TT4[:, j, 0], TT4[:, j, 1], ApT[:, j, 0, c4 * 4:(c4 + 1) * 4, :], ApT[:, j, 1, c4 * 4:(c4 + 1) * 4, :], False, 512,
                             psn='pb%d' % ka, outn='ApT%d' % c4)
                for c4 in range(CG // 4):
                    cs = slice(c4 * 4, (c4 + 1) * 4)
                    for kc in range(NC2):
                        kx = 2 + pbi[1] % 2
                        pbi[1] += 1
                        psX = [PB2[kx][:, 0, :], PB2[kx][:, 1, :]]
                        if mode == 'data':
                            xs = hi_ % 2
                            hi_ += 1
                            s.dma('sp', Ht[:, xs], hs['Hd'].ap()[kc, :, :, c0 + c4 * 4:c0 + c4 * 4 + 4, :].rearrange('r p c k -> p r c k'),
                                  reads=['Hd'], writes=['Htab%d' % xs])
                        for j in range(NC2):
                            ar = ApT[:, j, 0, cs, :].rearrange('p a b -> p (a b)')
                            ai = ApT[:, j, 1, cs, :].rearrange('p a b -> p (a b)')
                            s.op('pe', lambda e: e.matmul(psX[0], lhsT=CCb[:, j, kc, 0, :], rhs=ar, start=(j == 0), stop=False),
                                 reads=['ApT%d' % c4, 'Hconst'], writes=['pb%d' % kx])
                            s.op('pe', lambda e: e.matmul(psX[0], lhsT=CCb[:, j, kc, 2, :], rhs=ai, start=False, stop=(j == NC2 - 1)),
                                 reads=['ApT%d' % c4, 'Hconst'], writes=['pb%d' % kx])
                            s.op('pe', lambda e: e.matmul(psX[1], lhsT=CCb[:, j, kc, 1, :], rhs=ar, start=(j == 0), stop=False),
                                 reads=['ApT%d' % c4, 'Hconst'], writes=['pb%d' % kx])
                            s.op('pe', lambda e: e.matmul(psX[1], lhsT=CCb[:, j, kc, 0, :], rhs=ai, start=False, stop=(j == NC2 - 1)),
                                 reads=['ApT%d' % c4, 'Hconst'], writes=['pb%d' % kx])
                        if mode == 'filter':
                            xs = hi_ % 2
                            hi_ += 1
                            s.op('act', lambda e: e.copy(out=Xo[:, xs, 0].rearrange('p a b -> p (a b)'), in_=psX[0]),
                                 reads=['pb%d' % kx], writes=['Xo%d' % xs])
                            s.op('dve', lambda e: e.tensor_copy(out=Xo[:, xs, 1].rearrange('p a b -> p (a b)'), in_=psX[1]),
                                 reads=['pb%d' % kx], writes=['Xo%d' % xs])
                            s.dma('pool', hs['Hd'].ap()[kc, :, :, c0 + c4 * 4:c0 + c4 * 4 + 4, :].rearrange('r p c k -> p r c k'), Xo[:, xs],
                                  reads=['Xo%d' % xs], writes=['Hd'])
                        else:
                            cmul(psX[0].rearrange('p (a b) -> p a b', a=4), psX[1].rearrange('p (a b) -> p a b', a=4),
                                 Ht[:, xs, 0], Ht[:, xs, 1], Yt[:, kc, 0, cs, :], Yt[:, kc, 1, cs, :], False, 512, psn='pb%d' % kx, outn='Yt%d' % c4, tabn='Htab%d' % xs)
                if mode == 'filter':
                    continue
                for c2 in range(CG // 2):
                    kb_ = pbi[2] % 2
                    pbi[2] += 1
                    psB = PB2[kb_]
                    psY = PB2[2 + kb_][:, 0, :]
                    for ci in range(2):
                        c = c2 * 2 + ci
                        for kc in range(NC2):
                            s.op('pe', lambda e: e.matmul(psB[:, ci, 0:2 * N2], lhsT=Yt[:, kc, 0, c, :], rhs=CI1[:, kc, :],
                                                          start=(kc == 0), stop=False), reads=['Yt%d' % (c // 4), 'Hconst'], writes=['pb%d' % kb_])
                            s.op('pe', lambda e: e.matmul(psB[:, ci, 0:2 * N2], lhsT=Yt[:, kc, 1, c, :], rhs=CI2[:, kc, :],
                                                          start=False, stop=(kc == NC2 - 1)), reads=['Yt%d' % (c // 4), 'Hconst'], writes=['pb%d' % kb_])
                    bs_ = c2 % 2
                    cmul(psB[:, :, 0:N2], psB[:, :, N2:2 * N2], TTt[:, 0], TTt[:, 1], Bp[:, bs_, 0], Bp[:, bs_, 1], True, 2 * N2, psn='pb%d' % kb_, outn='Bp%d' % bs_)
                    s.op('pe', lambda e: e.matmul(psY[0:64, 0:2 * N2], lhsT=FIb[:, 0, :], rhs=Bp[:, bs_, 0].rearrange('p a b -> p (a b)'),
                                                  start=True, stop=False), reads=['Bp%d' % bs_, 'Hconst'], writes=['pb%d' % (2 + kb_)])
                    s.op('pe', lambda e: e.matmul(psY[0:64, 0:2 * N2], lhsT=FIb[:, 1, :], rhs=Bp[:, bs_, 1].rearrange('p a b -> p (a b)'),
                                                  start=False, stop=True), reads=['Bp%d' % bs_, 'Hconst'], writes=['pb%d' % (2 + kb_)])
                    s.op('dve', lambda e: e.tensor_tensor(out=ost[:, c2 * 2:c2 * 2 + 2, :],
                                                          in0=psY[0:64, 0:2 * N2].rearrange('p (a b) -> p a b', a=2),
                                                          in1=x0s[:, c2 * 2:c2 * 2 + 2, :], op=ALU.mult),
                         reads=['pb%d' % (2 + kb_), 'x0s'], writes=['Host'])
                s.dma('pool', sc['bsT'].ap()[2 * W + c0:2 * W + c0 + CG, :].rearrange('c (a b) -> a c b', b=N2), ost[:],
                      reads=['Host'], writes=['bsT'])
        s.barrier()

    B.pass_H = pass_H

    B.pass_W = pass_W
    B.locals = locals()
    return B


L_P, L_S = 16384, 8192


def two_pass(seqs, prog, taps=(), own=None):
    Bd = build(seqs, taps=taps, own=own)
    prog(Bd)
    Bd.s.finish()
    B = build(seqs, taps=taps, needed=Bd.s.needed, own=own)
    prog(B)
    B.s.finish()
    return B


def build_full():
    seqs = [('P', L_P), ('S', L_S)]
    own = {'P': (L_P // 8 // 128, [c * (L_P // 8 // 128) for c in range(8)]),
           'S': (L_S // 2 // 128, [(c % 2) * (L_S // 2 // 128) for c in range(8)])}
    return two_pass(seqs, lambda B: full_prog(B, seqs), own=own)


def full_prog(B, seqs):
    B.pass_init()
    B.pass_att_setup()
    for l in range(2):
        B.pass_W(l)
        B.pass_params(l)
        B.pass_lambda(l)
        for nm, L in seqs:
            last = (l == 1)
            B.pass_1(l, nm, L, blk_out=last)
            B.pass_2a(l, nm, L)
            B.pass_2b(l, nm, L)
            B.pass_2c(l, nm, L)
            B.pass_F(l, nm, L)
            B.pass_H(l, nm, L, 'filter')
            B.pass_H(l, nm, L, 'data')
            if last:
                B.pass_4o(l, nm, L)
            else:
                B.pass_4(l, nm, L)
            B.pass_5(l, nm, L, last, b_blk=last)


def kernel(**inputs):
    B = build_full()
    x_prompt = np.ascontiguousarray(np.asarray(inputs['x_prompt'], dtype=np.float32))
    x_sample = np.ascontiguousarray(np.asarray(inputs['x_sample'], dtype=np.float32))
    base = {}
    for name in B.inputs:
        if name in B.consts:
            base[name] = B.consts[name]
        elif name not in ('x_P', 'x_S'):
            base[name] = np.ascontiguousarray(np.asarray(inputs[name], dtype=np.float32))
    in_maps = []
    for c in range(8):
        m = dict(base)
        m['x_P'] = x_prompt[0]
        m['x_S'] = x_sample[c // 2]
        for name, arrs in B.core_consts.items():
            m[name] = arrs[c]
        in_maps.append(m)
    res = run_bass_kernel_spmd(B.nc, in_maps, core_ids=list(range(8)))
    y_prompt = np.empty((1, L_P, D), np.float32)
    y_sample = np.empty((4, L_S, D), np.float32)
    pc = L_P // 8
    for c in range(8):
        r = res.results[c]
        y_prompt[0, c * pc:(c + 1) * pc] = np.asarray(r['y_P'])[c * pc:(c + 1) * pc]
        hf = c % 2
        y_sample[c // 2, hf * (L_S // 2):(hf + 1) * (L_S // 2)] = np.asarray(r['y_S'])[hf * (L_S // 2):(hf + 1) * (L_S // 2)]
    return (y_prompt, y_sample)
```

```python
import math
from contextlib import ExitStack
import numpy as np
import concourse.bass as bass
import concourse.mybir as mybir
from concourse.bass_utils import run_bass_kernel_spmd

F32 = mybir.dt.float32
BF16 = mybir.dt.bfloat16
I32 = mybir.dt.int32
AF = mybir.ActivationFunctionType
ALU = mybir.AluOpType

D = 1024
W = 512
N_IN = 10752
NG4 = 21
EPS = 1e-6
G_POOL, G_Q, G_K, G_V, G_X0, G_X1, G_VH, G_CFA, G_CFG, G_SILU, G_MERGE = 0, 1, 2, 3, 4, 5, 6, 7, 8, 9, 13
CF_WIDTH = 31
HY_FAST, HY_SLOW, HY_DECAY_TARGET = 0.3, 1.5, 1e-2


class Sched:
    def __init__(self, nc, n_dma_sems=8, needed=None):
        self.nc = nc
        self.dry = needed is None
        self.needed = set() if needed is None else needed
        self.engs = {'pe': nc.tensor, 'act': nc.scalar, 'dve': nc.vector, 'pool': nc.gpsimd, 'sp': nc.sync}
        self.sems, self.vals, self.real, self.vmap = {}, {}, {}, {}
        for k in ('pe', 'act', 'dve', 'pool'):
            self.sems['c_' + k] = nc.alloc_semaphore('c_' + k)
            self.vals['c_' + k] = 0
            self.real['c_' + k] = 0
            self.vmap['c_' + k] = {}
        self.dq = {}
        for q in ('sp', 'pool', 'act'):
            self.dq[q] = []
            for i in range(n_dma_sems):
                sk = 'd_%s%d' % (q, i)
                self.sems[sk] = nc.alloc_semaphore(sk)
                self.vals[sk] = 0
                self.dq[q].append(sk)
        self.dqi = {q: 0 for q in self.dq}
        self.seen = {e: {} for e in self.engs}
        self.res = {}
        self.n_ins = 0

    def _deps(self, reads, writes):
        deps = {}

        def add(sk, v):
            if v > deps.get(sk, 0):
                deps[sk] = v
        for r in reads:
            st = self.res.get(r)
            if st and st['w']:
                add(*st['w'])
        for w in writes:
            st = self.res.get(w)
            if st:
                if st['w']:
                    add(*st['w'])
                for sk, v in st['r'].items():
                    add(sk, v)
        return deps

    def _wait(self, ek, deps):
        eng = self.engs[ek]
        seen = self.seen[ek]
        for sk, v in deps.items():
            if ek == 'pe' and sk == 'c_pe':
                continue
            if seen.get(sk, 0) < v:
                seen[sk] = v
                self.n_ins += 1
                if sk.startswith('c_'):
                    if self.dry:
                        self.needed.add((sk, v))
                    else:
                        eng.wait_ge(self.sems[sk], self.vmap[sk][v])
                elif not self.dry:
                    eng.wait_ge(self.sems[sk], v)

    def _update(self, sk, v, reads, writes):
        for r in reads:
            st = self.res.setdefault(r, {'w': None, 'r': {}})
            st['r'][sk] = v
        for w in writes:
            self.res[w] = {'w': (sk, v), 'r': {}}

    def op(self, ek, fn, reads=(), writes=()):
        self._wait(ek, self._deps(reads, writes))
        sk = 'c_' + ek
        self.vals[sk] += 1
        v = self.vals[sk]
        if not self.dry:
            ins = fn(self.engs[ek])
            if (sk, v) in self.needed:
                self.real[sk] += 1
                self.vmap[sk][v] = self.real[sk]
                ins.then_inc(self.sems[sk], 1)
        self.n_ins += 1
        self._update(sk, v, reads, writes)

    def dma(self, q, out, in_, reads=(), writes=(), fn=None):
        self._wait(q, self._deps(reads, writes))
        sk = self.dq[q][self.dqi[q] % len(self.dq[q])]
        self.dqi[q] += 1
        self.vals[sk] += 16
        if not self.dry:
            eng = self.engs[q]
            if fn is not None:
                ins = fn(eng)
            else:
                ins = eng.dma_start(out=out, in_=in_)
            ins.then_inc(self.sems[sk], 16)
        self.n_ins += 1
        self._update(sk, self.vals[sk], reads, writes)

    def barrier(self):
        for ek in self.engs:
            self._wait(ek, {sk: v for sk, v in self.vals.items() if v > 0})
        self.res = {}

    def finish(self):
        self._wait('sp', {sk: v for sk, v in self.vals.items() if v > 0})


def _bucket_table(lo, hi):
    import jax
    import jax.numpy as jnp
    with jax.default_device(jax.devices('cpu')[0]):
        rel = jnp.arange(lo, hi + 1)
        nb, max_exact = 16, 8
        ret = jnp.where(rel > 0, nb, 0)
        n = jnp.abs(rel)
        large = max_exact + (jnp.log(jnp.maximum(n, 1).astype(jnp.float32) / max_exact)
                             / math.log(128 / max_exact) * (nb - max_exact)).astype(jnp.int32)
        large = jnp.minimum(large, nb - 1)
        return np.asarray(ret + jnp.where(n < max_exact, n, large)).astype(np.int64)


class Builder:
    def __init__(self, seqs, n_layers=2, taps=(), stop_after=None, needed=None):
        self.seqs = seqs
        self.n_layers = n_layers
        self.taps = set(taps)
        self.stop_after = stop_after
        self.nc = bass.Bass("TRN2", target_bir_lowering=False)
        self.s = Sched(self.nc, needed=needed)
        self.inputs = {}
        self.consts = {}
        self.core_consts = {}

    def inp(self, name, shape, dtype=F32):
        t = self.nc.dram_tensor(name, list(shape), dtype, kind="ExternalInput")
        self.inputs[name] = t
        return t

    def scratch(self, name, shape, dtype):
        kind = "ExternalOutput" if name in self.taps else "Internal"
        return self.nc.dram_tensor(name, list(shape), dtype, kind=kind)

    def core_const_inp(self, name, arrs):
        arrs = [np.ascontiguousarray(a) for a in arrs]
        dt = {np.dtype('float32'): F32, np.dtype('int32'): I32}[arrs[0].dtype]
        t = self.nc.dram_tensor(name, list(arrs[0].shape), dt, kind="ExternalInput")
        self.core_consts[name] = arrs
        self.consts[name] = arrs[0]
        self.inputs[name] = t
        return t

    def const_inp(self, name, arr):
        arr = np.ascontiguousarray(arr)
        dt = {np.dtype('float32'): F32, np.dtype('int32'): I32}[arr.dtype]
        t = self.nc.dram_tensor(name, list(arr.shape), dt, kind="ExternalInput")
        self.consts[name] = arr
        self.inputs[name] = t
        return t


def build(seqs, n_layers=2, taps=(), stop_after=None, needed=None, own=None):
    B = Builder(seqs, n_layers, taps, stop_after, needed)
    B.own = own
    nc, s = B.nc, B.s
    DEPTH = 2
    uid = [0]

    def SBT(name, shape, dt):
        uid[0] += 1
        return nc.sbuf_tensor('%s_%d' % (name, uid[0]), shape, dt)

    def PST(name, shape, dt):
        uid[0] += 1
        return nc.psum_tensor('%s_%d' % (name, uid[0]), shape, dt)
    x_in = {nm: B.inp('x_' + nm, [L, D]) for nm, L in seqs}
    norm_g = B.inp('norm_g', [DEPTH, D])
    w_in = B.inp('w_in', [DEPTH, D, N_IN])
    pool_w = B.inp('pool_w', [DEPTH, 4, 128, 128])
    pool_scale = B.inp('pool_scale', [DEPTH, W])
    lam_in = {k: B.inp(k, [DEPTH, 64]) for k in ('lambda_q1', 'lambda_k1', 'lambda_q2', 'lambda_k2')}
    subln_g = B.inp('subln_g', [DEPTH, 128])
    rel_bias = B.inp('rel_bias', [32, 4])
    hy_short_w = B.inp('hy_short_w', [DEPTH, 3, 3 * W])
    hy_short_b = B.inp('hy_short_b', [DEPTH, 3 * W])
    hy_w1 = B.inp('hy_w1', [DEPTH, 33, 64])
    hy_b1 = B.inp('hy_b1', [DEPTH, 64])
    hy_freq = B.inp('hy_freq', [DEPTH, 2, 64])
    hy_w2 = B.inp('hy_w2', [DEPTH, 64, 64])
    hy_b2 = B.inp('hy_b2', [DEPTH, 64])
    hy_w3 = B.inp('hy_w3', [DEPTH, 64, 2 * W])
    hy_d = B.inp('hy_d', [DEPTH, W])
    cf_dw_w = B.inp('cf_dw_w', [DEPTH, CF_WIDTH, W])
    cf_dw_b = B.inp('cf_dw_b', [DEPTH, W])
    cf_ln_g = B.inp('cf_ln_g', [DEPTH, W])
    cf_ln_b = B.inp('cf_ln_b', [DEPTH, W])
    w_branch = B.inp('w_branch', [DEPTH, 4, W, D])
    w_out = B.inp('w_out', [DEPTH, D, D])
    final_g = B.inp('final_g', [D])
    y_out = {nm: nc.dram_tensor('y_' + nm, [L, D], F32, kind="ExternalOutput") for nm, L in seqs}

    wbf = B.scratch('wbf', [NG4, 128, 8, 512], BF16)
    wbr_bf = B.scratch('wbr_bf', [128, 16, D], BF16)
    wout_bf = B.scratch('wout_bf', [128, 8, D], BF16)
    SC = {}
    for nm, L in seqs:
        SC[nm] = dict(
            xres=B.scratch('xres_' + nm, [L, D], F32),
            uP=B.scratch('uP_' + nm, [W, L], BF16),
            Qt=B.scratch('Qt_' + nm, [W, L], BF16),
            Kt=B.scratch('Kt_' + nm, [W, L], BF16),
            V=B.scratch('V_' + nm, [L, W], BF16),
            uH=B.scratch('uH_' + nm, [3 * W, L], BF16),
            hcf=B.scratch('hcf_' + nm, [W, L], BF16),
            sgT=B.scratch('sgT_' + nm, [4 * W, L], BF16),
            mgT=B.scratch('mgT_' + nm, [4 * D, L], BF16),
            zT=B.scratch('zT_' + nm, [W, L], BF16),
            x0sT=B.scratch('x0sT_' + nm, [W, L], BF16),
            bsT=B.scratch('bsT_' + nm, [4 * W, L], BF16),
            QtB=B.scratch('QtB_' + nm, [4 * (L // 128), 128, 128], BF16),
            KtB=B.scratch('KtB_' + nm, [4 * (L // 128 + 1), 128, 128], BF16),
            sgB=B.scratch('sgB_' + nm, [4 * (L // 128), 128, 128], BF16),
            bsB=B.scratch('bsB_' + nm, [4 * (L // 128), 128, 128], BF16),
            V2=B.scratch('V2_' + nm, [4, L + 128, 128], BF16),
        )

    ident_np = np.eye(128, dtype=np.float32)
    ident_d = B.const_inp('c_ident', ident_np)

    rr = [0]

    def cast_eng():
        rr[0] += 1
        return ('dve', 'act', 'pool')[rr[0] % 3]

    def copy_on(ek, out, in_):
        if ek == 'act':
            return lambda e: e.copy(out=out, in_=in_)
        return lambda e: e.tensor_copy(out=out, in_=in_)

    def pass_W(l):
        with ExitStack() as es:
            wf = es.enter_context(SBT('wW_f', [128, 2, 8, 512], F32))
            wb = es.enter_context(SBT('wW_b', [128, 2, 8, 512], BF16))
            jobs = []
            for g in range(NG4):
                src = w_in.ap()[l, :, g * 512:(g + 1) * 512].rearrange('(dc p) c -> p dc c', p=128)
                jobs.append((src, wbf.ap()[g], 8, 512))
            for n in range(4):
                for hf in range(2):
                    src = w_branch.ap()[l, n, :, hf * 512:(hf + 1) * 512].rearrange('(wc p) c -> p wc c', p=128)
                    jobs.append((src, wbr_bf.ap()[:, n * 4:(n + 1) * 4, hf * 512:(hf + 1) * 512], 4, 512))
            for hf in range(2):
                src = w_out.ap()[l, :, hf * 512:(hf + 1) * 512].rearrange('(dc p) c -> p dc c', p=128)
                jobs.append((src, wout_bf.ap()[:, :, hf * 512:(hf + 1) * 512], 8, 512))
            for i, (src, dst, a, c) in enumerate(jobs):
                sl = i % 2
                s.dma('sp', wf[:, sl, 0:a, 0:c], src, reads=[], writes=['wWf%d' % sl])
                ek = cast_eng()
                s.op(ek, copy_on(ek, wb[:, sl, 0:a, 0:c], wf[:, sl, 0:a, 0:c]),
                     reads=['wWf%d' % sl], writes=['wWb%d' % sl])
                s.dma('pool', dst, wb[:, sl, 0:a, 0:c], reads=['wWb%d' % sl], writes=['wbf'])
        s.barrier()

    def pass_1(l, nm, L, blk_out=False):
        sc = SC[nm]
        NBLK = L // 128
        SGT = min(2048, L)
        NSG, TPS, GPS = L // SGT, SGT // 128, SGT // 512
        x_src = x_in[nm] if l == 0 else sc['xres']
        with ExitStack() as es:
            xt = es.enter_context(SBT('p1_xt', [128, 2, D], F32))
            junk = es.enter_context(SBT('p1_junk', [128, D], BF16))
            hn = es.enter_context(SBT('p1_hn', [128, 2, D], BF16))
            gt = es.enter_context(SBT('p1_gt', [128, D], F32))
            ss = es.enter_context(SBT('p1_ss', [128, 4], F32))
            idf = es.enter_context(SBT('p1_idf', [128, 128], F32))
            idb = es.enter_context(SBT('p1_idb', [128, 128], BF16))
            hnT = es.enter_context(SBT('p1_hnT', [128, 8, SGT], BF16))
            wt = es.enter_context(SBT('p1_wt', [128, 2, 8, 512], BF16))
            wa = es.enter_context(SBT('p1_wa', [128, 8, 512], BF16))
            stage = es.enter_context(SBT('p1_stage', [128, 2, SGT], BF16))
            vst = es.enter_context(SBT('p1_vst', [128, 2, 512], BF16))
            sig = es.enter_context(SBT('p1_sig', [128, 2, 512], F32))
            pT0 = es.enter_context(PST('p1_pT0', [128, D], BF16))
            pT1 = es.enter_context(PST('p1_pT1', [128, D], BF16))
            pm0 = es.enter_context(PST('p1_pm0', [128, 512], F32))
            pm1 = es.enter_context(PST('p1_pm1', [128, 512], F32))
            pm2 = es.enter_context(PST('p1_pm2', [128, 512], F32))
            pm3 = es.enter_context(PST('p1_pm3', [128, 512], F32))
            pT = [pT0, pT1]
            pm = [pm0, pm1, pm2, pm3]
            s.dma('sp', gt[:], norm_g.ap()[l:l + 1, :].partition_broadcast(128).rearrange('p a d -> p (a d)'), writes=['gt'])
            s.dma('sp', idf[:], ident_d.ap(), writes=['idf'])
            s.op('dve', lambda e: e.tensor_copy(out=idb[:], in_=idf[:]), reads=['idf'], writes=['idb'])
            kk = [0]
            wl = [0]
            if blk_out:
                s.op('pool', lambda e: e.memset(vst[:], 0.0), writes=['vst0', 'vst1'])
                for h_ in range(4):
                    s.dma('pool', sc['KtB'].ap()[h_ * (NBLK + 1) + NBLK], vst[:, 0, 0:128], reads=['vst0'], writes=['p1out'])
                s.dma('pool', sc['V2'].ap()[:, L:L + 128, :].rearrange('h p e -> p h e'),
                      vst[:, 1].rearrange('p (h e) -> p h e', h=4), reads=['vst1'], writes=['p1out'])
            for sg in range(NSG):
                t0 = sg * SGT
                for t in range(TPS):
                    sl = t % 2
                    r0 = t0 + t * 128
                    s.dma('sp', xt[:, sl], x_src.ap()[r0:r0 + 128, :], writes=['xt%d' % sl])
                    s.op('act', lambda e: e.activation(out=junk[:], in_=xt[:, sl], func=AF.Square,
                                                       accum_out=ss[:, sl:sl + 1]),
                         reads=['xt%d' % sl], writes=['junk', 'ss%d' % sl])
                    s.op('dve', lambda e: e.tensor_scalar(out=ss[:, 2 + sl:3 + sl], in0=ss[:, sl:sl + 1], scalar1=1.0 / D,
                                                          scalar2=EPS, op0=ALU.mult, op1=ALU.add),
                         reads=['ss%d' % sl], writes=['rs%d' % sl])
                    s.op('act', lambda e: e.activation(out=ss[:, 2 + sl:3 + sl], in_=ss[:, 2 + sl:3 + sl], func=AF.Sqrt),
                         reads=['rs%d' % sl], writes=['rs%d' % sl])
                    s.op('dve', lambda e: e.reciprocal(out=ss[:, 2 + sl:3 + sl], in_=ss[:, 2 + sl:3 + sl]),
                         reads=['rs%d' % sl], writes=['rs%d' % sl])
                    s.op('dve', lambda e: e.scalar_tensor_tensor(out=hn[:, sl], in0=xt[:, sl], scalar=ss[:, 2 + sl:3 + sl],
                                                                 in1=gt[:], op0=ALU.mult, op1=ALU.mult),
                         reads=['xt%d' % sl, 'rs%d' % sl, 'gt'], writes=['hn%d' % sl])
                    for dc in range(8):
                        s.op('pe', lambda e: e.transpose(out=pT[sl][:, dc * 128:(dc + 1) * 128],
                                                         in_=hn[:, sl, dc * 128:(dc + 1) * 128], identity=idb[:]),
                             reads=['hn%d' % sl, 'idb'], writes=['pT%d' % sl])
                    ek = 'act' if t % 2 else 'dve'
                    s.op(ek, copy_on(ek, hnT[:, :, t * 128:(t + 1) * 128], pT[sl][:].rearrange('p (dc t) -> p dc t', dc=8)),
                         reads=['pT%d' % sl], writes=['hnT_%d' % t])
                for g4 in range(NG4):
                    if g4 == G_CFA:
                        s.dma('sp', wa[:], wbf.ap()[g4], writes=['wa'])
                        continue
                    wsl = wl[0] % 2
                    wl[0] += 1
                    s.dma('sp', wt[:, wsl], wbf.ap()[g4], writes=['wt%d' % wsl])
                    if g4 == G_V:
                        for t in range(TPS):
                            k = kk[0] % 4
                            kk[0] += 1
                            for dc in range(8):
                                s.op('pe', lambda e: e.matmul(pm[k][:], lhsT=hnT[:, dc, t * 128:(t + 1) * 128],
                                                              rhs=wt[:, wsl, dc, :], start=(dc == 0), stop=(dc == 7)),
                                     reads=['hnT_%d' % t, 'wt%d' % wsl], writes=['pm%d' % k])
                            vs = t % 2
                            ek = 'act' if t % 2 else 'dve'
                            s.op(ek, copy_on(ek, vst[:, vs], pm[k][:]), reads=['pm%d' % k], writes=['vst%d' % vs])
                            r0 = t0 + t * 128
                            if blk_out:
                                s.dma('pool', sc['V2'].ap()[:, r0:r0 + 128, :].rearrange('h p e -> p h e'),
                                      vst[:, vs].rearrange('p (h e) -> p h e', h=4), reads=['vst%d' % vs], writes=['V'])
                            else:
                                s.dma('pool', sc['V'].ap()[r0:r0 + 128, :], vst[:, vs], reads=['vst%d' % vs], writes=['V'])
                        continue
                    for j in range(4):
                        stsl = (g4 * 4 + j) % 2
                        for grp in range(GPS):
                            rd = ['hnT_%d' % (grp * 4 + i) for i in range(4)]
                            k = kk[0] % 4
                            kk[0] += 1
                            for dc in range(8):
                                s.op('pe', lambda e: e.matmul(pm[k][:], lhsT=wt[:, wsl, dc, j * 128:(j + 1) * 128],
                                                              rhs=hnT[:, dc, grp * 512:(grp + 1) * 512],
                                                              start=(dc == 0), stop=(dc == 7)),
                                     reads=rd + ['wt%d' % wsl], writes=['pm%d' % k])
                            dst = stage[:, stsl, grp * 512:(grp + 1) * 512]
                            if g4 == G_CFG:
                                k2 = kk[0] % 4
                                kk[0] += 1
                                for dc in range(8):
                                    s.op('pe', lambda e: e.matmul(pm[k2][:], lhsT=wa[:, dc, j * 128:(j + 1) * 128],
                                                                  rhs=hnT[:, dc, grp * 512:(grp + 1) * 512],
                                                                  start=(dc == 0), stop=(dc == 7)),
                                         reads=rd + ['wa'], writes=['pm%d' % k2])
                                sgs = grp % 2
                                s.op('act', lambda e: e.activation(out=sig[:, sgs], in_=pm[k][:], func=AF.Sigmoid),
                                     reads=['pm%d' % k], writes=['sig%d' % sgs])
                                s.op('dve', lambda e: e.tensor_tensor(out=dst, in0=pm[k2][:], in1=sig[:, sgs], op=ALU.mult),
                                     reads=['pm%d' % k2, 'sig%d' % sgs], writes=['stage%d' % stsl])
                            elif g4 >= G_MERGE:
                                s.op('act', lambda e: e.activation(out=dst, in_=pm[k][:], func=AF.Sigmoid),
                                     reads=['pm%d' % k], writes=['stage%d' % stsl])
                            elif g4 >= G_SILU:
                                s.op('act', lambda e: e.activation(out=dst, in_=pm[k][:], func=AF.Silu),
                                     reads=['pm%d' % k], writes=['stage%d' % stsl])
                            else:
                                ek = 'act' if grp % 2 else 'dve'
                                s.op(ek, copy_on(ek, dst, pm[k][:]), reads=['pm%d' % k], writes=['stage%d' % stsl])
                        if g4 == G_POOL:
                            d_ap = sc['uP'].ap()[j * 128:(j + 1) * 128, t0:t0 + SGT]
                        elif g4 == G_Q:
                            d_ap = sc['Qt'].ap()[j * 128:(j + 1) * 128, t0:t0 + SGT]
                        elif g4 == G_K:
                            d_ap = sc['Kt'].ap()[j * 128:(j + 1) * 128, t0:t0 + SGT]
                        elif g4 in (G_X0, G_X1, G_VH):
                            r = (g4 - G_X0) * 512 + j * 128
                            d_ap = sc['uH'].ap()[r:r + 128, t0:t0 + SGT]
                        elif g4 == G_CFG:
                            d_ap = sc['hcf'].ap()[j * 128:(j + 1) * 128, t0:t0 + SGT]
                        elif g4 >= G_MERGE:
                            r = (g4 - G_MERGE) * 512 + j * 128
                            d_ap = sc['mgT'].ap()[r:r + 128, t0:t0 + SGT]
                        else:
                            r = (g4 - G_SILU) * 512 + j * 128
                            d_ap = sc['sgT'].ap()[r:r + 128, t0:t0 + SGT]
                        s.dma('pool', d_ap, stage[:, stsl], reads=['stage%d' % stsl], writes=['p1out'])
                        if blk_out and g4 in (G_Q, G_K, G_SILU + 1):
                            nb_ = SGT // 128
                            b0 = t0 // 128
                            if g4 == G_Q:
                                bt, base = sc['QtB'], j * NBLK
                            elif g4 == G_K:
                                bt, base = sc['KtB'], j * (NBLK + 1)
                            else:
                                bt, base = sc['sgB'], j * NBLK
                            s.dma('pool', bt.ap()[base + b0:base + b0 + nb_].rearrange('b p k -> p b k'),
                                  stage[:, stsl].rearrange('p (b k) -> p b k', k=128), reads=['stage%d' % stsl], writes=['p1out'])
        s.barrier()

    B.pass_1 = pass_1
    ES = ExitStack()
    B.ES = ES
    idf_g = ES.enter_context(SBT('g_idf', [128, 128], F32))
    idb_g = ES.enter_context(SBT('g_idb', [128, 128], BF16))
    ones_g = ES.enter_context(SBT('g_ones', [128, 128], F32))
    PRM = {}
    for nm_, nb_, r_ in (('hsw', 12, 3), ('hsb', 12, 1), ('cdw', 4, 31), ('cdb', 4, 1), ('clg', 4, 1), ('clb', 4, 1),
                         ('psc', 4, 1), ('hyd', 4, 1)):
        PRM[nm_] = ES.enter_context(SBT('prm_' + nm_, [128, nb_, r_], F32))

    def pass_init():
        s.dma('sp', idf_g[:], ident_d.ap(), writes=['idf_g'])
        s.op('dve', lambda e: e.tensor_copy(out=idb_g[:], in_=idf_g[:]), reads=['idf_g'], writes=['idb_g'])
        s.op('dve', lambda e: e.memset(ones_g[:], 1.0), writes=['ones_g'])
        s.barrier()

    def pass_params(l):
        srcs = dict(hsw=hy_short_w.ap()[l], hsb=hy_short_b.ap()[l:l + 1, :], cdw=cf_dw_w.ap()[l],
                    cdb=cf_dw_b.ap()[l:l + 1, :], clg=cf_ln_g.ap()[l:l + 1, :], clb=cf_ln_b.ap()[l:l + 1, :],
                    psc=pool_scale.ap()[l:l + 1, :], hyd=hy_d.ap()[l:l + 1, :])
        with ExitStack() as es:
            stg = es.enter_context(SBT('pp_stg', [32, 1536], F32))
            pp = es.enter_context(PST('pp_ps', [128, 512], F32))
            for nm_, src in srcs.items():
                R, C = src.shape
                s.dma('sp', stg[0:R, 0:C], src, writes=['pp_stg'])
                for b in range(C // 128):
                    s.op('pe', lambda e: e.transpose(out=pp[:, 0:R], in_=stg[0:R, b * 128:(b + 1) * 128],
                                                     identity=idf_g[0:R, 0:R]),
                         reads=['pp_stg'], writes=['pp_ps'])
                    s.op('dve', lambda e: e.tensor_copy(out=PRM[nm_][:, b, :], in_=pp[:, 0:R]),
                         reads=['pp_ps'], writes=['prm'])
        s.barrier()

    B.pass_init = pass_init
    B.pass_params = pass_params

    invcnt_d = {}
    for nm, L in seqs:
        pos = np.arange(L)
        tab = np.zeros((4, L), np.float32)
        for g, w in enumerate((2, 4, 8, 16)):
            lo = np.clip(pos - w // 2, 0, L - 1)
            hi = np.clip(pos + (w - 1 - w // 2), 0, L - 1)
            tab[g] = 1.0 / (hi - lo + 1)
        invcnt_d[nm] = B.const_inp('c_invcnt_' + nm, tab)

    def pass_2a(l, nm, L):
        sc = SC[nm]
        TC = min(2048, L)
        n = TC + 16
        with ExitStack() as es:
            uex = es.enter_context(SBT('a_uex', [128, 2, n], BF16))
            Ab = es.enter_context(SBT('a_A', [128, 2, n], F32))
            invc = es.enter_context(SBT('a_invc', [128, 2, TC], F32))
            tmp = es.enter_context(SBT('a_tmp', [128, TC], F32))
            pooled = es.enter_context(SBT('a_pooled', [128, 2, TC], BF16))
            sgt = es.enter_context(SBT('a_sgt', [128, 2, TC], BF16))
            ost = es.enter_context(SBT('a_ost', [128, 2, TC], BF16))
            wmf = es.enter_context(SBT('a_wmf', [128, 4, 128], F32))
            wmb = es.enter_context(SBT('a_wmb', [128, 4, 128], BF16))
            ps = [es.enter_context(PST('a_ps%d' % i, [128, 512], F32)) for i in range(2)]
            s.dma('sp', wmf[:], pool_w.ap()[l].rearrange('g c d -> c g d'), writes=['wmf'])
            s.op('dve', lambda e: e.tensor_copy(out=wmb[:], in_=wmf[:]), reads=['wmf'], writes=['wmb'])
            it = 0
            for g in range(4):
                for c in range(L // TC):
                    sl = it % 2
                    it += 1
                    t0 = c * TC
                    lo, hi = max(0, t0 - 8), min(L, t0 + TC + 8)
                    e0 = lo - (t0 - 8)
                    if e0 > 0:
                        s.op('pool', lambda e: e.memset(uex[:, sl, 0:e0], 0.0), writes=['uex%d' % sl])
                    if hi - (t0 - 8) < n:
                        s.op('pool', lambda e: e.memset(uex[:, sl, hi - (t0 - 8):n], 0.0), writes=['uex%d' % sl])
                    s.dma('sp', uex[:, sl, e0:e0 + hi - lo], sc['uP'].ap()[g * 128:(g + 1) * 128, lo:hi],
                          writes=['uex%d' % sl])
                    s.dma('sp', invc[:, sl], invcnt_d[nm].ap()[g:g + 1, t0:t0 + TC].partition_broadcast(128)
                          .rearrange('p a t -> p (a t)'), writes=['invc%d' % sl])
                    s.dma('sp', sgt[:, sl], sc['sgT'].ap()[g * 128:(g + 1) * 128, t0:t0 + TC], writes=['sgt%d' % sl])
                    u = uex[:, sl]
                    s.op('dve', lambda e: e.tensor_tensor(out=Ab[:, 0, 1:n], in0=u[:, 1:n], in1=u[:, 0:n - 1], op=ALU.add),
                         reads=['uex%d' % sl], writes=['A0'])
                    cur = 0
                    for st in range(1, g + 1):
                        sh = 1 << st
                        a0 = 2 * sh - 1
                        nxt = 1 - cur
                        s.op('dve', lambda e: e.tensor_tensor(out=Ab[:, nxt, a0:n], in0=Ab[:, cur, a0:n],
                                                              in1=Ab[:, cur, a0 - sh:n - sh], op=ALU.add),
                             reads=['A%d' % cur], writes=['A%d' % nxt])
                        cur = nxt
                    off = 8 + (1 << g) - 1
                    s.op('dve', lambda e: e.tensor_tensor(out=tmp[:], in0=Ab[:, cur, off:off + TC], in1=invc[:, sl], op=ALU.mult),
                         reads=['A%d' % cur, 'invc%d' % sl], writes=['a_tmp'])
                    s.op('dve', lambda e: e.tensor_tensor(out=pooled[:, sl], in0=tmp[:], in1=u[:, 8:8 + TC], op=ALU.subtract),
                         reads=['a_tmp', 'uex%d' % sl], writes=['pooled%d' % sl])
                    for q in range(TC // 512):
                        k = q % 2
                        s.op('pe', lambda e: e.matmul(ps[k][:], lhsT=wmb[:, g, :], rhs=pooled[:, sl, q * 512:(q + 1) * 512],
                                                      start=True, stop=True),
                             reads=['wmb', 'pooled%d' % sl], writes=['a_ps%d' % k])
                        s.op('dve', lambda e: e.scalar_tensor_tensor(out=ost[:, sl, q * 512:(q + 1) * 512], in0=ps[k][:],
                                                                     scalar=PRM['psc'][:, g, 0:1],
                                                                     in1=sgt[:, sl, q * 512:(q + 1) * 512],
                                                                     op0=ALU.mult, op1=ALU.mult),
                             reads=['a_ps%d' % k, 'sgt%d' % sl], writes=['ost%d' % sl])
                    s.dma('pool', sc['bsT'].ap()[g * 128:(g + 1) * 128, t0:t0 + TC], ost[:, sl],
                          reads=['ost%d' % sl], writes=['bsT'])
        s.barrier()

    B.pass_2a = pass_2a

    def pass_2c(l, nm, L):
        sc = SC[nm]
        TC = min(2048, L)
        n = TC + 2
        with ExitStack() as es:
            uex = es.enter_context(SBT('c_uex', [128, 2, 3, n], BF16))
            Dh = es.enter_context(SBT('c_Dh', [128, 9, 128], BF16))
            xc = es.enter_context(SBT('c_xc', [128, 3, 512], F32))
            sgt = es.enter_context(SBT('c_sgt', [128, 2, TC], BF16))
            zst = es.enter_context(SBT('c_zst', [128, 2, TC], BF16))
            xst = es.enter_context(SBT('c_xst', [128, 2, TC], BF16))
            ps = [es.enter_context(PST('c_ps%d' % i, [128, 512], F32)) for i in range(3)]
            it = 0
            for cb in range(4):
                for sj in range(9):
                    s_, j = sj // 3, sj % 3
                    s.op('dve', lambda e: e.tensor_scalar(out=Dh[:, sj, :], in0=idf_g[:], scalar1=PRM['hsw'][:, s_ * 4 + cb, j:j + 1],
                                                          scalar2=None, op0=ALU.mult),
                         reads=['idf_g'], writes=['Dh'])
                for c in range(L // TC):
                    sl = it % 2
                    it += 1
                    t0 = c * TC
                    lo, hi = max(0, t0 - 1), min(L, t0 + TC + 1)
                    e0 = lo - (t0 - 1)
                    if e0 > 0:
                        s.op('pool', lambda e: e.memset(uex[:, sl, :, 0:e0], 0.0), writes=['cuex%d' % sl])
                    if hi - (t0 - 1) < n:
                        s.op('pool', lambda e: e.memset(uex[:, sl, :, hi - (t0 - 1):n], 0.0), writes=['cuex%d' % sl])
                    s.dma('sp', uex[:, sl, :, e0:e0 + hi - lo],
                          sc['uH'].ap().rearrange('(s q) t -> q s t', s=3)[cb * 128:(cb + 1) * 128, :, lo:hi],
                          writes=['cuex%d' % sl])
                    s.dma('sp', sgt[:, sl], sc['sgT'].ap()[2 * W + cb * 128:2 * W + (cb + 1) * 128, t0:t0 + TC],
                          writes=['csgt%d' % sl])
                    for q in range(TC // 512):
                        for s_ in range(3):
                            for j in range(3):
                                s.op('pe', lambda e: e.matmul(ps[s_][:], lhsT=Dh[:, s_ * 3 + j, :],
                                                              rhs=uex[:, sl, s_, q * 512 + j:q * 512 + j + 512],
                                                              start=(j == 0), stop=(j == 2)),
                                     reads=['Dh', 'cuex%d' % sl], writes=['c_ps%d' % s_])
                            s.op('act', lambda e: e.activation(out=xc[:, s_], in_=ps[s_][:], func=AF.Identity,
                                                               bias=PRM['hsb'][:, s_ * 4 + cb, 0:1]),
                                 reads=['c_ps%d' % s_], writes=['xc%d' % s_])
                        s.op('dve', lambda e: e.tensor_tensor(out=zst[:, sl, q * 512:(q + 1) * 512], in0=xc[:, 1], in1=xc[:, 2],
                                                              op=ALU.mult),
                             reads=['xc1', 'xc2'], writes=['zst%d' % sl])
                        s.op('dve', lambda e: e.tensor_tensor(out=xst[:, sl, q * 512:(q + 1) * 512], in0=xc[:, 0],
                                                              in1=sgt[:, sl, q * 512:(q + 1) * 512], op=ALU.mult),
                             reads=['xc0', 'csgt%d' % sl], writes=['xst%d' % sl])
                    s.dma('pool', sc['zT'].ap()[cb * 128:(cb + 1) * 128, t0:t0 + TC], zst[:, sl],
                          reads=['zst%d' % sl], writes=['zT'])
                    s.dma('pool', sc['x0sT'].ap()[cb * 128:(cb + 1) * 128, t0:t0 + TC], xst[:, sl],
                          reads=['xst%d' % sl], writes=['x0sT'])
        s.barrier()

    B.pass_2c = pass_2c

    def pass_2b(l, nm, L):
        sc = SC[nm]
        n = 512 + 30
        with ExitStack() as es:
            hex_ = es.enter_context(SBT('b_hex', [128, 2, 4, n], BF16))
            Dc = es.enter_context(SBT('b_Dc', [128, 4 * CF_WIDTH, 128], BF16))
            onesN = es.enter_context(SBT('b_onesN', [128, 128], F32))
            hc = es.enter_context(SBT('b_hc', [128, 4, 512], F32))
            sq = es.enter_context(SBT('b_sq', [128, 4, 512], F32))
            mean = es.enter_context(SBT('b_mean', [128, 512], F32))
            rstd = es.enter_context(SBT('b_rstd', [128, 512], F32))
            xn = es.enter_context(SBT('b_xn', [128, 2, 512], F32))
            sgt = es.enter_context(SBT('b_sgt', [128, 2, 4, 512], BF16))
            ost = es.enter_context(SBT('b_ost', [128, 2, 4, 512], BF16))
            ps = [es.enter_context(PST('b_ps%d' % i, [128, 512], F32)) for i in range(4)]
            psm = es.enter_context(PST('b_psm', [128, 512], F32))
            psq = es.enter_context(PST('b_psq', [128, 512], F32))
            s.op('dve', lambda e: e.memset(onesN[:], 1.0 / W), writes=['onesN'])
            for blk in range(4):
                for j in range(CF_WIDTH):
                    ek = 'dve'
                    s.op(ek, lambda e: e.tensor_scalar(out=Dc[:, blk * CF_WIDTH + j, :], in0=idf_g[:],
                                                       scalar1=PRM['cdw'][:, blk, j:j + 1], scalar2=None, op0=ALU.mult),
                         reads=['idf_g'], writes=['Dc'])
            for c in range(L // 512):
                sl = c % 2
                t0 = c * 512
                lo, hi = max(0, t0 - 15), min(L, t0 + 512 + 15)
                e0 = lo - (t0 - 15)
                if e0 > 0:
                    s.op('pool', lambda e: e.memset(hex_[:, sl, :, 0:e0], 0.0), writes=['hex%d' % sl])
                if hi - (t0 - 15) < n:
                    s.op('pool', lambda e: e.memset(hex_[:, sl, :, hi - (t0 - 15):n], 0.0), writes=['hex%d' % sl])
                s.dma('sp', hex_[:, sl, :, e0:e0 + hi - lo],
                      sc['hcf'].ap().rearrange('(b q) t -> q b t', b=4)[:, :, lo:hi], writes=['hex%d' % sl])
                s.dma('sp', sgt[:, sl], sc['sgT'].ap()[3 * W:4 * W, t0:t0 + 512].rearrange('(b q) t -> q b t', b=4),
                      writes=['bsgt%d' % sl])
                for blk in range(4):
                    for j in range(CF_WIDTH):
                        s.op('pe', lambda e: e.matmul(ps[blk][:], lhsT=Dc[:, blk * CF_WIDTH + j, :],
                                                      rhs=hex_[:, sl, blk, j:j + 512], start=(j == 0), stop=(j == CF_WIDTH - 1)),
                             reads=['Dc', 'hex%d' % sl], writes=['b_ps%d' % blk])
                    s.op('act', lambda e: e.activation(out=hc[:, blk], in_=ps[blk][:], func=AF.Identity,
                                                       bias=PRM['cdb'][:, blk, 0:1]),
                         reads=['b_ps%d' % blk], writes=['hc%d' % blk])
                    s.op('act', lambda e: e.activation(out=sq[:, blk], in_=ps[blk][:], func=AF.Square,
                                                       bias=PRM['cdb'][:, blk, 0:1]),
                         reads=['b_ps%d' % blk], writes=['sq%d' % blk])
                for blk in range(4):
                    s.op('pe', lambda e: e.matmul(psm[:], lhsT=onesN[:], rhs=hc[:, blk], start=(blk == 0), stop=(blk == 3)),
                         reads=['onesN', 'hc%d' % blk], writes=['psm'])
                for blk in range(4):
                    s.op('pe', lambda e: e.matmul(psq[:], lhsT=onesN[:], rhs=sq[:, blk], start=(blk == 0), stop=(blk == 3)),
                         reads=['onesN', 'sq%d' % blk], writes=['psq'])
                s.op('act', lambda e: e.copy(out=mean[:], in_=psm[:]), reads=['psm'], writes=['mean'])
                s.op('dve', lambda e: e.tensor_tensor(out=rstd[:], in0=mean[:], in1=mean[:], op=ALU.mult),
                     reads=['mean'], writes=['rstd'])
                s.op('dve', lambda e: e.tensor_tensor(out=rstd[:], in0=psq[:], in1=rstd[:], op=ALU.subtract),
                     reads=['psq', 'rstd'], writes=['rstd'])
                s.op('dve', lambda e: e.tensor_scalar(out=rstd[:], in0=rstd[:], scalar1=EPS, scalar2=None, op0=ALU.add),
                     reads=['rstd'], writes=['rstd'])
                s.op('act', lambda e: e.activation(out=rstd[:], in_=rstd[:], func=AF.Sqrt), reads=['rstd'], writes=['rstd'])
                s.op('dve', lambda e: e.reciprocal(out=rstd[:], in_=rstd[:]), reads=['rstd'], writes=['rstd'])
                for blk in range(4):
                    xs = blk % 2
                    s.op('dve', lambda e: e.tensor_tensor(out=xn[:, xs], in0=hc[:, blk], in1=mean[:], op=ALU.subtract),
                         reads=['hc%d' % blk, 'mean'], writes=['xn%d' % xs])
                    s.op('dve', lambda e: e.tensor_tensor(out=xn[:, xs], in0=xn[:, xs], in1=rstd[:], op=ALU.mult),
                         reads=['xn%d' % xs, 'rstd'], writes=['xn%d' % xs])
                    s.op('act', lambda e: e.activation(out=xn[:, xs], in_=xn[:, xs], func=AF.Silu,
                                                       scale=PRM['clg'][:, blk, 0:1], bias=PRM['clb'][:, blk, 0:1]),
                         reads=['xn%d' % xs], writes=['xn%d' % xs])
                    s.op('dve', lambda e: e.tensor_tensor(out=ost[:, sl, blk], in0=xn[:, xs], in1=sgt[:, sl, blk], op=ALU.mult),
                         reads=['xn%d' % xs, 'bsgt%d' % sl], writes=['bost%d' % sl])
                s.dma('pool', sc['bsT'].ap()[3 * W:4 * W, t0:t0 + 512].rearrange('(b q) t -> q b t', b=4), ost[:, sl],
                      reads=['bost%d' % sl], writes=['bsT'])
        s.barrier()

    B.pass_2b = pass_2b

    NA = 1280
    bk = _bucket_table(-640, 639)
    oh = np.zeros((32, NA), np.float32)
    for n_ in range(NA - 1):
        oh[bk[(639 - n_) + 640], n_] = 1.0
    onehot_d = B.const_inp('c_onehot', oh)
    jmat_d = B.const_inp('c_jmat', np.ascontiguousarray(np.eye(128, dtype=np.float32)[::-1]))
    Gd = B.scratch('Gd', [4, NA], F32)
    Bt = B.scratch('Bt', [4, 6, 128, 512], F32)
    cbcol = ES.enter_context(SBT('g_cbcol', [128, 2, 4], F32))
    neglam = ES.enter_context(SBT('g_neglam', [128, 1], F32))
    gsub = ES.enter_context(SBT('g_gsub', [128, 128], F32))

    def pass_att_setup():
        with ExitStack() as es:
            ohs = es.enter_context(SBT('as_oh', [32, NA], F32))
            rb = es.enter_context(SBT('as_rb', [32, 4], F32))
            gs = es.enter_context(SBT('as_gs', [4, NA], F32))
            jm = es.enter_context(SBT('as_jm', [128, 128], F32))
            hk = es.enter_context(SBT('as_hk', [128, 2, 512], F32))
            tt = es.enter_context(SBT('as_tt', [128, 2, 512], F32))
            ps = [es.enter_context(PST('as_ps%d' % i, [128, 512], F32)) for i in range(2)]
            s.dma('sp', ohs[:], onehot_d.ap(), writes=['ohs'])
            s.dma('sp', rb[:], rel_bias.ap(), writes=['rb'])
            s.dma('sp', jm[:], jmat_d.ap(), writes=['jm'])
            s.dma('sp', cbcol[:, 0, :], rel_bias.ap()[15:16, :].partition_broadcast(128).rearrange('p a h -> p (a h)'), writes=['cbcol'])
            s.dma('sp', cbcol[:, 1, :], rel_bias.ap()[31:32, :].partition_broadcast(128).rearrange('p a h -> p (a h)'), writes=['cbcol'])
            for i, (c0, c1) in enumerate(((0, 512), (512, 1024), (1024, NA))):
                k = i % 2
                s.op('pe', lambda e: e.matmul(ps[k][0:4, 0:c1 - c0], lhsT=rb[:], rhs=ohs[:, c0:c1], start=True, stop=True),
                     reads=['rb', 'ohs'], writes=['as_ps%d' % k])
                s.op('dve', lambda e: e.tensor_copy(out=gs[:, c0:c1], in_=ps[k][0:4, 0:c1 - c0]),
                     reads=['as_ps%d' % k], writes=['gs'])
            s.dma('sp', Gd.ap(), gs[:], reads=['gs'], writes=['Gd'])
            it = 0
            for h in range(4):
                for di in range(6):
                    dl = di - 1
                    off = 512 - 128 * dl
                    sl = it % 2
                    it += 1
                    src = bass.AP(tensor=Gd.ap().tensor, offset=h * NA + off, ap=[[1, 128], [1, 512]])
                    s.dma('sp', hk[:, sl], src, reads=['Gd'], writes=['hk%d' % sl])
                    s.op('pe', lambda e: e.matmul(ps[sl][:], lhsT=jm[:], rhs=hk[:, sl], start=True, stop=True),
                         reads=['jm', 'hk%d' % sl], writes=['as_ps%d' % sl])
                    s.op('act', lambda e: e.mul(out=tt[:, sl], in_=ps[sl][:], mul=8.0),
                         reads=['as_ps%d' % sl], writes=['tt%d' % sl])
                    s.dma('pool', Bt.ap()[h, di], tt[:, sl], reads=['tt%d' % sl], writes=['Bt'])
        s.barrier()

    def pass_lambda(l):
        lam_init = 0.8 - 0.6 * math.exp(-0.3 * l)
        with ExitStack() as es:
            lt = es.enter_context(SBT('lm_lt', [128, 4, 64], F32))
            pr = es.enter_context(SBT('lm_pr', [128, 2, 64], F32))
            dd = es.enter_context(SBT('lm_dd', [128, 4], F32))
            for i, k in enumerate(('lambda_q1', 'lambda_k1', 'lambda_q2', 'lambda_k2')):
                s.dma('sp', lt[:, i], lam_in[k].ap()[l:l + 1, :].partition_broadcast(128).rearrange('p a d -> p (a d)'),
                      writes=['lt'])
            for i in range(2):
                s.op('dve', lambda e: e.tensor_tensor(out=pr[:, i], in0=lt[:, 2 * i], in1=lt[:, 2 * i + 1], op=ALU.mult),
                     reads=['lt'], writes=['pr'])
                s.op('act', lambda e: e.activation(out=pr[:, i], in_=pr[:, i], func=AF.Identity, accum_out=dd[:, i:i + 1]),
                     reads=['pr'], writes=['pr', 'dd'])
                s.op('act', lambda e: e.activation(out=dd[:, 2 + i:3 + i], in_=dd[:, i:i + 1], func=AF.Exp),
                     reads=['dd'], writes=['dd'])
            s.op('dve', lambda e: e.tensor_tensor(out=dd[:, 0:1], in0=dd[:, 3:4], in1=dd[:, 2:3], op=ALU.subtract),
                 reads=['dd'], writes=['dd'])
            s.op('dve', lambda e: e.tensor_scalar(out=neglam[:], in0=dd[:, 0:1], scalar1=-lam_init, scalar2=None, op0=ALU.add),
                 reads=['dd'], writes=['neglam'])
            s.dma('sp', gsub[:], subln_g.ap()[l:l + 1, :].partition_broadcast(128).rearrange('p a d -> p (a d)'), writes=['gsub'])
            s.op('act', lambda e: e.mul(out=gsub[:], in_=gsub[:], mul=1.0 - lam_init), reads=['gsub'], writes=['gsub'])
        s.barrier()

    B.pass_att_setup = pass_att_setup
    B.pass_lambda = pass_lambda

    def att_alloc(es, pfx):
        A = {}
        A['Qc'] = es.enter_context(SBT(pfx + 'Qc', [128, 2, 512], BF16))
        A['Pt'] = es.enter_context(SBT(pfx + 'Pt', [128, 2, 2, 512], BF16))
        A['tS'] = es.enter_context(SBT(pfx + 'tS', [128, 2, 2, 512], F32))
        A['osb'] = es.enter_context(SBT(pfx + 'osb', [128, 2, 4, 129], F32))
        A['om'] = es.enter_context(SBT(pfx + 'om', [128, 2, 4, 128], F32))
        A['rc'] = es.enter_context(SBT(pfx + 'rc', [128, 8], F32))
        A['junk'] = es.enter_context(SBT(pfx + 'junk', [128, 128], F32))
        A['yb'] = es.enter_context(SBT(pfx + 'yb', [128, 4, 128], F32))
        A['sgt'] = es.enter_context(SBT(pfx + 'sgt', [128, 2, 512], BF16))
        A['ost'] = es.enter_context(SBT(pfx + 'ost', [128, 2, 512], BF16))
        A['Bth'] = es.enter_context(SBT(pfx + 'Bth', [128, 6, 512], F32))
        A['psS'] = [es.enter_context(PST(pfx + 'psS%d' % i, [128, 2, 512], F32)) for i in range(2)]
        A['psO'] = [[es.enter_context(PST(pfx + 'psO%d%d' % (m, j), [128, 512], F32)) for j in range(2)] for m in range(2)]
        A['tsi'] = 0
        return A

    def att_chunk(A, qs, NTt, getK, getV, kind, kv_res):
        Qc, Pt, tS, osb, om, rc, junk, yb, sgt, ost, Bth, psS, psO = (A[k] for k in (
            'Qc', 'Pt', 'tS', 'osb', 'om', 'rc', 'junk', 'yb', 'sgt', 'ost', 'Bth', 'psS', 'psO'))
        scale = 64 ** -0.5

        def Oap(m, qb):
            return psO[m][0][:, qb * 129:(qb + 1) * 129] if qb < 3 else psO[m][1][:, 0:129]

        def QK(kt):
            k = kt % 2
            for m in range(2):
                s.op('pe', lambda e: e.matmul(psS[k][:, m, :], lhsT=getK(kt)[64 * m:64 * m + 64, :], rhs=Qc[64 * m:64 * m + 64, qs, :],
                                              start=True, stop=True, tile_position=(64 * m, 0)),
                     reads=kv_res + ['aQc%d' % qs], writes=['apsS%d' % k])

        def EXP(kt):
            k = kt % 2
            kd = kind(kt)
            if kd[0] == 'mixed':
                ts_ = A['tsi'] % 2
                A['tsi'] += 1
                for m in range(2):
                    s.op('dve', lambda e: e.tensor_tensor(out=tS[:, ts_, m], in0=psS[k][:, m, :], in1=Bth[:, kd[1], :], op=ALU.add),
                         reads=['apsS%d' % k, 'aBth'], writes=['atS%d' % ts_])
                src = tS[:, ts_].rearrange('p a b -> p (a b)')
                rd = ['atS%d' % ts_]
            else:
                src = psS[k][:].rearrange('p a b -> p (a b)')
                rd = ['apsS%d' % k]
            dst = Pt[:, k].rearrange('p a b -> p (a b)')
            if kd[-1] is None:
                s.op('act', lambda e: e.activation(out=dst, in_=src, func=AF.Exp, scale=scale), reads=rd, writes=['aPt%d' % k])
            else:
                s.op('act', lambda e: e.activation(out=dst, in_=src, func=AF.Exp, scale=scale, bias=kd[-1]),
                     reads=rd + ['cbcol', 'kbc', 'otab'], writes=['aPt%d' % k])

        def PV(kt):
            k = kt % 2
            for m in range(2):
                for qb in range(4):
                    s.op('pe', lambda e: e.matmul(Oap(m, qb), lhsT=Pt[:, k, m, qb * 128:(qb + 1) * 128], rhs=getV(kt),
                                                  start=(kt == 0 and qb in (0, 3)), stop=(kt == NTt - 1)),
                         reads=['aPt%d' % k] + kv_res, writes=['apsO'])
        QK(0)
        for kt in range(NTt):
            if kt + 1 < NTt:
                QK(kt + 1)
            EXP(kt)
            PV(kt)
        for m in range(2):
            ek = 'dve' if m == 0 else 'act'
            s.op(ek, copy_on(ek, osb[:, m, 0:3, :].rearrange('p a b -> p (a b)'), psO[m][0][:, 0:387]),
                 reads=['apsO'], writes=['aosb'])
            s.op('dve', lambda e: e.tensor_copy(out=osb[:, m, 3, :], in_=psO[m][1][:, 0:129]), reads=['apsO'], writes=['aosb'])
        s.op('dve', lambda e: e.reciprocal(out=rc[:, 0:8].rearrange('p (a b) -> p a b', a=2), in_=osb[:, :, :, 128]),
             reads=['aosb'], writes=['arc'])
        for m in range(2):
            for qb in range(4):
                s.op('dve', lambda e: e.tensor_scalar(out=om[:, m, qb], in0=osb[:, m, qb, 0:128],
                                                      scalar1=rc[:, m * 4 + qb:m * 4 + qb + 1], scalar2=None, op0=ALU.mult),
                     reads=['aosb', 'arc'], writes=['aom'])
        psT = psS[0][:, 0, :]
        omf = om[:, 0].rearrange('p a b -> p (a b)')
        s.op('dve', lambda e: e.scalar_tensor_tensor(out=omf, in0=om[:, 1].rearrange('p a b -> p (a b)'), scalar=neglam[:, 0:1],
                                                     in1=omf, op0=ALU.mult, op1=ALU.add),
             reads=['aom', 'neglam'], writes=['aom'])
        sqv = om[:, 1]
        s.op('dve', lambda e: e.tensor_tensor(out=sqv, in0=om[:, 0], in1=om[:, 0], op=ALU.mult), reads=['aom'], writes=['aom1'])
        s.op('dve', lambda e: e.reduce_sum(out=rc[:, 0:4], in_=sqv, axis=mybir.AxisListType.X), reads=['aom1', 'arc'], writes=['arc'])
        s.op('dve', lambda e: e.tensor_scalar(out=rc[:, 0:4], in0=rc[:, 0:4], scalar1=1.0 / 128, scalar2=EPS, op0=ALU.mult, op1=ALU.add),
             reads=['arc'], writes=['arc'])
        s.op('act', lambda e: e.activation(out=rc[:, 0:4], in_=rc[:, 0:4], func=AF.Sqrt), reads=['arc'], writes=['arc'])
        s.op('dve', lambda e: e.reciprocal(out=rc[:, 0:4], in_=rc[:, 0:4]), reads=['arc'], writes=['arc'])
        for qb in range(4):
            s.op('dve', lambda e: e.scalar_tensor_tensor(out=yb[:, qb], in0=om[:, 0, qb], scalar=rc[:, qb:qb + 1],
                                                         in1=gsub[:], op0=ALU.mult, op1=ALU.mult),
                 reads=['aom', 'arc', 'gsub'], writes=['ayb'])
            s.op('pe', lambda e: e.transpose(out=psT[:, qb * 128:(qb + 1) * 128], in_=yb[:, qb], identity=idf_g[:]),
                 reads=['ayb', 'idf_g'], writes=['apsS0'])
        s.op('dve', lambda e: e.tensor_tensor(out=ost[:, qs], in0=psT, in1=sgt[:, qs], op=ALU.mult),
             reads=['apsS0', 'asgt%d' % qs], writes=['aost%d' % qs])

    def pass_4(l, nm, L, heads=(0, 1, 2, 3)):
        sc = SC[nm]
        NT, NJ = L // 128, L // 512
        with ExitStack() as es:
            A = att_alloc(es, 't_')
            Kh2 = es.enter_context(SBT('t_Kh', [128, 2, L], BF16))
            Vh2 = es.enter_context(SBT('t_Vh', [128, 2, NT, 129], BF16))

            def load_kv(hi):
                h_, hb_ = heads[hi], hi % 2
                s.dma('sp', Kh2[:, hb_], sc['Kt'].ap()[h_ * 128:(h_ + 1) * 128, :], writes=['aKh%d' % hb_])
                s.op('pool', lambda e: e.memset(Vh2[:, hb_, :, 128:129], 1.0), writes=['aVh%d' % hb_])
                for v0 in range(0, NT, 32):
                    v1 = min(NT, v0 + 32)
                    s.dma('sp', Vh2[:, hb_, v0:v1, 0:128],
                          sc['V'].ap()[v0 * 128:v1 * 128, h_ * 128:(h_ + 1) * 128].rearrange('(kt p) e -> p kt e', p=128),
                          writes=['aVh%d' % hb_])
            load_kv(0)
            for hi, h in enumerate(heads):
                hb = hi % 2
                Kh, Vh = Kh2[:, hb], Vh2[:, hb]
                if hi + 1 < len(heads):
                    load_kv(hi + 1)
                s.dma('sp', A['Bth'][:], Bt.ap()[h].rearrange('d k q -> k d q'), writes=['aBth'])
                for J in range(NJ):
                    qs = J % 2
                    s.dma('sp', A['Qc'][:, qs], sc['Qt'].ap()[h * 128:(h + 1) * 128, J * 512:(J + 1) * 512], writes=['aQc%d' % qs])
                    s.dma('sp', A['sgt'][:, qs], sc['sgT'].ap()[W + h * 128:W + (h + 1) * 128, J * 512:(J + 1) * 512],
                          writes=['asgt%d' % qs])

                    def kind(kt, J=J, h=h):
                        dl = kt - 4 * J
                        if -1 <= dl <= 4:
                            return ('mixed', dl + 1, None)
                        return ('far', cbcol[:, (0 if dl < 0 else 1), h:h + 1])
                    att_chunk(A, qs, NT, lambda kt, Kh=Kh: Kh[:, kt * 128:(kt + 1) * 128], lambda kt, Vh=Vh: Vh[:, kt, :], kind, ['aKh%d' % hb, 'aVh%d' % hb])
                    s.dma('pool', sc['bsT'].ap()[W + h * 128:W + (h + 1) * 128, J * 512:(J + 1) * 512], A['ost'][:, qs],
                          reads=['aost%d' % qs], writes=['bsT'])
        s.barrier()

    B.pass_4 = pass_4

    OWN = {}
    for nm, L in seqs:
        if not own or nm not in own:
            continue
        NB, s0s = own[nm]
        NBLK = L // 128
        NV = NBLK + 2
        tabs = dict(idxK=[], idxV=[], idxQ=[], idxO=[], msk=[], dcol=[])
        for s0 in s0s:
            slots = [s0 - 1 if s0 > 0 else None] + list(range(s0, s0 + NB)) + [s0 + NB if s0 + NB < NBLK else None]
            slots += list(range(s0 + NB + 1, NBLK)) + list(range(0, max(0, s0 - 1)))
            slots += [None] * (NV - len(slots))
            assert len(slots) == NV and sorted(b for b in slots if b is not None) == list(range(NBLK))
            iK = np.zeros((128, 4, NV), np.int32)
            iV = np.zeros((128, 4, NV), np.int32)
            msk = np.zeros((128, 3, NV), np.float32)
            for v, b in enumerate(slots):
                bb = NBLK if b is None else b
                for h in range(4):
                    iK[:, h, v] = (h * (NBLK + 1) + bb) * 128 + np.arange(128)
                    iV[:, h, v] = h * (L + 128) + bb * 128 + np.arange(128)
                if v >= NB + 2:
                    if b is None:
                        msk[:, 2, v] = 1.0
                    elif b > s0:
                        msk[:, 1, v] = 1.0
                    else:
                        msk[:, 0, v] = 1.0
            iQ = np.zeros((64, 4, 2, NB), np.int32)
            iO = np.zeros((128, 4, NB), np.int32)
            for h in range(4):
                for b in range(NB):
                    for m in range(2):
                        iQ[:, h, m, b] = (h * NBLK + s0 + b) * 128 + m * 64 + np.arange(64)
                    iO[:, h, b] = (h * NBLK + s0 + b) * 128 + np.arange(128)
            dcol = np.zeros((128, 2), np.float32)
            dcol[:, 0] = -30000.0 if slots[0] is None else 0.0
            dcol[:, 1] = -30000.0 if slots[NB + 1] is None else 0.0
            tabs['idxK'].append(iK.reshape(128, -1)); tabs['idxV'].append(iV.reshape(128, -1))
            tabs['idxQ'].append(iQ.reshape(64, -1)); tabs['idxO'].append(iO.reshape(128, -1))
            tabs['msk'].append(msk.reshape(128, -1)); tabs['dcol'].append(dcol)
        OWN[nm] = dict(NB=NB, NV=NV, **{k: B.core_const_inp('o_%s_%s' % (k, nm), v) for k, v in tabs.items()})

    def pass_4o(l, nm, L, heads=(0, 1, 2, 3)):
        sc = SC[nm]
        ow = OWN[nm]
        NB, NV = ow['NB'], ow['NV']
        NJ = NB // 4
        KtBv = sc['KtB'].ap().rearrange('b p k -> (b p) k')
        QtBv = sc['QtB'].ap().rearrange('b p k -> (b p) k')
        sgBv = sc['sgB'].ap().rearrange('b p k -> (b p) k')
        bsBv = sc['bsB'].ap().rearrange('b p k -> (b p) k')
        V2v = sc['V2'].ap().rearrange('h t e -> (h t) e')
        with ExitStack() as es:
            A = att_alloc(es, 'o_')
            iV = es.enter_context(SBT('o_iV', [128, 4 * NV], I32))
            iKf = es.enter_context(SBT('o_iKf', [128, 4 * NV], I32))
            iO = es.enter_context(SBT('o_iO', [128, 4 * NB], I32))
            msk = es.enter_context(SBT('o_msk', [128, 3, NV], F32))
            dcol = es.enter_context(SBT('o_dcol', [128, 2], F32))
            kbc = es.enter_context(SBT('o_kbc', [128, 4, NV], F32))
            kbe = es.enter_context(SBT('o_kbe', [128, 4, 2], F32))
            Kh = es.enter_context(SBT('o_Kh', [128, NV * 128], BF16))
            Vh = es.enter_context(SBT('o_Vh', [128, NV, 129], BF16))

            def gather(out, src2d, idx_ap, reads, writes):
                s.dma('pool', None, None, reads=reads, writes=writes,
                      fn=lambda e: e.indirect_dma_start(out=out, out_offset=None, in_=src2d,
                                                        in_offset=bass.IndirectOffsetOnAxis(ap=idx_ap, axis=0)))
            for nm_, tl in (('idxV', iV), ('idxK', iKf), ('idxO', iO), ('dcol', dcol)):
                s.dma('sp', tl[:], ow[nm_].ap(), writes=['otab'])
            s.dma('sp', msk[:].rearrange('p a b -> p (a b)'), ow['msk'].ap(), writes=['otab'])
            for h in range(4):
                s.op('dve', lambda e: e.tensor_scalar(out=kbc[:, h, :], in0=msk[:, 0, :], scalar1=cbcol[:, 0, h:h + 1], scalar2=None, op0=ALU.mult),
                     reads=['otab', 'cbcol'], writes=['kbc'])
                s.op('dve', lambda e: e.scalar_tensor_tensor(out=kbc[:, h, :], in0=msk[:, 1, :], scalar=cbcol[:, 1, h:h + 1], in1=kbc[:, h, :],
                                                             op0=ALU.mult, op1=ALU.add), reads=['otab', 'cbcol', 'kbc'], writes=['kbc'])
                s.op('dve', lambda e: e.scalar_tensor_tensor(out=kbc[:, h, :], in0=msk[:, 2, :], scalar=-30000.0, in1=kbc[:, h, :],
                                                             op0=ALU.mult, op1=ALU.add), reads=['otab', 'kbc'], writes=['kbc'])
                s.op('dve', lambda e: e.tensor_tensor(out=kbe[:, h, 0:1], in0=cbcol[:, 0, h:h + 1], in1=dcol[:, 0:1], op=ALU.add),
                     reads=['otab', 'cbcol'], writes=['kbc'])
                s.op('dve', lambda e: e.tensor_tensor(out=kbe[:, h, 1:2], in0=cbcol[:, 1, h:h + 1], in1=dcol[:, 1:2], op=ALU.add),
                     reads=['otab', 'cbcol'], writes=['kbc'])
            for h in heads:
                s.op('pool', lambda e: e.memset(Vh[:, :, 128:129], 1.0), writes=['aVh'])
                for v in range(NV):
                    c_ = h * NV + v
                    gather(Kh[:, v * 128:(v + 1) * 128], KtBv, iKf[:, c_:c_ + 1], ['otab', 'p1out'], ['aKh'])
                    gather(Vh[:, v, 0:128], V2v, iV[:, c_:c_ + 1], ['otab', 'p1out'], ['aVh'])
                s.dma('sp', A['Bth'][:], Bt.ap()[h].rearrange('d k q -> k d q'), writes=['aBth'])
                for J in range(NJ):
                    qs = J % 2
                    for b_ in range(4):
                        c_ = h * NB + J * 4 + b_
                        gather(A['Qc'][:, qs, b_ * 128:(b_ + 1) * 128], QtBv, iO[:, c_:c_ + 1], ['otab', 'p1out'], ['aQc%d' % qs])
                        gather(A['sgt'][:, qs, b_ * 128:(b_ + 1) * 128], sgBv, iO[:, c_:c_ + 1], ['otab', 'p1out'], ['asgt%d' % qs])

                    def kind(v, J=J, h=h):
                        dl = v - 4 * J
                        if 0 <= dl <= 5:
                            if v == 0:
                                return ('mixed', dl, dcol[:, 0:1])
                            if v == NB + 1:
                                return ('mixed', dl, dcol[:, 1:2])
                            return ('mixed', dl, None)
                        if v == 0:
                            return ('far', kbe[:, h, 0:1])
                        if v == NB + 1:
                            return ('far', kbe[:, h, 1:2])
                        if v <= NB:
                            return ('far', cbcol[:, (0 if v < 4 * J else 1), h:h + 1])
                        return ('far', kbc[:, h, v:v + 1])
                    att_chunk(A, qs, NV, lambda v: Kh[:, v * 128:(v + 1) * 128], lambda v: Vh[:, v, :], kind, ['aKh', 'aVh'])
                    for b_ in range(4):
                        c_ = h * NB + J * 4 + b_
                        s.dma('pool', None, None, reads=['aost%d' % qs, 'otab'], writes=['bsB'],
                              fn=lambda e: e.indirect_dma_start(out=bsBv, out_offset=bass.IndirectOffsetOnAxis(ap=iO[:, c_:c_ + 1], axis=0),
                                                                in_=A['ost'][:, qs, b_ * 128:(b_ + 1) * 128], in_offset=None))
        s.barrier()

    B.pass_4o = pass_4o

    def pass_5(l, nm, L, last, b_blk=False):
        sc = SC[nm]
        x_src = x_in[nm] if l == 0 else sc['xres']
        with ExitStack() as es:
            wbr = es.enter_context(SBT('f_wbr', [128, 16, D], BF16))
            wo = es.enter_context(SBT('f_wo', [128, 8, D], BF16))
            bs = es.enter_context(SBT('f_bs', [128, 2, 16, 512], BF16))
            mg = es.enter_context(SBT('f_mg', [128, 2, 4, 512], BF16))
            acc = es.enter_context(SBT('f_acc', [128, 512], F32))
            tmp = es.enter_context(SBT('f_tmp', [128, 2, 512], F32))
            mx = es.enter_context(SBT('f_mx', [128, 8, 512], BF16))
            xt = es.enter_context(SBT('f_xt', [128, 2, D], F32))
            xo = es.enter_context(SBT('f_xo', [128, 2, D], F32))
            gt = es.enter_context(SBT('f_gt', [128, D], F32))
            junk = es.enter_context(SBT('f_junk', [128, D], BF16))
            ss = es.enter_context(SBT('f_ss', [128, 4], F32))
            psp = [es.enter_context(PST('f_psp%d' % i, [128, 512], F32)) for i in range(4)]
            pso = [es.enter_context(PST('f_pso%d' % i, [128, 512], F32)) for i in range(2)]
            s.dma('sp', wbr[:], wbr_bf.ap(), writes=['wbr'])
            s.dma('sp', wo[:], wout_bf.ap(), writes=['wo'])
            if last:
                s.dma('sp', gt[:], final_g.ap().rearrange('(a d) -> a d', a=1).partition_broadcast(128).rearrange('p a d -> p (a d)'),
                      writes=['fgt'])
            mgv = sc['mgT'].ap().rearrange('(n r) t -> r n t', n=4)
            mi = 0
            ti = 0
            for g in range(L // 512):
                bsl = g % 2
                t0 = g * 512
                if not b_blk:
                    s.dma('sp', bs[:, bsl], sc['bsT'].ap()[:, t0:t0 + 512].rearrange('(a p) t -> p a t', p=128), writes=['bs%d' % bsl])
                else:
                    s.dma('sp', bs[:, bsl, 0:4], sc['bsT'].ap()[0:W, t0:t0 + 512].rearrange('(a p) t -> p a t', p=128), writes=['bs%d' % bsl])
                    s.dma('sp', bs[:, bsl, 8:16], sc['bsT'].ap()[2 * W:4 * W, t0:t0 + 512].rearrange('(a p) t -> p a t', p=128), writes=['bs%d' % bsl])
                    NBK = L // 128
                    for h_ in range(4):
                        s.dma('sp', bs[:, bsl, 4 + h_, :].rearrange('p (b k) -> p b k', k=128),
                              sc['bsB'].ap()[h_ * NBK + g * 4:h_ * NBK + g * 4 + 4].rearrange('b p k -> p b k'), writes=['bs%d' % bsl])
                for dmb in range(8):
                    msl = mi % 2
                    mi += 1
                    s.dma('sp', mg[:, msl], mgv[dmb * 128:(dmb + 1) * 128, :, t0:t0 + 512], writes=['mg%d' % msl])
                    for n in range(4):
                        for wc in range(4):
                            s.op('pe', lambda e: e.matmul(psp[n][:], lhsT=wbr[:, n * 4 + wc, dmb * 128:(dmb + 1) * 128],
                                                          rhs=bs[:, bsl, n * 4 + wc, :], start=(wc == 0), stop=(wc == 3)),
                                 reads=['wbr', 'bs%d' % bsl], writes=['psp%d' % n])
                        if n == 0:
                            s.op('dve', lambda e: e.tensor_tensor(out=acc[:], in0=psp[n][:], in1=mg[:, msl, n], op=ALU.mult),
                                 reads=['psp%d' % n, 'mg%d' % msl], writes=['acc'])
                        else:
                            ts_ = n % 2
                            s.op('dve', lambda e: e.tensor_tensor(out=tmp[:, ts_], in0=psp[n][:], in1=mg[:, msl, n], op=ALU.mult),
                                 reads=['psp%d' % n, 'mg%d' % msl], writes=['ftmp%d' % ts_])
                            if n < 3:
                                s.op('pool', lambda e: e.tensor_tensor(out=acc[:], in0=acc[:], in1=tmp[:, ts_], op=ALU.add),
                                     reads=['acc', 'ftmp%d' % ts_], writes=['acc'])
                            else:
                                s.op('pool', lambda e: e.tensor_tensor(out=mx[:, dmb], in0=acc[:], in1=tmp[:, ts_], op=ALU.add),
                                     reads=['acc', 'ftmp%d' % ts_], writes=['mx%d' % dmb])
                for tt in range(4):
                    xs = ti % 2
                    ti += 1
                    r0 = t0 + tt * 128
                    s.dma('sp', xt[:, xs], x_src.ap()[r0:r0 + 128, :], writes=['fxt%d' % xs])
                    for hf in range(2):
                        for dc in range(8):
                            s.op('pe', lambda e: e.matmul(pso[hf][:], lhsT=mx[:, dc, tt * 128:(tt + 1) * 128],
                                                          rhs=wo[:, dc, hf * 512:(hf + 1) * 512], start=(dc == 0), stop=(dc == 7)),
                                 reads=['mx%d' % dc, 'wo'], writes=['pso%d' % hf])
                        s.op('dve', lambda e: e.tensor_tensor(out=xo[:, xs, hf * 512:(hf + 1) * 512], in0=pso[hf][:],
                                                              in1=xt[:, xs, hf * 512:(hf + 1) * 512], op=ALU.add),
                             reads=['pso%d' % hf, 'fxt%d' % xs], writes=['fxo%d' % xs])
                    if not last:
                        s.dma('pool', sc['xres'].ap()[r0:r0 + 128, :], xo[:, xs], reads=['fxo%d' % xs], writes=['xres'])
                    else:
                        s.op('act', lambda e: e.activation(out=junk[:], in_=xo[:, xs], func=AF.Square, accum_out=ss[:, xs:xs + 1]),
                             reads=['fxo%d' % xs], writes=['fjunk', 'fss%d' % xs])
                        s.op('dve', lambda e: e.tensor_scalar(out=ss[:, 2 + xs:3 + xs], in0=ss[:, xs:xs + 1], scalar1=1.0 / D,
                                                              scalar2=EPS, op0=ALU.mult, op1=ALU.add),
                             reads=['fss%d' % xs], writes=['frs%d' % xs])
                        s.op('act', lambda e: e.activation(out=ss[:, 2 + xs:3 + xs], in_=ss[:, 2 + xs:3 + xs], func=AF.Sqrt),
                             reads=['frs%d' % xs], writes=['frs%d' % xs])
                        s.op('dve', lambda e: e.reciprocal(out=ss[:, 2 + xs:3 + xs], in_=ss[:, 2 + xs:3 + xs]),
                             reads=['frs%d' % xs], writes=['frs%d' % xs])
                        s.op('dve', lambda e: e.scalar_tensor_tensor(out=xt[:, xs], in0=xo[:, xs], scalar=ss[:, 2 + xs:3 + xs],
                                                                     in1=gt[:], op0=ALU.mult, op1=ALU.mult),
                             reads=['fxo%d' % xs, 'frs%d' % xs, 'fgt'], writes=['fxt%d' % xs])
                        s.dma('pool', y_out[nm].ap()[r0:r0 + 128, :], xt[:, xs], reads=['fxt%d' % xs], writes=['yout'])
        s.barrier()

    B.pass_5 = pass_5

    HC = {}
    for nm, L in seqs:
        if L % 8192:
            continue
        N = 2 * L
        N2 = N // 128
        NC2 = N2 // 128
        if N2 in HC:
            continue
        a128 = np.arange(128)
        th = 2 * np.pi * np.outer(a128, a128) / 128.0
        FA = np.concatenate([np.cos(th), -np.sin(th)], 1)
        n2 = np.arange(N2)
        ph = 2 * np.pi * np.outer(n2, a128) / N
        Tr, Ti = np.cos(ph), -np.sin(ph)
        TT4 = np.stack([np.repeat(Tr[:, None, :], 4, 1), np.repeat(Ti[:, None, :], 4, 1)], 1)
        TT4 = TT4.reshape(NC2, 128, 2, 4, 128).transpose(1, 0, 2, 3, 4)
        TTt = np.stack([np.repeat(Tr.T[:, None, :], 2, 1), np.repeat(Ti.T[:, None, :], 2, 1)], 1)
        cph = 2 * np.pi * np.outer(n2, n2) / N2
        Cr, Ci = np.cos(cph), -np.sin(cph)
        CC = np.zeros((128, NC2, NC2, 3, 128))
        for j in range(NC2):
            for kc in range(NC2):
                blk = (slice(j * 128, (j + 1) * 128), slice(kc * 128, (kc + 1) * 128))
                CC[:, j, kc, 0], CC[:, j, kc, 1], CC[:, j, kc, 2] = Cr[blk], Ci[blk], -Ci[blk]
        CI1 = np.concatenate([Cr, -Ci], 1).reshape(NC2, 128, 2 * N2).transpose(1, 0, 2)
        CI2 = np.concatenate([Ci, Cr], 1).reshape(NC2, 128, 2 * N2).transpose(1, 0, 2)
        thi = 2 * np.pi * np.outer(a128, np.arange(64)) / 128.0
        FI = np.stack([np.cos(thi) / N, -np.sin(thi) / N], 1)
        f32 = lambda a: np.ascontiguousarray(a, dtype=np.float32)
        HC[N2] = dict(FA=B.const_inp('c_FA', f32(FA)) if 'FA' not in HC.get('_', {}) else HC['_']['FA'],
                      TT4=B.const_inp('c_TT4_%d' % N2, f32(TT4)), TTt=B.const_inp('c_TTt_%d' % N2, f32(TTt)),
                      CC=B.const_inp('c_CC_%d' % N2, f32(CC)), CI1=B.const_inp('c_CI1_%d' % N2, f32(CI1)),
                      CI2=B.const_inp('c_CI2_%d' % N2, f32(CI2)), FI=B.const_inp('c_FI_%d' % N2, f32(FI)))
        HC.setdefault('_', {})['FA'] = HC[N2]['FA']
    HS = {}
    for nm, L in seqs:
        if L % 8192:
            continue
        N = 2 * L
        import jax
        import jax.numpy as jnp
        with jax.default_device(jax.devices('cpu')[0]):
            tt_ = jnp.linspace(0.0, 1.0, L, dtype=jnp.float32)[:, None]
            w_ = 2.0 * math.pi * jnp.arange(L, dtype=jnp.float32)[:, None] / L
            bands = jnp.linspace(1e-4, 15, 16, dtype=jnp.float32)[None, :]
            emb = np.asarray(jnp.concatenate([tt_, jnp.cos(bands * w_), -jnp.sin(bands * w_)], axis=-1))
            max_decay = math.log(HY_DECAY_TARGET) / HY_FAST
            min_decay = math.log(HY_DECAY_TARGET) / HY_SLOW
            deltas = np.asarray(jnp.abs(jnp.linspace(min_decay, max_decay, W, dtype=jnp.float32)))
        posn = np.concatenate([np.arange(L), (2 * L - np.arange(L, 2 * L)) % L])
        HS[nm] = dict(
            embT=B.const_inp('c_embT_' + nm, np.ascontiguousarray(emb[posn].T)),
            negd=B.const_inp('c_negd', np.ascontiguousarray(-deltas.reshape(4, 128).T)) if 'negd' not in HS.get('_', {}) else HS['_']['negd'],
            filt=B.scratch('filt_' + nm, [W, N], F32),
            filtn=B.scratch('filtn_' + nm, [W, N], BF16),
            Hd=B.scratch('Hd_' + nm, [N // 128 // 128, 2, 128, W, 128], BF16),
        )
        HS.setdefault('_', {})['negd'] = HS[nm]['negd']

    def pass_F(l, nm, L):
        N = 2 * L
        hs = HS[nm]
        NCH = N // 2048
        with ExitStack() as es:
            w1 = es.enter_context(SBT('F_w1', [33, 64], F32))
            w2 = es.enter_context(SBT('F_w2', [64, 64], F32))
            w3 = es.enter_context(SBT('F_w3', [64, 2 * W], F32))
            cols = es.enter_context(SBT('F_cols', [64, 8], F32))
            negd = es.enter_context(SBT('F_negd', [128, 4], F32))
            emb = es.enter_context(SBT('F_emb', [33, 2, 2048], F32))
            trow = es.enter_context(SBT('F_trow', [128, 2, 2048], F32))
            arg = es.enter_context(SBT('F_arg', [64, 2048], F32))
            kf = es.enter_context(SBT('F_kf', [64, 2048], F32))
            ki = es.enter_context(SBT('F_ki', [64, 2048], I32))
            sn = es.enter_context(SBT('F_sn', [64, 2, 2048], F32))
            h1 = es.enter_context(SBT('F_h1', [64, 2048], F32))
            h2 = es.enter_context(SBT('F_h2', [64, 2048], F32))
            dec = es.enter_context(SBT('F_dec', [128, 2048], F32))
            fo = es.enter_context(SBT('F_fo', [128, 2, 2048], F32))
            junk = es.enter_context(SBT('F_junk', [128, 2048], F32))
            asum = es.enter_context(SBT('F_asum', [128, 4, NCH], F32))
            rn = es.enter_context(SBT('F_rn', [128, 8], F32))
            fl = es.enter_context(SBT('F_fl', [128, 2, 2048], F32))
            fb = es.enter_context(SBT('F_fb', [128, 2, 2048], BF16))
            psh = es.enter_context(PST('F_psh', [128, 4, 512], F32))
            psf = [es.enter_context(PST('F_psf%d' % i, [128, 512], F32)) for i in range(2)]
            s.dma('sp', w1[:], hy_w1.ap()[l], writes=['Fw'])
            s.dma('sp', w2[:], hy_w2.ap()[l], writes=['Fw'])
            s.dma('sp', w3[:], hy_w3.ap()[l], writes=['Fw'])
            s.dma('sp', cols[:, 0:1], hy_b1.ap()[l].rearrange('(p o) -> p o', o=1), writes=['Fcols'])
            s.dma('sp', cols[:, 1:2], hy_freq.ap()[l, 0].rearrange('(p o) -> p o', o=1), writes=['Fcols'])
            s.dma('sp', cols[:, 2:3], hy_b2.ap()[l].rearrange('(p o) -> p o', o=1), writes=['Fcols'])
            s.dma('sp', cols[:, 3:4], hy_freq.ap()[l, 1].rearrange('(p o) -> p o', o=1), writes=['Fcols'])
            s.dma('sp', negd[:], hs['negd'].ap(), writes=['Fnegd'])
            s.op('dve', lambda e: e.tensor_tensor(out=cols[:, 4:5], in0=cols[:, 0:1], in1=cols[:, 1:2], op=ALU.mult),
                 reads=['Fcols'], writes=['Fcols'])
            s.op('dve', lambda e: e.tensor_tensor(out=cols[:, 5:6], in0=cols[:, 2:3], in1=cols[:, 3:4], op=ALU.mult),
                 reads=['Fcols'], writes=['Fcols'])
            s.op('dve', lambda e: e.memset(cols[:, 6:7], math.pi / 2), reads=['Fcols'], writes=['Fcols'])
            s.op('dve', lambda e: e.memset(cols[:, 7:8], 0.0), reads=['Fcols'], writes=['Fcols'])

            def sin_layer(ps, fcol, fbcol, out):
                s.op('act', lambda e: e.activation(out=arg[:], in_=ps, func=AF.Identity, scale=cols[:, fcol:fcol + 1],
                                                   bias=cols[:, fbcol:fbcol + 1]), reads=['psh', 'Fcols'], writes=['arg'])
                s.op('dve', lambda e: e.tensor_scalar(out=kf[:], in0=arg[:], scalar1=1.0 / (2 * math.pi), scalar2=None, op0=ALU.mult),
                     reads=['arg'], writes=['kf'])
                s.op('dve', lambda e: e.tensor_copy(out=ki[:], in_=kf[:]), reads=['kf'], writes=['ki'])
                s.op('dve', lambda e: e.tensor_copy(out=kf[:], in_=ki[:]), reads=['ki'], writes=['kf'])
                s.op('dve', lambda e: e.scalar_tensor_tensor(out=arg[:], in0=kf[:], scalar=-2 * math.pi, in1=arg[:],
                                                             op0=ALU.mult, op1=ALU.add), reads=['kf', 'arg'], writes=['arg'])
                s.op('act', lambda e: e.activation(out=sn[:, 0], in_=arg[:], func=AF.Sin, scale=0.25, bias=cols[:, 7:8]),
                     reads=['arg', 'Fcols'], writes=['sn0'])
                s.op('act', lambda e: e.activation(out=sn[:, 1], in_=arg[:], func=AF.Sin, scale=0.25, bias=cols[:, 6:7]),
                     reads=['arg', 'Fcols'], writes=['sn1'])
                s.op('dve', lambda e: e.tensor_tensor(out=kf[:], in0=sn[:, 0], in1=sn[:, 0], op=ALU.mult), reads=['sn0'], writes=['kf'])
                s.op('dve', lambda e: e.tensor_scalar(out=kf[:], in0=kf[:], scalar1=-2.0, scalar2=1.0, op0=ALU.mult, op1=ALU.add),
                     reads=['kf'], writes=['kf'])
                s.op('dve', lambda e: e.tensor_tensor(out=sn[:, 0], in0=sn[:, 0], in1=sn[:, 1], op=ALU.mult),
                     reads=['sn0', 'sn1'], writes=['sn0'])
                s.op('dve', lambda e: e.scalar_tensor_tensor(out=out, in0=sn[:, 0], scalar=4.0, in1=kf[:], op0=ALU.mult, op1=ALU.mult),
                     reads=['sn0', 'kf'], writes=['Fh'])

            pshv = psh[0:64].rearrange('p a b -> p (a b)')
            for ch in range(NCH):
                sl = ch % 2
                c0 = ch * 2048
                d = 0 if c0 < L else 1
                s.dma('sp', emb[:, sl], hs['embT'].ap()[:, c0:c0 + 2048], writes=['emb%d' % sl])
                s.dma('sp', trow[:, sl], hs['embT'].ap()[0:1, c0:c0 + 2048].partition_broadcast(128).rearrange('p a t -> p (a t)'),
                      writes=['trow%d' % sl])
                for q in range(4):
                    s.op('pe', lambda e: e.matmul(psh[0:64, q, :], lhsT=w1[:], rhs=emb[:, sl, q * 512:(q + 1) * 512], start=True, stop=True),
                         reads=['Fw', 'emb%d' % sl], writes=['psh'])
                sin_layer(pshv, 1, 4, h1[:])
                for q in range(4):
                    s.op('pe', lambda e: e.matmul(psh[0:64, q, :], lhsT=w2[:], rhs=h1[:, q * 512:(q + 1) * 512], start=True, stop=True),
                         reads=['Fw', 'Fh'], writes=['psh'])
                sin_layer(pshv, 3, 5, h2[:])
                for cb in range(4):
                    fk = cb % 2
                    s.op('act', lambda e: e.activation(out=dec[:], in_=trow[:, sl], func=AF.Exp, scale=negd[:, cb:cb + 1]),
                         reads=['trow%d' % sl, 'Fnegd'], writes=['dec'])
                    for q in range(4):
                        k = q % 2
                        s.op('pe', lambda e: e.matmul(psf[k][:], lhsT=w3[:, d * W + cb * 128:d * W + (cb + 1) * 128], rhs=h2[:, q * 512:(q + 1) * 512],
                                                      start=True, stop=True), reads=['Fw', 'Fh'], writes=['psf%d' % k])
                        s.op('dve', lambda e: e.tensor_tensor(out=fo[:, fk, q * 512:(q + 1) * 512], in0=psf[k][:], in1=dec[:, q * 512:(q + 1) * 512],
                                                              op=ALU.mult), reads=['psf%d' % k, 'dec'], writes=['fo%d' % fk])
                    s.op('act', lambda e: e.activation(out=junk[:], in_=fo[:, fk], func=AF.Abs, accum_out=asum[:, cb, ch:ch + 1]),
                         reads=['fo%d' % fk], writes=['Fjunk', 'asum'])
                    s.dma('pool', hs['filt'].ap()[cb * 128:(cb + 1) * 128, c0:c0 + 2048], fo[:, fk], reads=['fo%d' % fk], writes=['filt'])
            for cb in range(4):
                s.op('act', lambda e: e.activation(out=junk[:, 0:NCH], in_=asum[:, cb, :], func=AF.Identity, accum_out=rn[:, cb:cb + 1]),
                     reads=['asum'], writes=['Fjunk', 'rn'])
            s.op('dve', lambda e: e.tensor_scalar(out=rn[:, 0:4], in0=rn[:, 0:4], scalar1=EPS, scalar2=None, op0=ALU.add),
                 reads=['rn'], writes=['rn'])
            s.op('dve', lambda e: e.reciprocal(out=rn[:, 4:8], in_=rn[:, 0:4]), reads=['rn'], writes=['rn'])
            it = 0
            for cb in range(4):
                for c0 in range(0, N, 2048):
                    sl = it % 2
                    it += 1
                    s.dma('sp', fl[:, sl], hs['filt'].ap()[cb * 128:(cb + 1) * 128, c0:c0 + 2048], reads=['filt'], writes=['fl%d' % sl])
                    s.op('dve', lambda e: e.tensor_scalar(out=fl[:, sl], in0=fl[:, sl], scalar1=rn[:, 4 + cb:5 + cb], scalar2=None,
                                                          op0=ALU.mult), reads=['fl%d' % sl, 'rn'], writes=['fl%d' % sl])
                    if c0 == 0:
                        s.op('dve', lambda e: e.tensor_tensor(out=fl[:, sl, 0:1], in0=fl[:, sl, 0:1], in1=PRM['hyd'][:, cb, 0:1], op=ALU.add),
                             reads=['fl%d' % sl], writes=['fl%d' % sl])
                    if c0 <= L < c0 + 2048:
                        s.op('dve', lambda e: e.memset(fl[:, sl, L - c0:L - c0 + 1], 0.0), reads=['fl%d' % sl], writes=['fl%d' % sl])
                    s.op('act', lambda e: e.copy(out=fb[:, sl], in_=fl[:, sl]), reads=['fl%d' % sl], writes=['fb%d' % sl])
                    s.dma('pool', hs['filtn'].ap()[cb * 128:(cb + 1) * 128, c0:c0 + 2048], fb[:, sl], reads=['fb%d' % sl], writes=['filtn'])
        s.barrier()

    B.pass_F = pass_F

    def pass_H(l, nm, L, mode, cgroups=None):
        sc = SC[nm]
        hs = HS[nm]
        N = 2 * L
        N2 = N // 128
        NC2 = N2 // 128
        hc = HC[N2]
        CG = 32
        KA = 128 if mode == 'filter' else 64
        src = hs['filtn'] if mode == 'filter' else sc['zT']
        with ExitStack() as es:
            cf32 = es.enter_context(SBT('H_cf32', [128, 3 * NC2 * NC2 * 128], F32))
            FAb = es.enter_context(SBT('H_FA', [128, 256], BF16))
            TT4 = es.enter_context(SBT('H_TT4', [128, NC2, 2, 4, 128], F32))
            CCb = es.enter_context(SBT('H_CC', [128, NC2, NC2, 3, 128], BF16))
            zA = es.enter_context(SBT('H_zA', [128, CG, N2], BF16))
            ApT = es.enter_context(SBT('H_ApT', [128, NC2, 2, CG, 128], BF16))
            tq = es.enter_context(SBT('H_tq', [128, 2, 4, 1024], F32))
            cq = es.enter_context(SBT('H_cq', [128, 2, 2, 1024], F32))
            tqi = [0]
            PB2 = [es.enter_context(PST('H_pb%d' % i, [128, 2, 512], F32)) for i in range(4)]
            pbi = [0, 0, 0, 0]
            if mode == 'data':
                TTt = es.enter_context(SBT('H_TTt', [128, 2, 2, N2], F32))
                CI1 = es.enter_context(SBT('H_CI1', [128, NC2, 2 * N2], BF16))
                CI2 = es.enter_context(SBT('H_CI2', [128, NC2, 2 * N2], BF16))
                FIb = es.enter_context(SBT('H_FI', [128, 2, 64], BF16))
                x0s = es.enter_context(SBT('H_x0s', [64, CG, N2], BF16))
                Yt = es.enter_context(SBT('H_Yt', [128, NC2, 2, CG, 128], BF16))
                Ht = es.enter_context(SBT('H_Ht', [128, 2, 2, 4, 128], BF16))
                Bp = es.enter_context(SBT('H_Bp', [128, 2, 2, 2, N2], BF16))
                ost = es.enter_context(SBT('H_ost', [64, CG, N2], BF16))
            else:
                Xo = es.enter_context(SBT('H_Xo', [128, 2, 2, 4, 128], BF16))

            def load_cast(dst, src_ap, shape_cols):
                v = cf32[:, 0:shape_cols]
                s.dma('sp', v, src_ap, writes=['cf32'])
                s.op('dve', lambda e: e.tensor_copy(out=dst, in_=v), reads=['cf32'], writes=['Hconst'])
            load_cast(FAb[:], hc['FA'].ap(), 256)
            load_cast(CCb[:].rearrange('p a b c d -> p (a b c d)'), hc['CC'].ap().rearrange('p a b c d -> p (a b c d)'), NC2 * NC2 * 3 * 128)
            s.dma('sp', TT4[:], hc['TT4'].ap(), writes=['Hconst'])
            if mode == 'data':
                load_cast(CI1[:].rearrange('p a b -> p (a b)'), hc['CI1'].ap().rearrange('p a b -> p (a b)'), NC2 * 2 * N2)
                load_cast(CI2[:].rearrange('p a b -> p (a b)'), hc['CI2'].ap().rearrange('p a b -> p (a b)'), NC2 * 2 * N2)
                load_cast(FIb[:].rearrange('p a b -> p (a b)'), hc['FI'].ap().rearrange('p a b -> p (a b)'), 128)
                s.dma('sp', TTt[:], hc['TTt'].ap(), writes=['Hconst'])

            def cmul(pr, pi, tr, ti, outr, outi, conj, n, psn='psC', outn='cm_out', tabn='Hconst'):
                tb = tqi[0] % 2
                tqi[0] += 1
                t1, t2, t3, t4 = (tq[:, tb, i, 0:n] for i in range(4))
                sh = list(pr.shape)

                def vw(a):
                    return a if len(sh) == 2 else a.rearrange('p (a b) -> p a b', a=sh[1])
                s.op('dve', lambda e: e.tensor_tensor(out=vw(t1), in0=pr, in1=tr, op=ALU.mult), reads=[psn, 'Hconst', tabn], writes=['tq1_%d' % tb])
                s.op('dve', lambda e: e.tensor_tensor(out=vw(t2), in0=pi, in1=ti, op=ALU.mult), reads=[psn, 'Hconst', tabn], writes=['tq2_%d' % tb])
                s.op('pool', lambda e: e.tensor_tensor(out=outr, in0=vw(t1), in1=vw(t2), op=(ALU.add if conj else ALU.subtract)),
                     reads=['tq1_%d' % tb, 'tq2_%d' % tb], writes=[outn])
                if conj:
                    s.op('dve', lambda e: e.tensor_tensor(out=vw(t3), in0=pi, in1=tr, op=ALU.mult), reads=[psn, 'Hconst', tabn], writes=['tq3_%d' % tb])
                    s.op('dve', lambda e: e.tensor_tensor(out=vw(t4), in0=pr, in1=ti, op=ALU.mult), reads=[psn, 'Hconst', tabn], writes=['tq4_%d' % tb])
                    s.op('pool', lambda e: e.tensor_tensor(out=outi, in0=vw(t3), in1=vw(t4), op=ALU.subtract),
                         reads=['tq3_%d' % tb, 'tq4_%d' % tb], writes=[outn])
                else:
                    s.op('dve', lambda e: e.tensor_tensor(out=vw(t3), in0=pr, in1=ti, op=ALU.mult), reads=[psn, 'Hconst', tabn], writes=['tq3_%d' % tb])
                    s.op('dve', lambda e: e.tensor_tensor(out=vw(t4), in0=pi, in1=tr, op=ALU.mult), reads=[psn, 'Hconst', tabn], writes=['tq4_%d' % tb])
                    s.op('pool', lambda e: e.tensor_tensor(out=outi, in0=vw(t3), in1=vw(t4), op=ALU.add),
                         reads=['tq3_%d' % tb, 'tq4_%d' % tb], writes=[outn])

            hi_ = 0
            for cg in (cgroups if cgroups is not None else range(W // CG)):
                c0 = cg * CG
                s.dma('sp', zA[0:KA], src.ap()[c0:c0 + CG, 0:KA * N2].rearrange('c (a b) -> a c b', b=N2),
                      reads=['zT'], writes=['zA'])
                if mode == 'data':
                    s.dma('sp', x0s[:], sc['x0sT'].ap()[c0:c0 + CG, :].rearrange('c (a b) -> a c b', b=N2), writes=['x0s'])
                for c4 in range(CG // 4):
                    for j in range(NC2):
                        ka = pbi[0] % 2
                        pbi[0] += 1
                        psA = [PB2[ka][:, 0, :], PB2[ka][:, 1, :]]
                        for ci in range(4):
                            c = c4 * 4 + ci
                            for ri in range(2):
                                s.op('pe', lambda e: e.matmul(psA[ri][:, ci * 128:(ci + 1) * 128], lhsT=zA[0:KA, c, j * 128:(j + 1) * 128],
                                                              rhs=FAb[0:KA, ri * 128:(ri + 1) * 128], start=(ci == 0), stop=True),
                                     reads=['zA', 'Hconst'], writes=['pb%d' % ka])
                        cmul(psA[0].rearrange('p (a b) -> p a b', a=4), psA[1].rearrange('p (a b) -> p a b', a=4),
                             TT4[:, j, 0], TT4[:, j, 1], ApT[:, j, 0, c4 * 4:(c4 + 1) * 4, :], ApT[:, j, 1, c4 * 4:(c4 + 1) * 4, :], False, 512,
                             psn='pb%d' % ka, outn='ApT%d' % c4)
                for c4 in range(CG // 4):
                    cs = slice(c4 * 4, (c4 + 1) * 4)
                    for kc in range(NC2):
                        kx = 2 + pbi[1] % 2
                        pbi[1] += 1
                        psX = [PB2[kx][:, 0, :], PB2[kx][:, 1, :]]
                        for j in range(NC2):
                            ar = ApT[:, j, 0, cs, :].rearrange('p a b -> p (a b)')
                            ai = ApT[:, j, 1, cs, :].rearrange('p a b -> p (a b)')
                            s.op('pe', lambda e: e.matmul(psX[0], lhsT=CCb[:, j, kc, 0, :], rhs=ar, start=(j == 0), stop=False),
                                 reads=['ApT%d' % c4, 'Hconst'], writes=['pb%d' % kx])
                            s.op('pe', lambda e: e.matmul(psX[0], lhsT=CCb[:, j, kc, 2, :], rhs=ai, start=False, stop=(j == NC2 - 1)),
                                 reads=['ApT%d' % c4, 'Hconst'], writes=['pb%d' % kx])
                            s.op('pe', lambda e: e.matmul(psX[1], lhsT=CCb[:, j, kc, 1, :], rhs=ar, start=(j == 0), stop=False),
                                 reads=['ApT%d' % c4, 'Hconst'], writes=['pb%d' % kx])
                            s.op('pe', lambda e: e.matmul(psX[1], lhsT=CCb[:, j, kc, 0, :], rhs=ai, start=False, stop=(j == NC2 - 1)),
                                 reads=['ApT%d' % c4, 'Hconst'], writes=['pb%d' % kx])
                        if mode == 'filter':
                            xs = hi_ % 2
                            hi_ += 1
                            s.op('act', lambda e: e.copy(out=Xo[:, xs, 0].rearrange('p a b -> p (a b)'), in_=psX[0]),
                                 reads=['pb%d' % kx], writes=['Xo%d' % xs])
                            s.op('dve', lambda e: e.tensor_copy(out=Xo[:, xs, 1].rearrange('p a b -> p (a b)'), in_=psX[1]),
                                 reads=['pb%d' % kx], writes=['Xo%d' % xs])
                            s.dma('pool', hs['Hd'].ap()[kc, :, :, c0 + c4 * 4:c0 + c4 * 4 + 4, :].rearrange('r p c k -> p r c k'), Xo[:, xs],
                                  reads=['Xo%d' % xs], writes=['Hd'])
                        else:
                            xs = hi_ % 2
                            hi_ += 1
                            s.dma('sp', Ht[:, xs], hs['Hd'].ap()[kc, :, :, c0 + c4 * 4:c0 + c4 * 4 + 4, :].rearrange('r p c k -> p r c k'),
                                  reads=['Hd'], writes=['Htab%d' % xs])
                            cmul(psX[0].rearrange('p (a b) -> p a b', a=4), psX[1].rearrange('p (a b) -> p a b', a=4),
                                 Ht[:, xs, 0], Ht[:, xs, 1], Yt[:, kc, 0, cs, :], Yt[:, kc, 1, cs, :], False, 512, psn='pb%d' % kx, outn='Yt%d' % c4, tabn='Htab%d' % xs)
                if mode == 'filter':
                    continue
                for c2 in range(CG // 2):
                    kb_ = pbi[2] % 2
                    pbi[2] += 1
                    psB = PB2[kb_]
                    psY = PB2[2 + kb_][:, 0, :]
                    for ci in range(2):
                        c = c2 * 2 + ci
                        for kc in range(NC2):
                            s.op('pe', lambda e: e.matmul(psB[:, ci, 0:2 * N2], lhsT=Yt[:, kc, 0, c, :], rhs=CI1[:, kc, :],
                                                          start=(kc == 0), stop=False), reads=['Yt%d' % (c // 4), 'Hconst'], writes=['pb%d' % kb_])
                            s.op('pe', lambda e: e.matmul(psB[:, ci, 0:2 * N2], lhsT=Yt[:, kc, 1, c, :], rhs=CI2[:, kc, :],
                                                          start=False, stop=(kc == NC2 - 1)), reads=['Yt%d' % (c // 4), 'Hconst'], writes=['pb%d' % kb_])
                    bs_ = c2 % 2
                    cmul(psB[:, :, 0:N2], psB[:, :, N2:2 * N2], TTt[:, 0], TTt[:, 1], Bp[:, bs_, 0], Bp[:, bs_, 1], True, 2 * N2, psn='pb%d' % kb_, outn='Bp%d' % bs_)
                    s.op('pe', lambda e: e.matmul(psY[0:64, 0:2 * N2], lhsT=FIb[:, 0, :], rhs=Bp[:, bs_, 0].rearrange('p a b -> p (a b)'),
                                                  start=True, stop=False), reads=['Bp%d' % bs_, 'Hconst'], writes=['pb%d' % (2 + kb_)])
                    s.op('pe', lambda e: e.matmul(psY[0:64, 0:2 * N2], lhsT=FIb[:, 1, :], rhs=Bp[:, bs_, 1].rearrange('p a b -> p (a b)'),
                                                  start=False, stop=True), reads=['Bp%d' % bs_, 'Hconst'], writes=['pb%d' % (2 + kb_)])
                    s.op('dve', lambda e: e.tensor_tensor(out=ost[:, c2 * 2:c2 * 2 + 2, :],
                                                          in0=psY[0:64, 0:2 * N2].rearrange('p (a b) -> p a b', a=2),
                                                          in1=x0s[:, c2 * 2:c2 * 2 + 2, :], op=ALU.mult),
                         reads=['pb%d' % (2 + kb_), 'x0s'], writes=['Host'])
                s.dma('pool', sc['bsT'].ap()[2 * W + c0:2 * W + c0 + CG, :].rearrange('c (a b) -> a c b', b=N2), ost[:],
                      reads=['Host'], writes=['bsT'])
        s.barrier()

    B.pass_H = pass_H

    B.pass_W = pass_W
    B.locals = locals()
    return B


L_P, L_S = 16384, 8192


def two_pass(seqs, prog, taps=(), own=None):
    Bd = build(seqs, taps=taps, own=own)
    prog(Bd)
    Bd.s.finish()
    B = build(seqs, taps=taps, needed=Bd.s.needed, own=own)
    prog(B)
    B.s.finish()
    return B


def build_full():
    seqs = [('P', L_P), ('S', L_S)]
    own = {'P': (L_P // 8 // 128, [c * (L_P // 8 // 128) for c in range(8)]),
           'S': (L_S // 2 // 128, [(c % 2) * (L_S // 2 // 128) for c in range(8)])}
    return two_pass(seqs, lambda B: full_prog(B, seqs), own=own)


def full_prog(B, seqs):
    B.pass_init()
    B.pass_att_setup()
    for l in range(2):
        B.pass_W(l)
        B.pass_params(l)
        B.pass_lambda(l)
        for nm, L in seqs:
            last = (l == 1)
            B.pass_1(l, nm, L, blk_out=last)
            B.pass_2a(l, nm, L)
            B.pass_2b(l, nm, L)
            B.pass_2c(l, nm, L)
            B.pass_F(l, nm, L)
            B.pass_H(l, nm, L, 'filter')
            B.pass_H(l, nm, L, 'data')
            if last:
                B.pass_4o(l, nm, L)
            else:
                B.pass_4(l, nm, L)
            B.pass_5(l, nm, L, last, b_blk=last)


def kernel(**inputs):
    B = build_full()
    x_prompt = np.ascontiguousarray(np.asarray(inputs['x_prompt'], dtype=np.float32))
    x_sample = np.ascontiguousarray(np.asarray(inputs['x_sample'], dtype=np.float32))
    base = {}
    for name in B.inputs:
        if name in B.consts:
            base[name] = B.consts[name]
        elif name not in ('x_P', 'x_S'):
            base[name] = np.ascontiguousarray(np.asarray(inputs[name], dtype=np.float32))
    in_maps = []
    for c in range(8):
        m = dict(base)
        m['x_P'] = x_prompt[0]
        m['x_S'] = x_sample[c // 2]
        for name, arrs in B.core_consts.items():
            m[name] = arrs[c]
        in_maps.append(m)
    res = run_bass_kernel_spmd(B.nc, in_maps, core_ids=list(range(8)))
    y_prompt = np.empty((1, L_P, D), np.float32)
    y_sample = np.empty((4, L_S, D), np.float32)
    pc = L_P // 8
    for c in range(8):
        r = res.results[c]
        y_prompt[0, c * pc:(c + 1) * pc] = np.asarray(r['y_P'])[c * pc:(c + 1) * pc]
        hf = c % 2
        y_sample[c // 2, hf * (L_S // 2):(hf + 1) * (L_S // 2)] = np.asarray(r['y_S'])[hf * (L_S // 2):(hf + 1) * (L_S // 2)]
    return (y_prompt, y_sample)
```
